# Optimizing a Trainium2 kernel written in Bass

```python
import math
import jax, jax.numpy as jnp
from jax import lax
import numpy as np


D_MODEL = 1024
BATCH = 32
SEQ = 2048
DEPTH = 2
DEC_BATCH = 8
DEC_SEQ = 32
PAST_LEN = 1024

CHUNK = 64
Q_BLOCK = 128
DA_HEADS = 4
DA_HD = 64
DA_QK = 2 * DA_HD
DA_VD = 2 * DA_HD
DA_WIDTH = DA_HEADS * DA_VD
M_HEADS = 4
M_HD = 128
M_WIDTH = M_HEADS * M_HD
CONV_W = 4
D_FF = -(-8 * D_MODEL // (3 * 256)) * 256
IN_SIZES = (DA_HEADS * DA_QK, DA_HEADS * DA_QK, DA_WIDTH, 2 * M_WIDTH, M_WIDTH, 2 * M_HEADS, M_WIDTH)
IN_COLS = int(sum(IN_SIZES))
IN_SPLITS = tuple(int(v) for v in np.cumsum(IN_SIZES)[:-1])
ALPHA = (2 * DEPTH) ** 0.25
BETA = (8 * DEPTH) ** -0.25
LN_EPS = 1e-5
F32 = jnp.float32

kernel_name = 'diffattn_mlstm_streaming_encoder_step'


def layer_norm(x, g=None, b=None):
    xf = x.astype(F32)
    mu = xf.mean(-1, keepdims=True)
    var = jnp.mean(jnp.square(xf - mu), -1, keepdims=True)
    y = (xf - mu) * lax.rsqrt(var + LN_EPS)
    if g is not None:
        y = y * g.astype(F32) + b.astype(F32)
    return y.astype(x.dtype)


def head_norm(h, g, center):
    hf = h.astype(F32)
    if center:
        hf = hf - hf.mean(-1, keepdims=True)
    y = hf * lax.rsqrt(jnp.mean(hf * hf, -1, keepdims=True) + LN_EPS)
    return y.reshape(*h.shape[:-2], -1) * g.astype(F32)


def alibi_slopes(n):
    return jnp.asarray([2.0 ** (-8.0 * (i + 1) / n) for i in range(n)], F32)


def diff_attention(q, k, v, q_pos, k_pos, lam):
    s = jnp.einsum('bqhcd,bkhcd->bhcqk', q.astype(F32), k.astype(F32)) * (DA_HD ** -0.5)
    dist = jnp.abs(q_pos[:, None] - k_pos[None, :]).astype(F32)
    bias = -alibi_slopes(DA_HEADS)[:, None, None, None] * dist
    visible = (k_pos[None, :] // CHUNK) <= (q_pos[:, None] // CHUNK)
    s = jnp.where(visible, s + bias, -jnp.inf)
    p = jax.nn.softmax(s, axis=-1)
    w = p[:, :, 0] - lam * p[:, :, 1]
    return jnp.einsum('bhqk,bkhd->bqhd', w, v.astype(F32))


def diff_attention_prompt(q, k, v, lam):
    B, S = q.shape[0], q.shape[1]
    nblk = S // Q_BLOCK
    k_pos = jnp.arange(S)
    qb = q.reshape(B, nblk, Q_BLOCK, *q.shape[2:]).swapaxes(0, 1)

    def one_block(args):
        q_blk, i = args
        q_pos = i * Q_BLOCK + jnp.arange(Q_BLOCK)
        return diff_attention(q_blk, k, v, q_pos, k_pos, lam)

    out = lax.map(one_block, (qb, jnp.arange(nblk)))
    return out.swapaxes(0, 1).reshape(B, S, DA_HEADS, DA_VD)


def mlstm_chunk(state, inp):
    C, n, m = state
    q, k, v, ig, lf = inp
    L = q.shape[1]
    b = jnp.cumsum(lf, axis=1)
    Dm = b[:, :, None, :] - b[:, None, :, :] + ig[:, None, :, :]
    causal = jnp.tril(jnp.ones((L, L), dtype=bool))
    Dm = jnp.where(causal[None, :, :, None], Dm, -jnp.inf)
    inter = b + m[:, None, :]
    m_t = jnp.maximum(inter, Dm.max(axis=2))
    w_intra = jnp.exp(Dm - m_t[:, :, None, :])
    w_inter = jnp.exp(inter - m_t)
    qk = jnp.einsum('bthd,bshd->btsh', q, k) * w_intra
    num = jnp.einsum('btsh,bshd->bthd', qk, v) + w_inter[..., None] * jnp.einsum('bhvd,bthd->bthv', C, q)
    den = qk.sum(axis=2) + w_inter * jnp.einsum('bhd,bthd->bth', n, q)
    h = num / jnp.maximum(jnp.abs(den), jnp.exp(-m_t))[..., None]
    bL = b[:, -1]
    dec_s = bL[:, None, :] - b + ig
    m_new = jnp.maximum(bL + m, dec_s.max(axis=1))
    ws = jnp.exp(dec_s - m_new[:, None, :])
    wc = jnp.exp(bL + m - m_new)
    C_new = wc[..., None, None] * C + jnp.einsum('bsh,bshv,bshd->bhvd', ws, v, k)
    n_new = wc[..., None] * n + jnp.einsum('bsh,bshd->bhd', ws, k)
    return (C_new, n_new, m_new), h


def mlstm_prompt(q, k, v, ig, lf):
    B, S = q.shape[0], q.shape[1]
    nc = S // CHUNK

    def chunks(a):
        return a.reshape(B, nc, CHUNK, *a.shape[2:]).swapaxes(0, 1)

    init = (jnp.zeros((B, M_HEADS, M_HD, M_HD), F32), jnp.zeros((B, M_HEADS, M_HD), F32),
            jnp.zeros((B, M_HEADS), F32))
    state, h = lax.scan(mlstm_chunk, init, (chunks(q), chunks(k), chunks(v), chunks(ig), chunks(lf)))
    return h.swapaxes(0, 1).reshape(B, S, M_HEADS, M_HD), state


def causal_conv(u, buf, w, b):
    T = u.shape[1]
    full = jnp.concatenate([buf.astype(u.dtype), u], axis=1)
    y = b
    for j in range(CONV_W):
        y = y + full[:, j:j + T] * w[j]
    return jax.nn.silu(y), full[:, -(CONV_W - 1):]


def trunk_layer(x, c, layer, attend, recur, conv_buf,
                w_ada, b_ada, w_in, b_if, conv_w, conv_b, lam_p, da_norm_w, m_norm_w,
                w_br_a, w_br_b, w_gate, b_gate, w_o, ln1_g, ln1_b, w_gu, w_down, ln2_g, ln2_b):
    B, T, _ = x.shape
    mod = jnp.einsum('bd,de->be', jax.nn.silu(c), w_ada) + b_ada
    sh1, sc1, g1, sh2, sc2, g2 = jnp.split(mod[:, None, :], 6, axis=-1)
    h = layer_norm(x) * (1 + sc1) + sh1
    z = jnp.einsum('btd,de->bte', h, w_in)
    a_q, a_k, a_v, m_qk, m_v, m_if, m_o = jnp.split(z, IN_SPLITS, axis=-1)
    aq = a_q.reshape(B, T, DA_HEADS, 2, DA_HD)
    ak = a_k.reshape(B, T, DA_HEADS, 2, DA_HD)
    av = a_v.reshape(B, T, DA_HEADS, DA_VD)
    lam_init = 0.8 - 0.6 * math.exp(-0.3 * layer)
    lp = lam_p.astype(F32)
    lam = jnp.exp(jnp.sum(lp[0] * lp[1])) - jnp.exp(jnp.sum(lp[2] * lp[3])) + lam_init
    a_out = attend(aq, ak, av, lam)
    qk_c, conv_state = causal_conv(m_qk, conv_buf, conv_w, conv_b)
    mq, mk = jnp.split(qk_c, 2, axis=-1)
    mq = mq.reshape(B, T, M_HEADS, M_HD).astype(F32)
    mk = mk.reshape(B, T, M_HEADS, M_HD).astype(F32) * (M_HD ** -0.5)
    mv = m_v.reshape(B, T, M_HEADS, M_HD).astype(F32)
    gates = (m_if + b_if).astype(F32)
    ig = gates[..., :M_HEADS]
    lf = jax.nn.log_sigmoid(gates[..., M_HEADS:])
    m_out, m_state = recur(mq, mk, mv, ig, lf)
    a_n = head_norm(a_out, da_norm_w, False) * (1.0 - lam_init)
    m_n = head_norm(m_out, m_norm_w, True) * jax.nn.sigmoid(m_o.astype(F32))
    y_a = a_n.astype(x.dtype) @ w_br_a
    y_b = m_n.astype(x.dtype) @ w_br_b
    g_a, g_b = jnp.split(jax.nn.sigmoid(h @ w_gate + b_gate), 2, axis=-1)
    mix = (g_a * y_a + g_b * y_b) @ w_o
    x = layer_norm(ALPHA * x + (1 + g1) * mix, ln1_g, ln1_b)
    h2 = layer_norm(x) * (1 + sc2) + sh2
    gt, up = jnp.split(h2 @ w_gu, 2, axis=-1)
    ffn = (jax.nn.silu(gt) * up) @ w_down
    x = layer_norm(ALPHA * x + (1 + g2) * ffn, ln2_g, ln2_b)
    k_rows = ak.reshape(B, T, DA_HEADS, DA_QK)
    return x.astype(c.dtype), k_rows, av, m_state, conv_state


def setup_inputs(seed: int = 0) -> dict:
    key = jax.random.key(seed)
    ks = iter(jax.random.split(key, 48))

    def nrm(shape, s):
        return jax.random.normal(next(ks), shape, F32) * s

    col_scale = np.ones((IN_COLS,), np.float32)
    off = np.cumsum((0,) + IN_SIZES)
    col_scale[off[2]:off[3]] = BETA
    col_scale[off[4]:off[5]] = BETA
    b_if = jnp.concatenate([nrm((DEPTH, M_HEADS), 0.1),
                            jnp.broadcast_to(jnp.linspace(3.0, 6.0, M_HEADS), (DEPTH, M_HEADS)) + nrm((DEPTH, M_HEADS), 0.1)], -1)
    return {
        'x_prompt': nrm((BATCH, SEQ, D_MODEL), 1.0),
        'x_sample': nrm((DEC_BATCH, DEC_SEQ, D_MODEL), 1.0),
        'c_prompt': nrm((BATCH, D_MODEL), 1.0),
        'c_sample': nrm((DEC_BATCH, D_MODEL), 1.0),
        'cache_attn_k': nrm((DEPTH, DEC_BATCH, PAST_LEN, DA_HEADS, DA_QK), 1.0),
        'cache_attn_v': nrm((DEPTH, DEC_BATCH, PAST_LEN, DA_HEADS, DA_VD), 0.5),
        'state_mlstm_C': nrm((DEPTH, DEC_BATCH, M_HEADS, M_HD, M_HD), 0.3),
        'state_mlstm_n': nrm((DEPTH, DEC_BATCH, M_HEADS, M_HD), 0.3),
        'state_mlstm_m': nrm((DEPTH, DEC_BATCH, M_HEADS), 1.0),
        'state_mlstm_conv': nrm((DEPTH, DEC_BATCH, CONV_W - 1, 2 * M_WIDTH), 1.0),
        'w_ada': nrm((DEPTH, D_MODEL, 6 * D_MODEL), 0.2 * D_MODEL ** -0.5),
        'b_ada': nrm((DEPTH, 6 * D_MODEL), 0.01),
        'w_in': nrm((DEPTH, D_MODEL, IN_COLS), D_MODEL ** -0.5) * jnp.asarray(col_scale),
        'b_if': b_if,
        'conv_w': nrm((DEPTH, CONV_W, 2 * M_WIDTH), CONV_W ** -0.5),
        'conv_b': nrm((DEPTH, 2 * M_WIDTH), 0.01),
        'lam_p': nrm((DEPTH, 4, DA_HD), 0.1),
        'da_norm_w': 1.0 + nrm((DEPTH, DA_WIDTH), 0.02),
        'm_norm_w': 1.0 + nrm((DEPTH, M_WIDTH), 0.02),
        'w_br_a': nrm((DEPTH, DA_WIDTH, D_MODEL), BETA * DA_WIDTH ** -0.5),
        'w_br_b': nrm((DEPTH, M_WIDTH, D_MODEL), BETA * M_WIDTH ** -0.5),
        'w_gate': nrm((DEPTH, D_MODEL, 2 * D_MODEL), D_MODEL ** -0.5),
        'b_gate': nrm((DEPTH, 2 * D_MODEL), 0.01),
        'w_o': nrm((DEPTH, D_MODEL, D_MODEL), BETA * D_MODEL ** -0.5),
        'ln1_g': 1.0 + nrm((DEPTH, D_MODEL), 0.02),
        'ln1_b': nrm((DEPTH, D_MODEL), 0.01),
        'w_gu': nrm((DEPTH, D_MODEL, 2 * D_FF), D_MODEL ** -0.5),
        'w_down': nrm((DEPTH, D_FF, D_MODEL), BETA * D_FF ** -0.5),
        'ln2_g': 1.0 + nrm((DEPTH, D_MODEL), 0.02),
        'ln2_b': nrm((DEPTH, D_MODEL), 0.01),
    }


def reference(x_prompt, x_sample, c_prompt, c_sample, cache_attn_k, cache_attn_v,
              state_mlstm_C, state_mlstm_n, state_mlstm_m, state_mlstm_conv,
              w_ada, b_ada, w_in, b_if, conv_w, conv_b, lam_p, da_norm_w, m_norm_w,
              w_br_a, w_br_b, w_gate, b_gate, w_o, ln1_g, ln1_b, w_gu, w_down, ln2_g, ln2_b):
    xp, xs = x_prompt, x_sample
    Bp = xp.shape[0]
    Bs, Ts = xs.shape[0], xs.shape[1]
    P = cache_attn_k.shape[2]
    kp_l, vp_l, Cp_l, np_l, mp_l, cvp_l = [], [], [], [], [], []
    ks_l, vs_l, Cs_l, ns_l, ms_l, cvs_l = [], [], [], [], [], []
    for l in range(DEPTH):
        weights = (w_ada[l], b_ada[l], w_in[l], b_if[l], conv_w[l], conv_b[l], lam_p[l], da_norm_w[l],
                   m_norm_w[l], w_br_a[l], w_br_b[l], w_gate[l], b_gate[l], w_o[l], ln1_g[l], ln1_b[l],
                   w_gu[l], w_down[l], ln2_g[l], ln2_b[l])
        conv0 = jnp.zeros((Bp, CONV_W - 1, 2 * M_WIDTH), xp.dtype)
        xp, k_new, v_new, (C_f, n_f, m_f), cv = trunk_layer(
            xp, c_prompt, l, diff_attention_prompt, mlstm_prompt, conv0, *weights)
        kp_l.append(k_new); vp_l.append(v_new); Cp_l.append(C_f); np_l.append(n_f); mp_l.append(m_f); cvp_l.append(cv)

        ck = cache_attn_k[l].reshape(Bs, P, DA_HEADS, 2, DA_HD)
        cvv = cache_attn_v[l]

        def attend_sample(q, k, v, lam, ck=ck, cvv=cvv):
            k_all = jnp.concatenate([ck.astype(k.dtype), k], axis=1)
            v_all = jnp.concatenate([cvv.astype(v.dtype), v], axis=1)
            q_pos = P + jnp.arange(Ts)
            k_pos = jnp.arange(P + Ts)
            return diff_attention(q, k_all, v_all, q_pos, k_pos, lam)

        st = (state_mlstm_C[l].astype(F32), state_mlstm_n[l].astype(F32), state_mlstm_m[l].astype(F32))

        def recur_sample(q, k, v, ig, lf, st=st):
            st_new, h = mlstm_chunk(st, (q, k, v, ig, lf))
            return h, st_new

        xs, k_new, v_new, (C_f, n_f, m_f), cv = trunk_layer(
            xs, c_sample, l, attend_sample, recur_sample, state_mlstm_conv[l], *weights)
        ks_l.append(k_new); vs_l.append(v_new); Cs_l.append(C_f); ns_l.append(n_f); ms_l.append(m_f); cvs_l.append(cv)

    attn_k_prompt = jnp.stack(kp_l); attn_v_prompt = jnp.stack(vp_l)
    attn_k_sample = jnp.stack(ks_l); attn_v_sample = jnp.stack(vs_l)
    mlstm_C_prompt = jnp.stack(Cp_l); mlstm_n_prompt = jnp.stack(np_l)
    mlstm_m_prompt = jnp.stack(mp_l); mlstm_conv_prompt = jnp.stack(cvp_l)
    mlstm_C_sample = jnp.stack(Cs_l); mlstm_n_sample = jnp.stack(ns_l)
    mlstm_m_sample = jnp.stack(ms_l); mlstm_conv_sample = jnp.stack(cvs_l)
    return (xp, xs, attn_k_prompt, attn_v_prompt, attn_k_sample, attn_v_sample,
            mlstm_C_prompt, mlstm_n_prompt, mlstm_m_prompt, mlstm_conv_prompt,
            mlstm_C_sample, mlstm_n_sample, mlstm_m_sample, mlstm_conv_sample)
```

```python
import math
from contextlib import ExitStack

import numpy as np
import concourse.bass as bass
import concourse.mybir as mybir
from concourse.bass_utils import run_bass_kernel_spmd

F32 = mybir.dt.float32
BF16 = mybir.dt.bfloat16
AF = mybir.ActivationFunctionType
ALU = mybir.AluOpType
AX = mybir.AxisListType

D = 1024
DEPTH = 2
NH = 4
DFF = 2816
NFF = 22
IN_COLS = 3592
LN_EPS = 1e-5
ALPHA = (2 * DEPTH) ** 0.25
SLOPES = [2.0 ** (-8.0 * (i + 1) / 4) for i in range(4)]
PAST = 1024
TS = 32
NEG = -30000.0
KSCALE = 128 ** -0.5

V_BADA = 0
V_CONVW = 48
V_CONVB = 80
V_DAN = 88
V_MN = 92
V_BGATE = 96
V_LN1G = 112
V_LN1B = 120
V_LN2G = 128
V_LN2B = 136
NV = 144

W_SHAPES = {
    "w_ada": (D, 6 * D), "w_in": (D, IN_COLS), "w_br_a": (512, D), "w_br_b": (512, D),
    "w_gate": (D, 2 * D), "w_o": (D, D), "w_gu": (D, 2 * DFF), "w_down": (DFF, D),
}


STOP = None
DEBUG = None


class _Stop(Exception):
    pass


def _chk(tag):
    if STOP == tag:
        raise _Stop()


class TT:
    __slots__ = ("w", "r", "excl", "small")

    def __init__(self, excl=False, small=False):
        self.w = None
        self.r = {}
        self.excl = excl
        self.small = small


class P:
    def __init__(self, nc, es):
        self.nc = nc
        self.es = es
        self.engs = {"pe": nc.tensor, "act": nc.scalar, "dve": nc.vector, "pool": nc.gpsimd, "sp": nc.sync}
        self.sems = {}
        self.cnt = {}
        self.seen = {e: {} for e in self.engs}
        for e in self.engs:
            self.sems[e] = es.enter_context(nc.semaphore("s_" + e))
            self.cnt[e] = 0
        self.dsems = {}
        self.dcnt = {}
        self.nwait = 0
        self.ninst = 0
        self.small_mode = False
        self.tag = ''
        self.pe_tags = []
        self.know = {}

    def _deps(self, reads, writes, eng=None):
        deps = {}
        same = 0
        sm = self.small_mode

        def add(k, v, small):
            nonlocal same
            if k == eng:
                if eng != "pe" and (small or sm) and v > same:
                    same = v
                return
            if deps.get(k, 0) < v:
                deps[k] = v
        for t in reads:
            if t.w is not None:
                add(t.w[0], t.w[1], t.small)
            if t.excl:
                for k, v in t.r.items():
                    add(k, v, t.small)
        for t in writes:
            if t.w is not None:
                add(t.w[0], t.w[1], t.small)
            for k, v in t.r.items():
                add(k, v, t.small)
        if same:
            deps[eng] = same
        return deps

    def _wait(self, eng, deps):
        seen = self.seen[eng]
        for k, v in deps.items():
            if seen.get(k, 0) >= v:
                continue
            sem = self.sems[k] if k in self.sems else self.dsems[k]
            self.engs[eng].wait_ge(sem, v)
            seen[k] = v
            self.nwait += 1
            kn = self.know.get((k, v))
            if kn:
                for k2, v2 in kn.items():
                    if seen.get(k2, 0) < v2:
                        seen[k2] = v2

    def _commit(self, ev, reads, writes):
        k, v = ev
        for t in writes:
            t.w = ev
            t.r = {}
        for t in reads:
            if t.excl:
                t.w = ev
                t.r = {}
            else:
                if t.r.get(k, 0) < v:
                    t.r[k] = v

    def op(self, eng, fn, reads=(), writes=()):
        deps = self._deps(reads, writes, eng)
        self._wait(eng, deps)
        inst = fn(self.engs[eng])
        if eng == 'pe':
            self.pe_tags.append(self.tag)
        self.cnt[eng] += 1
        inst.then_inc(self.sems[eng], 1)
        self.know[(eng, self.cnt[eng])] = dict(self.seen[eng])
        self._commit((eng, self.cnt[eng]), reads, writes)
        self.ninst += 1
        return inst

    def dma(self, q, key, out, in_, reads=(), writes=()):
        if key not in self.dsems:
            self.dsems[key] = self.es.enter_context(self.nc.semaphore("d_" + key))
            self.dcnt[key] = 0
        deps = self._deps(reads, writes)
        self._wait(q, deps)
        self.dcnt[key] += 16
        self.engs[q].dma_start(out=out, in_=in_).then_inc(self.dsems[key], 16)
        kn = dict(self.seen[q])
        kn[q] = max(kn.get(q, 0), self.cnt[q])
        self.know[(key, self.dcnt[key])] = kn
        self._commit((key, self.dcnt[key]), reads, writes)
        self.ninst += 1

    def finish(self, tiles):
        deps = self._deps(tiles, tiles)
        self._wait("sp", deps)


class Pool:
    def __init__(self, views, small=False):
        self.views = views
        self.tiles = [TT(small=small) for _ in views]
        self.i = 0

    def get(self):
        i = self.i
        self.i = (i + 1) % len(self.views)
        return self.views[i], self.tiles[i]


def build_program(NP, T, with_sample=True, TTK=512):
    nc = bass.Bass("TRN2", target_bir_lowering=False, dynamic_dma_scratch_size=4096)
    NSEQ = NP + (1 if with_sample else 0)
    NTILE = T // TTK
    NBLK = max(T // 128, 9)

    def din(name, shape, dt=F32):
        return nc.dram_tensor(name, list(shape), dt, kind="ExternalInput").ap()

    def dout(name, shape, dt=F32):
        return nc.dram_tensor(name, list(shape), dt, kind="ExternalOutput").ap()

    xT = din("xT", (NP, D, T))
    cT = din("cT", (128, 8, NSEQ))
    vecs = din("vecs", (DEPTH, 128, NV))
    bif = din("bif", (DEPTH, 4, 2))
    lamp = din("lamp", (DEPTH, 1, 256))
    consts = din("consts", (128, 128 * 2 + 4 * 128 + 64))
    delta4 = din("delta4", (4, 16))
    W = {k: din(k, (DEPTH,) + v) for k, v in W_SHAPES.items()}
    WB = {k: nc.dram_tensor(k + "_bf", [DEPTH] + list(v), BF16, kind="Internal").ap() for k, v in W_SHAPES.items()}
    yT = dout("yT", (NP, D, T))
    okT = dout("okT", (DEPTH, NP, 512, T))
    ov = dout("ov", (DEPTH, NP, T, 512))
    oG = dout("oG", (DEPTH, NP, 128, 4 * 130))
    om = dout("om", (DEPTH, NP, 4, 1))
    oconv = dout("oconv", (DEPTH, NP, 128, 24))
    if with_sample:
        xsT = din("xsT", (D, TS))
        ckT = din("ckT", (DEPTH, 4, 128, PAST))
        cvv = din("cvv", (DEPTH, PAST, 512))
        sG = din("sG", (DEPTH, 128, 4 * 130))
        sm = din("sm", (DEPTH, 4, 1))
        sconv = din("sconv", (DEPTH, 128, 24))
        ysT = dout("ysT", (D, TS))
        oksT = dout("oksT", (DEPTH, 512, TS))
        ovs = dout("ovs", (DEPTH, TS, 512))
        oGs = dout("oGs", (DEPTH, 128, 4 * 130))
        oms = dout("oms", (DEPTH, 4, 1))
        oconvs = dout("oconvs", (DEPTH, 128, 24))

    es = ExitStack()
    p = P(nc, es)
    dbg_count = [0]

    def dbg(name, ap, tiles, once=True):
        if DEBUG is None or name not in DEBUG:
            return
        if once and name in dbg_seen:
            return
        dbg_seen.add(name)
        shape = list(ap.shape)
        d = nc.dram_tensor("dbg_" + name, shape, ap.dtype, kind="ExternalOutput").ap()
        p.dma("act", "dbg_" + name, d, ap, reads=tiles)
    dbg_seen = set()

    def sb(name, shape, dt):
        return es.enter_context(nc.sbuf_tensor(name, list(shape), dt))

    x_t = sb("x_t", (128, 8, TTK), F32)
    x_tl = [TT() for _ in range(8)]
    h_t = sb("h_t", (128, 8, TTK), BF16)
    h_tl = [TT() for _ in range(8)]
    U = sb("U", (128, 28, TTK), BF16)
    U_tl = [TT() for _ in range(28)]
    mixin = sb("mixin", (128, 8, TTK), BF16)
    mix_tl = [TT() for _ in range(8)]
    Kc = [sb(f"Kc{l}", (128, 4, max(T, PAST + TS)), BF16) for l in range(DEPTH)]
    Kc_tl = [[TT() for _ in range(4)] for l in range(DEPTH)]
    Vc = [sb(f"Vc{l}", (128, NBLK, 4 * 130), BF16) for l in range(DEPTH)]
    Vc_tl = [[TT() for _ in range(NBLK)] for l in range(DEPTH)]
    G = [sb(f"G{l}", (128, 4 * 130), F32) for l in range(DEPTH)]
    G_tl = [TT() for l in range(DEPTH)]
    carry = [sb(f"carry{l}", (128, 24), F32) for l in range(DEPTH)]
    carry_tl = [TT(small=True) for l in range(DEPTH)]
    mv_t = sb("mv_t", (128, 4, 4 * 130), BF16)
    mv_tl = [TT() for _ in range(4)]
    wslots = Pool([sb(f"wslot{i}", (128, 8 * 512), BF16) for i in range(3)])
    NFP = 6
    FP = Pool([sb(f"fp{i}", (128, 520), F32) for i in range(NFP)])
    ANP = Pool([sb(f"anp{i}", (128, 512), BF16) for i in range(2)])
    MNP = Pool([sb(f"mnp{i}", (128, 512), BF16) for i in range(2)])
    AO = Pool([sb(f"ao{i}", (128, 512), F32) for i in range(2)])
    NHP = 8
    HP = Pool([sb(f"hp{i}", (128, 520), BF16) for i in range(NHP)])
    pre = Pool([sb(f"pre{i}", (128, 3 + TTK), F32) for i in range(2)])
    g_u = sb("g_u", (4, TTK), F32)
    g_lf = sb("g_lf", (4, TTK), F32)
    g_nb = sb("g_nb", (4, TTK), F32)
    g_ek = sb("g_ek", (4, TTK), F32)
    g_cl = sb("g_cl", (4, TTK), F32)
    g_tl = TT()
    g2_tl = TT()
    gsm = sb("gsm", (128, 64), F32)
    gsm_tl = TT(small=True)
    ekcl = sb("ekcl", (128, 4, 8), F32)
    ekcl_tl = [TT(small=True) for _ in range(4)]
    wcb = sb("wcb", (128, 16), F32)
    wcb_tl = TT(small=True)
    gst = [sb(f"gst{l}", (128, 4), F32) for l in range(DEPTH)]
    gst_tl = [TT(small=True) for l in range(DEPTH)]
    smalls = Pool([sb(f"sml{i}", (128, 32), F32) for i in range(8)], small=True)
    cst = sb("cst", (128, 128 * 2 + 4 * 128 + 64), F32)
    cst_tl = TT()
    identb = sb("identb", (128, 128), BF16)
    onesb = sb("onesb", (128, 128), BF16)
    maskb = sb("maskb", (128, 128), BF16)
    ones4 = sb("ones4", (4, 128), F32)
    d4 = sb("d4", (4, 16), F32)
    cb_tl = TT(small=True)
    vec_sb = sb("vec_sb", (128, DEPTH, NV), F32)
    vec_tl = TT(small=True)
    c_sb = sb("c_sb", (128, 8, NSEQ), F32)
    c_bf = sb("c_bf", (128, 8, NSEQ), BF16)
    c_tl = TT(small=True)
    mod = sb("mod", (128, DEPTH, 48, NSEQ), F32)
    mod_tl = TT(small=True)
    mod2 = sb("mod2", (128, DEPTH, 48, NSEQ), F32)
    lam_sb = sb("lam_sb", (128, DEPTH, 256), F32)
    lamv = sb("lamv", (128, DEPTH, 4), F32)
    lam_tl = TT(small=True)
    bif_sb = sb("bif_sb", (4, DEPTH, 4), F32)
    bif_tl = TT(small=True)

    ident_f = cst[:, 0:128]
    mask_f = cst[:, 128:256]
    Dtab = cst[:, 256:256 + 512]
    biasT = cst[:, 768:768 + 64]

    PS = [es.enter_context(nc.psum_tensor(f"ps{i}", [128, 512], F32)) for i in range(8)]
    PA = Pool(PS[0:4])
    PB = Pool(PS[0:6])
    PB.tiles[0:4] = PA.tiles[0:4]
    PC = Pool(PS[6:8])
    PBC = Pool(PS[4:8])
    PBC.tiles = [PB.tiles[4], PB.tiles[5], PC.tiles[0], PC.tiles[1]]
    for pl in (PA, PB, PC):
        for t in pl.tiles:
            t.excl = True

    WB_tl = {}
    order = ["w_in", "w_gate", "w_br_a", "w_br_b", "w_o", "w_gu", "w_down"]
    casts_issued = set()

    def issue_casts(l):
        if l in casts_issued:
            return
        casts_issued.add(l)
        for k in order:
            R = W_SHAPES[k][0]
            t = TT()
            WB_tl[(k, l)] = t
            nsplit = 4 if R * W_SHAPES[k][1] > 2 ** 21 else 1
            rs = R // nsplit
            key = f"wc_{k}{l}"
            if key not in p.dsems:
                p.dsems[key] = es.enter_context(nc.semaphore("d_" + key))
                p.dcnt[key] = 0
            for i in range(nsplit):
                nc.gpsimd.dma_start(out=WB[k][l, i * rs:(i + 1) * rs, :], in_=W[k][l, i * rs:(i + 1) * rs, :]).then_inc(p.dsems[key], 16)
                p.dcnt[key] += 16
            t.w = (key, p.dcnt[key])

    p.dma("act", "c_cst", cst[:], consts[:, :], writes=[cst_tl])
    p.dma("act", "c_vec", vec_sb[:], vecs.rearrange("l p n -> p l n"), writes=[vec_tl])
    p.dma("act", "c_c", c_sb[:], cT[:, :, :], writes=[c_tl])
    p.dma("act", "c_lam", lam_sb[:], lamp.rearrange("l o n -> o l n").broadcast_to([128, DEPTH, 256]), writes=[lam_tl])
    p.dma("act", "c_bif", bif_sb[:, :, 0:2], bif.rearrange("l h t -> h l t"), writes=[bif_tl])
    p.dma("act", "c_d4", d4[:], delta4[:, :], writes=[cb_tl])
    p.op("dve", lambda e: e.tensor_copy(out=identb[:], in_=ident_f), reads=[cst_tl], writes=[cb_tl])
    p.op("dve", lambda e: e.tensor_copy(out=maskb[:], in_=mask_f), reads=[cst_tl], writes=[cb_tl])
    p.op("dve", lambda e: e.memset(onesb[:], 1.0 / 1024.0), writes=[cb_tl])
    p.op("dve", lambda e: e.memset(ones4[:], 1.0), writes=[cb_tl])
    p.op("dve", lambda e: e.tensor_scalar(out=bif_sb[:, :, 2:3], in0=bif_sb[:, :, 1:2], scalar1=-1.0, scalar2=None, op0=ALU.mult),
         reads=[bif_tl], writes=[bif_tl])
    for l in range(DEPTH):
        lam_init = 0.8 - 0.6 * math.exp(-0.3 * l)
        fv, ft = FP.get()
        p.op("dve", lambda e: e.tensor_tensor(out=fv[:, 0:64], in0=lam_sb[:, l, 0:64], in1=lam_sb[:, l, 64:128], op=ALU.mult), reads=[lam_tl], writes=[ft])
        p.op("dve", lambda e: e.tensor_tensor(out=fv[:, 64:128], in0=lam_sb[:, l, 128:192], in1=lam_sb[:, l, 192:256], op=ALU.mult), reads=[lam_tl], writes=[ft])
        p.op("dve", lambda e: e.tensor_reduce(out=lamv[:, l, 1:3], in_=fv[:, 0:128].rearrange("p (a b) -> p a b", a=2), axis=AX.X, op=ALU.add), reads=[ft], writes=[lam_tl])
        p.op("act", lambda e: e.activation(out=lamv[:, l, 1:3], in_=lamv[:, l, 1:3], func=AF.Exp), reads=[lam_tl], writes=[lam_tl])
        p.op("dve", lambda e: e.scalar_tensor_tensor(out=lamv[:, l, 0:1], in0=lamv[:, l, 2:3], scalar=-lam_init, in1=lamv[:, l, 1:2], op0=ALU.add, op1=ALU.subtract),
             reads=[lam_tl], writes=[lam_tl])

    def wload(name, l, k0, nk, c0, ncol):
        view, tl = wslots.get()
        v3 = view[:, 0:nk * ncol].rearrange("p (k c) -> p k c", k=nk)
        src = WB[name][l].rearrange("(k p) c -> p k c", p=128)[:, k0:k0 + nk, c0:c0 + ncol]
        p.dma("sp", f"ws{wslots.views.index(view)}", v3, src, reads=[WB_tl[(name, l)]], writes=[tl])
        return v3, tl

    fv, ft = FP.get()
    p.op("act", lambda e: e.activation(out=c_sb[:].rearrange("p k s -> p (k s)"), in_=c_sb[:].rearrange("p k s -> p (k s)"), func=AF.Silu),
         reads=[c_tl], writes=[c_tl])
    mod_done = set()

    def compute_mod(l):
        if l in mod_done:
            return
        p.tag = 'mod'
        mod_done.add(l)
        sm_save = p.small_mode
        p.small_mode = False
        for g in range(24):
            view, wt = wslots.get()
            wv = view[:].bitcast(F32)[:, 0:8 * 256].rearrange("p (k c) -> p k c", k=8)
            p.dma("sp", f"ws{wslots.views.index(view)}", wv, W["w_ada"][l].rearrange("(k p) c -> p k c", p=128)[:, :, g * 256:(g + 1) * 256], writes=[wt])
            pv, pt = PA.get()
            for cc in range(2):
                for k in range(8):
                    p.op("pe", lambda e: e.matmul(pv[:, cc * NSEQ:(cc + 1) * NSEQ], lhsT=wv[:, k, cc * 128:(cc + 1) * 128], rhs=c_sb[:, k, :],
                                                  start=(k == 0 and cc == 0), stop=(k == 7), skip_group_check=True),
                         reads=[wt, c_tl], writes=[pt])
            p.op("dve", lambda e: e.tensor_tensor(out=mod[:, l, g * 2:(g + 1) * 2, :], in0=pv[:, 0:2 * NSEQ].rearrange("p (c s) -> p c s", c=2),
                                                  in1=vec_sb[:, l, V_BADA + g * 2:V_BADA + (g + 1) * 2].unsqueeze(2).broadcast_to([128, 2, NSEQ]), op=ALU.add),
                 reads=[pt, vec_tl], writes=[modl_tl[l]])
        for (a, mul) in ((8, 1.0), (16, 1.0 / ALPHA), (32, 1.0), (40, 1.0 / ALPHA)):
            p.op("dve", lambda e: e.tensor_scalar(out=mod2[:, l, a:a + 8, :], in0=mod[:, l, a:a + 8, :], scalar1=1.0, scalar2=mul, op0=ALU.add, op1=ALU.mult),
                 reads=[modl_tl[l]], writes=[modl_tl[l]])
        p.small_mode = sm_save

    modl_tl = [TT(small=True) for _ in range(DEPTH)]
    compute_mod(0)
    nc.gpsimd.wait_ge(p.dsems["ws2"], 16 * 6)
    issue_casts(0)

    def layer_norm_stats(src_chunks, src_tls, n, eps):
        mps, mt = PA.get()
        qps, qt = PA.get()
        for j in range(8):
            sqv, sqt = HP.get()
            xbv, xbt = HP.get()
            p.op("act", lambda e: e.activation(out=sqv[:, 0:n], in_=src_chunks[j], func=AF.Square), reads=[src_tls[j]], writes=[sqt])
            p.op("dve", lambda e: e.tensor_copy(out=xbv[:, 0:n], in_=src_chunks[j]), reads=[src_tls[j]], writes=[xbt])
            p.op("pe", lambda e: e.matmul(mps[:, 0:n], lhsT=onesb[:], rhs=xbv[:, 0:n], start=(j == 0), stop=(j == 7)), reads=[xbt, cb_tl], writes=[mt])
            p.op("pe", lambda e: e.matmul(qps[:, 0:n], lhsT=onesb[:], rhs=sqv[:, 0:n], start=(j == 0), stop=(j == 7)), reads=[sqt, cb_tl], writes=[qt])
        b1, b1t = FP.get()
        b2, b2t = FP.get()
        p.op("act", lambda e: e.activation(out=b2[:, 0:n], in_=mps[:, 0:n], func=AF.Square), reads=[mt], writes=[b2t])
        p.op("dve", lambda e: e.tensor_tensor(out=b1[:, 0:n], in0=qps[:, 0:n], in1=b2[:, 0:n], op=ALU.subtract), reads=[qt, b2t], writes=[b1t])
        p.op("dve", lambda e: e.tensor_scalar(out=b1[:, 0:n], in0=b1[:, 0:n], scalar1=0.0, scalar2=eps, op0=ALU.max, op1=ALU.add), reads=[b1t], writes=[b1t])
        p.op("act", lambda e: e.activation(out=b1[:, 0:n], in_=b1[:, 0:n], func=AF.Ln), reads=[b1t], writes=[b1t])
        p.op("act", lambda e: e.activation(out=b1[:, 0:n], in_=b1[:, 0:n], func=AF.Exp, scale=-0.5), reads=[b1t], writes=[b1t])
        p.op("dve", lambda e: e.scalar_tensor_tensor(out=mps[:, 0:n], in0=mps[:, 0:n], scalar=-1.0, in1=b1[:, 0:n], op0=ALU.mult, op1=ALU.mult),
             reads=[mt, b1t], writes=[mt])
        p.op("dve", lambda e: e.tensor_copy(out=qps[:, 0:n], in_=b1[:, 0:n]), reads=[b1t], writes=[qt])
        return (qps, qt), (mps, mt)

    def ln_apply(src, src_tl, dst, dst_tl, n, B1, B2, scale_ap, bias_ap, extra_reads):
        (b1, b1t), (b2, b2t) = B1, B2
        t1, t1t = FP.get()
        p.op("dve", lambda e: e.tensor_tensor(out=t1[:, 0:n], in0=b1[:, 0:n], in1=src, op=ALU.mult), reads=[src_tl, b1t], writes=[t1t])
        p.op("dve", lambda e: e.tensor_tensor(out=t1[:, 0:n], in0=b2[:, 0:n], in1=t1[:, 0:n], op=ALU.add), reads=[t1t, b2t], writes=[t1t])
        p.op("act", lambda e: e.activation(out=dst, in_=t1[:, 0:n], func=AF.Identity, scale=scale_ap, bias=bias_ap), reads=[t1t] + extra_reads, writes=[dst_tl])

    def run_sequence(si, is_sample):
        p.small_mode = is_sample
        Tq = TS if is_sample else T
        TTs = TS if is_sample else TTK
        SUB = TS if is_sample else 128
        NS = TTs // SUB
        ntile = 1 if is_sample else NTILE
        x_src = xsT if is_sample else xT[si]
        y_dst = ysT if is_sample else yT[si]

        for l in range(DEPTH):
            if is_sample:
                p.dma("act", f"g{l}", G[l][:], sG[l], writes=[G_tl[l]])
                p.dma("act", f"cr{l}", carry[l][:], sconv[l], writes=[carry_tl[l]])
                p.op("dve", lambda e: e.memset(gst[l][0:4, 0:1], 0.0), writes=[gst_tl[l]])
                p.dma("act", f"gs{l}", gst[l][0:4, 1:2], sm[l], writes=[gst_tl[l]])
                for hh in range(4):
                    for half in range(2):
                        fv_, ft_ = FP.get()
                        p.dma("act", f"fp{FP.views.index(fv_)}", fv_[:, 0:512], ckT[l, hh, :, half * 512:(half + 1) * 512], writes=[ft_])
                        p.op("pool", lambda e: e.tensor_copy(out=Kc[l][:, hh, half * 512:(half + 1) * 512], in_=fv_[:, 0:512]), reads=[ft_], writes=[Kc_tl[l][hh]])
                for b in range(8):
                    fv_, ft_ = FP.get()
                    p.dma("act", f"fp{FP.views.index(fv_)}", fv_[:, 0:512], cvv[l, b * 128:(b + 1) * 128, :], writes=[ft_])
                    p.op("pool", lambda e: e.memset(Vc[l][:, b, :], 1.0), writes=[Vc_tl[l][b]])
                    p.op("pool", lambda e: e.tensor_copy(out=Vc[l][:, b, :].rearrange("p (h d) -> p h d", h=4)[:, :, 0:128],
                                                         in_=fv_[:, 0:512].rearrange("p (h d) -> p h d", h=4)), reads=[ft_], writes=[Vc_tl[l][b]])
                p.op("pool", lambda e: e.memset(Vc[l][:, 8, :], 1.0), writes=[Vc_tl[l][8]])
            else:
                p.op("pool", lambda e: e.memset(G[l][:], 0.0), writes=[G_tl[l]])
                p.op("pool", lambda e: e.memset(carry[l][:], 0.0), writes=[carry_tl[l]])
                p.op("dve", lambda e: e.memset(gst[l][0:4, 0:2], 0.0), writes=[gst_tl[l]])
                for b in range(T // 128):
                    p.op("pool", lambda e: e.memset(Vc[l][:, b, :], 1.0), writes=[Vc_tl[l][b]])

        for it in range(ntile):
            tok0 = it * TTs
            p.dma("act", "xin", x_t[:, :, 0:TTs], x_src.rearrange("(k p) t -> p k t", p=128)[:, :, tok0:tok0 + TTs], writes=x_tl)
            for l in range(DEPTH):
                run_tile_layer(si, is_sample, l, it, tok0, TTs, SUB, NS)
            p.dma("act", "yout", y_dst.rearrange("(k p) t -> p k t", p=128)[:, :, tok0:tok0 + TTs], x_t[:, :, 0:TTs], reads=x_tl)

        for l in range(DEPTH):
            p.op("dve", lambda e: e.tensor_tensor(out=gst[l][0:4, 2:3], in0=gst[l][0:4, 1:2], in1=gst[l][0:4, 0:1], op=ALU.subtract), reads=[gst_tl[l]], writes=[gst_tl[l]])
            if is_sample:
                p.dma("act", f"g{l}", oGs[l], G[l][:], reads=[G_tl[l]])
                p.dma("act", f"cr{l}", oconvs[l], carry[l][:], reads=[carry_tl[l]])
                p.dma("act", f"gs{l}", oms[l], gst[l][0:4, 2:3], reads=[gst_tl[l]])
            else:
                p.dma("act", f"g{l}", oG[l, si], G[l][:], reads=[G_tl[l]])
                p.dma("act", f"cr{l}", oconv[l, si], carry[l][:], reads=[carry_tl[l]])
                p.dma("act", f"gs{l}", om[l, si], gst[l][0:4, 2:3], reads=[gst_tl[l]])

    def run_tile_layer(si, is_sample, l, it, tok0, n, SUB, NS):
        compute_mod(l)
        mod_tl = modl_tl[l]
        lam_init = 0.8 - 0.6 * math.exp(-0.3 * l)
        mcol = lambda piece, j: mod[:, l, piece * 8 + j, si:si + 1]
        m2col = lambda piece, j: mod2[:, l, piece * 8 + j, si:si + 1]
        vcol = lambda off: vec_sb[:, l, off:off + 1]
        kbase = PAST if is_sample else 0
        xs = [x_t[:, j, 0:n] for j in range(8)]

        p.tag = 'a-ln'
        B1, B2 = layer_norm_stats(xs, x_tl, n, LN_EPS)
        for j in range(8):
            ln_apply(xs[j], x_tl[j], h_t[:, j, 0:n], h_tl[j], n, B1, B2, m2col(1, j), mcol(0, j), [mod_tl])

        def proj_fm(wv, wt, ci, act_chunks, act_tls, nk, k0=0, first=True, last=True, pv=None, pt=None):
            if pv is None:
                pv, pt = PA.get()
            for k in range(nk):
                p.op("pe", lambda e: e.matmul(pv[:, 0:n], lhsT=wv[:, k, ci * 128:(ci + 1) * 128], rhs=act_chunks[k0 + k],
                                              start=(first and k == 0), stop=(last and k == nk - 1)),
                     reads=[wt, act_tls[k0 + k]], writes=[pt])
            return pv, pt

        dbg('B1', B1[0][:, 0:n], [B1[1]])
        dbg('B2', B2[0][:, 0:n], [B2[1]])
        dbg('h', h_t[:, :, 0:n], h_tl)
        _chk('a')
        hs = [h_t[:, k, 0:n] for k in range(8)]

        p.tag = 'b-aq'
        wv, wt = wload("w_in", l, 0, 8, 0, 512)
        aqb = [PA.get() for _ in range(4)]
        for k in range(8):
            for hh in range(4):
                p.op("pe", lambda e: e.matmul(aqb[hh][0][:, 0:n], lhsT=wv[:, k, hh * 128:(hh + 1) * 128], rhs=hs[k], start=(k == 0), stop=(k == 7)),
                     reads=[wt, h_tl[k]], writes=[aqb[hh][1]])
        for hh in range(4):
            pv, pt = aqb[hh]
            p.op("pool", lambda e: e.memset(U[64:128, hh, 0:n], 0.0), writes=[U_tl[hh]])
            p.op("pool", lambda e: e.memset(U[0:64, 24 + hh, 0:n], 0.0), writes=[U_tl[24 + hh]])
            p.op("act", lambda e: e.activation(out=U[0:64, hh, 0:n], in_=pv[0:64, 0:n], func=AF.Copy), reads=[pt], writes=[U_tl[hh]])
            p.op("act", lambda e: e.activation(out=U[64:128, 24 + hh, 0:n], in_=pv[64:128, 0:n], func=AF.Copy), reads=[pt], writes=[U_tl[24 + hh]])
        p.tag = 'b-ak'
        wv, wt = wload("w_in", l, 0, 8, 512, 512)
        for hh in range(4):
            pv, pt = proj_fm(wv, wt, hh, hs, h_tl, 8)
            fv, ft = FP.get()
            p.op("act", lambda e: e.activation(out=fv[:, 0:n], in_=pv[:, 0:n], func=AF.Copy), reads=[pt], writes=[ft])
            p.op("dve", lambda e: e.tensor_copy(out=Kc[l][:, hh, kbase + tok0:kbase + tok0 + n], in_=pv[:, 0:n]), reads=[pt], writes=[Kc_tl[l][hh]])
            dst = oksT[l, hh * 128:(hh + 1) * 128, :] if is_sample else okT[l, si, hh * 128:(hh + 1) * 128, tok0:tok0 + n]
            p.dma("act", f"fp{FP.views.index(fv)}", dst, fv[:, 0:n], reads=[ft])
        p.tag = 'b-av'
        wv, wt = wload("w_in", l, 0, 8, 1024, 512)
        for r in range(NS):
            pv, pt = PA.get()
            for k in range(8):
                p.op("pe", lambda e: e.matmul(pv[0:SUB, :], lhsT=h_t[:, k, r * SUB:(r + 1) * SUB], rhs=wv[:, k, :], start=(k == 0), stop=(k == 7)),
                     reads=[wt, h_tl[k]], writes=[pt])
            fv, ft = FP.get()
            blk = (kbase + tok0) // 128 + r
            p.op("act", lambda e: e.activation(out=fv[0:SUB, 0:512], in_=pv[0:SUB, :], func=AF.Copy), reads=[pt], writes=[ft])
            p.op("dve", lambda e: e.tensor_copy(out=Vc[l][0:SUB, blk, :].rearrange("p (h d) -> p h d", h=4)[:, :, 0:128],
                                                in_=pv[0:SUB, :].rearrange("p (h d) -> p h d", h=4)), reads=[pt], writes=[Vc_tl[l][blk]])
            dst = ovs[l] if is_sample else ov[l, si, tok0 + r * SUB:tok0 + (r + 1) * SUB, :]
            p.dma("act", f"fp{FP.views.index(fv)}", dst, fv[0:SUB, 0:512], reads=[ft])
        p.tag = 'b-mqk'
        for half in range(2):
            wv, wt = wload("w_in", l, 0, 8, 1536 + half * 512, 512)
            for cc in range(4):
                c = half * 4 + cc
                pv, pt = proj_fm(wv, wt, cc, hs, h_tl, 8)
                prv, prt = pre.get()
                p.op("pool", lambda e: e.tensor_copy(out=prv[:, 0:3], in_=carry[l][:, c * 3:c * 3 + 3]), reads=[carry_tl[l]], writes=[prt])
                p.op("act", lambda e: e.activation(out=prv[:, 3:3 + n], in_=pv[:, 0:n], func=AF.Copy), reads=[pt], writes=[prt])
                p.op("pool", lambda e: e.tensor_copy(out=carry[l][:, c * 3:c * 3 + 3], in_=prv[:, n:n + 3]), reads=[prt], writes=[carry_tl[l]])
                yv, yt = FP.get()
                p.op("dve", lambda e: e.tensor_scalar(out=yv[:, 0:n], in0=prv[:, 0:n], scalar1=vcol(V_CONVW + 0 * 8 + c), scalar2=vcol(V_CONVB + c), op0=ALU.mult, op1=ALU.add),
                     reads=[prt, vec_tl], writes=[yt])
                for tap in range(1, 4):
                    p.op("dve", lambda e: e.scalar_tensor_tensor(out=yv[:, 0:n], in0=prv[:, tap:tap + n], scalar=vcol(V_CONVW + tap * 8 + c), in1=yv[:, 0:n], op0=ALU.mult, op1=ALU.add),
                         reads=[prt, vec_tl, yt], writes=[yt])
                p.op("act", lambda e: e.activation(out=U[:, 4 + c, 0:n], in_=yv[:, 0:n], func=AF.Silu), reads=[yt], writes=[U_tl[4 + c]])
        p.tag = 'b-mv'
        wv, wt = wload("w_in", l, 0, 8, 2560, 512)
        for r in range(NS):
            pv, pt = PA.get()
            for k in range(8):
                p.op("pe", lambda e: e.matmul(pv[0:SUB, :], lhsT=h_t[:, k, r * SUB:(r + 1) * SUB], rhs=wv[:, k, :], start=(k == 0), stop=(k == 7)),
                     reads=[wt, h_tl[k]], writes=[pt])
            p.op("pool", lambda e: e.memset(mv_t[:, r, :], 1.0), writes=[mv_tl[r]])
            p.op("dve", lambda e: e.tensor_copy(out=mv_t[0:SUB, r, :].rearrange("p (h d) -> p h d", h=4)[:, :, 0:128],
                                                in_=pv[0:SUB, :].rearrange("p (h d) -> p h d", h=4)), reads=[pt], writes=[mv_tl[r]])
        p.tag = 'b-if'
        wv, wt = wload("w_in", l, 0, 8, 3072, 8)
        pvi, pti = PA.get()
        pvf, ptf = PA.get()
        for k in range(8):
            p.op("pe", lambda e: e.matmul(pvi[0:4, 0:n], lhsT=wv[:, k, 0:4], rhs=hs[k], start=(k == 0), stop=(k == 7)), reads=[wt, h_tl[k]], writes=[pti])
        for k in range(8):
            p.op("pe", lambda e: e.matmul(pvf[0:4, 0:n], lhsT=wv[:, k, 4:8], rhs=hs[k], start=(k == 0), stop=(k == 7)), reads=[wt, h_tl[k]], writes=[ptf])
        p.op("act", lambda e: e.activation(out=g_u[0:4, 0:n], in_=pvi[0:4, 0:n], func=AF.Identity, bias=bif_sb[:, l, 0:1]), reads=[pti, bif_tl], writes=[g_tl])
        p.op("act", lambda e: e.activation(out=g_lf[0:4, 0:n], in_=pvf[0:4, 0:n], func=AF.Exp, scale=-1.0, bias=bif_sb[:, l, 2:3]), reads=[ptf, bif_tl], writes=[g_tl])
        p.op("act", lambda e: e.activation(out=g_lf[0:4, 0:n], in_=g_lf[0:4, 0:n], func=AF.Ln, bias=1.0), reads=[g_tl], writes=[g_tl])
        p.tag = 'b-mo'
        wv, wt = wload("w_in", l, 0, 8, 3080, 512)
        for hh in range(4):
            pv, pt = proj_fm(wv, wt, hh, hs, h_tl, 8)
            p.op("act", lambda e: e.activation(out=U[:, 12 + hh, 0:n], in_=pv[:, 0:n], func=AF.Sigmoid), reads=[pt], writes=[U_tl[12 + hh]])

        _chk('b')
        p.tag = 'c'
        gs_ = gsm[0:4, :]
        p.op("dve", lambda e: e.tensor_tensor_scan(out=g_nb[0:4, 0:n], data0=g_lf[0:4, 0:n], data1=g_lf[0:4, 0:n], initial=gst[l][0:4, 0:1], op0=ALU.add, op1=ALU.max),
             reads=[g_tl, gst_tl[l]], writes=[g_tl])
        p.op("dve", lambda e: e.tensor_tensor(out=g_u[0:4, 0:n], in0=g_u[0:4, 0:n], in1=g_nb[0:4, 0:n], op=ALU.add), reads=[g_tl], writes=[g_tl])
        p.op("dve", lambda e: e.tensor_reduce(out=gs_[:, 0:NS], in_=g_u[0:4, 0:n].rearrange("p (r s) -> p r s", r=NS), axis=AX.X, op=ALU.max), reads=[g_tl], writes=[gsm_tl])
        p.op("dve", lambda e: e.tensor_tensor_scan(out=gs_[:, 8:8 + NS], data0=gs_[:, 0:NS], data1=gs_[:, 0:NS], initial=gst[l][0:4, 1:2], op0=ALU.max, op1=ALU.max),
             reads=[gsm_tl, gst_tl[l]], writes=[gsm_tl])
        p.op("dve", lambda e: e.tensor_copy(out=gs_[:, 16:17], in_=gst[l][0:4, 1:2]), reads=[gst_tl[l]], writes=[gsm_tl])
        if NS > 1:
            p.op("dve", lambda e: e.tensor_copy(out=gs_[:, 17:16 + NS], in_=gs_[:, 8:8 + NS - 1]), reads=[gsm_tl], writes=[gsm_tl])
        p.op("dve", lambda e: e.tensor_tensor(out=gs_[:, 24:24 + NS], in0=gs_[:, 16:16 + NS], in1=gs_[:, 8:8 + NS], op=ALU.subtract), reads=[gsm_tl], writes=[gsm_tl])
        p.op("act", lambda e: e.activation(out=gs_[:, 24:24 + NS], in_=gs_[:, 24:24 + NS], func=AF.Exp), reads=[gsm_tl], writes=[gsm_tl])
        p.op("dve", lambda e: e.tensor_scalar(out=gs_[:, 32:32 + NS], in0=gs_[:, 8:8 + NS], scalar1=-1.0, scalar2=None, op0=ALU.mult), reads=[gsm_tl], writes=[gsm_tl])
        p.op("dve", lambda e: e.tensor_scalar(out=gs_[:, 40:40 + NS], in0=gs_[:, 8:8 + NS], scalar1=-1.0, scalar2=math.log(KSCALE), op0=ALU.mult, op1=ALU.add), reads=[gsm_tl], writes=[gsm_tl])
        for r in range(NS):
            p.op("act", lambda e: e.activation(out=g_ek[0:4, r * SUB:(r + 1) * SUB], in_=g_u[0:4, r * SUB:(r + 1) * SUB], func=AF.Exp, bias=gs_[:, 40 + r:41 + r]),
                 reads=[g_tl, gsm_tl], writes=[g2_tl])
            p.op("act", lambda e: e.activation(out=g_cl[0:4, r * SUB:(r + 1) * SUB], in_=g_nb[0:4, r * SUB:(r + 1) * SUB], func=AF.Exp, bias=gs_[:, 32 + r:33 + r]),
                 reads=[g_tl, gsm_tl], writes=[g2_tl])
        p.op("dve", lambda e: e.tensor_copy(out=gst[l][0:4, 0:1], in_=g_nb[0:4, n - 1:n]), reads=[g_tl], writes=[gst_tl[l]])
        p.op("dve", lambda e: e.tensor_copy(out=gst[l][0:4, 1:2], in_=gs_[:, 8 + NS - 1:8 + NS]), reads=[gsm_tl], writes=[gst_tl[l]])
        for r in range(NS):
            pv, pt = PA.get()
            p.op("pe", lambda e: e.transpose(out=pv[0:SUB, 0:4], in_=g_ek[0:4, r * SUB:(r + 1) * SUB], identity=ident_f[0:4, 0:4]), reads=[g2_tl, cst_tl], writes=[pt])
            p.op("pe", lambda e: e.transpose(out=pv[0:SUB, 4:8], in_=g_cl[0:4, r * SUB:(r + 1) * SUB], identity=ident_f[0:4, 0:4]), reads=[g2_tl, cst_tl], writes=[pt])
            p.op("dve", lambda e: e.tensor_copy(out=ekcl[0:SUB, r, :], in_=pv[0:SUB, 0:8]), reads=[pt], writes=[ekcl_tl[r]])
        p.op("dve", lambda e: e.tensor_tensor(out=gs_[:, 48:48 + NS * 4].rearrange("p (r h) -> p r h", h=4), in0=gs_[:, 24:24 + NS].unsqueeze(2).broadcast_to([4, NS, 4]),
                                              in1=d4[:, 0:NS * 4].rearrange("p (r h) -> p r h", h=4), op=ALU.mult), reads=[gsm_tl, cb_tl], writes=[gsm_tl])
        pv, pt = PA.get()
        p.op("pe", lambda e: e.matmul(pv[:, 0:NS * 4], lhsT=ones4[:, :], rhs=gs_[:, 48:48 + NS * 4], start=True, stop=True), reads=[gsm_tl, cb_tl], writes=[pt])
        p.op("dve", lambda e: e.tensor_copy(out=wcb[:, 0:NS * 4], in_=pv[:, 0:NS * 4]), reads=[pt], writes=[wcb_tl])

        dbg('gu', g_u[0:4, 0:n], [g_tl])
        dbg('gnb', g_nb[0:4, 0:n], [g_tl])
        dbg('gek', g_ek[0:4, 0:n], [g2_tl])
        dbg('gcl', g_cl[0:4, 0:n], [g2_tl])
        dbg('gsm', gsm[0:4, :], [gsm_tl])
        dbg('wcb', wcb[:, :], [wcb_tl])
        dbg('ekcl', ekcl[:, :, :], ekcl_tl)
        dbg('mq', U[:, 4:8, 0:n], U_tl[4:8])
        dbg('mk', U[:, 8:12, 0:n], U_tl[8:12])
        dbg('mv', mv_t[:, :, :], mv_tl)
        _chk('c')
        deferred = []
        for r in range(NS):
            q0 = r * SUB
            qs = slice(q0, q0 + SUB)
            p.tag = 'd-attn'
            if is_sample:
                blocks = [(j, 8 - j, 128, j * 128, False) for j in range(8)] + [(8, 0, TS, PAST, True)]
            else:
                qi = (tok0 + q0) // 128
                blocks = [(j, qi - j, 128, j * 128, False) for j in range(qi)] + [(qi, 0, 128, qi * 128, True)]
            aov, aot = AO.get()
            def live(hh, bi):
                dl = blocks[bi][1]
                return dl == 0 or SLOPES[hh] * (128.0 * dl - 127.0) <= 80.0
            units = [(hh, bi) for hh in range(4) for bi in range(len(blocks)) if live(hh, bi)]
            first_bi = {hh: min(bi for (h2, bi) in units if h2 == hh) for hh in range(4)}
            nb = len(blocks)
            Ocur = {}

            def emit_qk(hh, bi):
                (vb, dl, kb, kc0, diag) = blocks[bi]
                sv, st = PB.get()
                p.op("pe", lambda e: e.matmul(sv[0:kb, 0:2 * SUB].rearrange("p (c s) -> p c s", c=2), lhsT=Kc[l][:, hh, kc0:kc0 + kb],
                                              rhs=U[:, hh:hh + 25:24, qs], start=True, stop=True),
                     reads=[Kc_tl[l][hh], U_tl[hh], U_tl[24 + hh]], writes=[st])
                ptv, ptt = HP.get()
                if not diag:
                    p.op("act", lambda e: e.activation(out=ptv[0:kb, 0:2 * SUB], in_=sv[0:kb, 0:2 * SUB], func=AF.Exp, scale=0.125, bias=biasT[0:kb, hh * 16 + dl:hh * 16 + dl + 1]),
                         reads=[st, cst_tl], writes=[ptt])
                else:
                    tv, tt_ = FP.get()
                    for c in range(2):
                        p.op("dve", lambda e: e.scalar_tensor_tensor(out=tv[0:kb, c * SUB:(c + 1) * SUB], in0=sv[0:kb, c * SUB:(c + 1) * SUB], scalar=0.125,
                                                                     in1=Dtab[0:kb, hh * 128:hh * 128 + SUB], op0=ALU.mult, op1=ALU.add),
                             reads=[st, cst_tl], writes=[tt_])
                    p.op("act", lambda e: e.activation(out=ptv[0:kb, 0:2 * SUB], in_=tv[0:kb, 0:2 * SUB], func=AF.Exp), reads=[tt_], writes=[ptt])
                return ptv, ptt

            def emit_pv(hh, bi, ptv, ptt):
                (vb, dl, kb, kc0, diag) = blocks[bi]
                if bi == first_bi[hh]:
                    Ocur[hh] = PC.get()
                Ov, Ot = Ocur[hh]
                for c in range(2):
                    p.op("pe", lambda e: e.matmul(Ov[0:SUB, c * 130:(c + 1) * 130], lhsT=ptv[0:kb, c * SUB:(c + 1) * SUB], rhs=Vc[l][0:kb, vb, hh * 130:(hh + 1) * 130],
                                                  start=(bi == first_bi[hh] and c == 0), stop=(bi == nb - 1), skip_group_check=True),
                         reads=[ptt, Vc_tl[l][vb]], writes=[Ot])
                if bi != nb - 1:
                    return
                rcv, rct = smalls.get()
                p.op("dve", lambda e: e.reciprocal(out=rcv[0:SUB, 0:1], in_=Ov[0:SUB, 128:129]), reads=[Ot], writes=[rct])
                p.op("dve", lambda e: e.reciprocal(out=rcv[0:SUB, 1:2], in_=Ov[0:SUB, 258:259]), reads=[Ot], writes=[rct])
                p.op("dve", lambda e: e.tensor_scalar(out=rcv[0:SUB, 1:2], in0=rcv[0:SUB, 1:2], scalar1=lamv[0:SUB, l, 0:1], scalar2=None, op0=ALU.mult), reads=[rct, lam_tl], writes=[rct])
                tv, tt_ = FP.get()
                p.op("dve", lambda e: e.tensor_scalar(out=tv[0:SUB, 0:128], in0=Ov[0:SUB, 130:258], scalar1=rcv[0:SUB, 1:2], scalar2=None, op0=ALU.mult), reads=[Ot, rct], writes=[tt_])
                p.op("dve", lambda e: e.scalar_tensor_tensor(out=aov[0:SUB, hh * 128:(hh + 1) * 128], in0=Ov[0:SUB, 0:128], scalar=rcv[0:SUB, 0:1], in1=tv[0:SUB, 0:128],
                                                             op0=ALU.mult, op1=ALU.add), reads=[Ot, rct, tt_], writes=[aot])

            pend = []
            for (hh, bi) in units:
                pt_ = emit_qk(hh, bi)
                pend.append((hh, bi) + pt_)
                if len(pend) > 5:
                    emit_pv(*pend.pop(0))
            while pend:
                emit_pv(*pend.pop(0))
            for fn_ in deferred:
                fn_()
            deferred.clear()
            dbg('ao', aov[0:SUB, :], [aot])
            ssv, sst = smalls.get()
            jv, jt = FP.get()
            p.op("act", lambda e: e.activation(out=jv[0:SUB, 0:512], in_=aov[0:SUB, 0:512], func=AF.Square), reads=[aot], writes=[jt])
            p.op("dve", lambda e: e.tensor_reduce(out=ssv[0:SUB, 0:4], in_=jv[0:SUB, 0:512].rearrange("p (h d) -> p h d", h=4), axis=AX.X, op=ALU.add), reads=[jt], writes=[sst])
            p.op("dve", lambda e: e.tensor_scalar(out=ssv[0:SUB, 0:4], in0=ssv[0:SUB, 0:4], scalar1=1.0 / 128.0, scalar2=LN_EPS, op0=ALU.mult, op1=ALU.add), reads=[sst], writes=[sst])
            p.op("act", lambda e: e.activation(out=ssv[0:SUB, 0:4], in_=ssv[0:SUB, 0:4], func=AF.Ln), reads=[sst], writes=[sst])
            p.op("act", lambda e: e.activation(out=ssv[0:SUB, 0:4], in_=ssv[0:SUB, 0:4], func=AF.Exp, scale=-0.5), reads=[sst], writes=[sst])
            _chk('d0e')
            anv, ant = ANP.get()
            for hh in range(4):
                p.op("act", lambda e: e.activation(out=anv[0:SUB, hh * 128:(hh + 1) * 128], in_=aov[0:SUB, hh * 128:(hh + 1) * 128], func=AF.Identity, scale=ssv[0:SUB, hh:hh + 1]),
                     reads=[aot, sst], writes=[ant])

            def an_transposes(anv=anv, ant=ant, qs=qs):
                pv, pt = PA.get()
                pvb = pv[:].bitcast(BF16)
                for hh in range(4):
                    p.op("pe", lambda e: e.transpose(out=pvb[:, hh * SUB:(hh + 1) * SUB], in_=anv[0:SUB, hh * 128:(hh + 1) * 128], identity=identb[0:SUB, 0:SUB]), reads=[ant, cb_tl], writes=[pt])
                for hh in range(4):
                    p.op("act", lambda e: e.activation(out=U[:, 16 + hh, qs], in_=pvb[:, hh * SUB:(hh + 1) * SUB], func=AF.Identity, scale=vcol(V_DAN + hh)), reads=[pt, vec_tl], writes=[U_tl[16 + hh]])

            _chk('d1')
            p.tag = 'd-mlstm'
            psS, psSt = PA.get()
            for hh in range(4):
                p.op("pe", lambda e: e.matmul(psS[0:SUB, hh * SUB:(hh + 1) * SUB], lhsT=U[:, 8 + hh, qs], rhs=U[:, 4 + hh, qs], start=True, stop=True),
                     reads=[U_tl[8 + hh], U_tl[4 + hh]], writes=[psSt])
            smv, smt = HP.get()
            for hh in range(4):
                p.op("dve", lambda e: e.scalar_tensor_tensor(out=smv[0:SUB, hh * SUB:(hh + 1) * SUB], in0=psS[0:SUB, hh * SUB:(hh + 1) * SUB], scalar=ekcl[0:SUB, r, hh:hh + 1],
                                                             in1=maskb[0:SUB, 0:SUB], op0=ALU.mult, op1=ALU.mult), reads=[psSt, ekcl_tl[r], cb_tl], writes=[smt])
            psK, psKt = PA.get()
            psKb = psK[:].bitcast(BF16)
            for hh in range(4):
                p.op("pe", lambda e: e.transpose(out=psKb[0:SUB, hh * 128:(hh + 1) * 128], in_=U[:, 8 + hh, qs], identity=identb[:, :]), reads=[U_tl[8 + hh], cb_tl], writes=[psKt])
            khv, kht = HP.get()
            for hh in range(4):
                p.op("act", lambda e: e.activation(out=khv[0:SUB, hh * 128:(hh + 1) * 128], in_=psKb[0:SUB, hh * 128:(hh + 1) * 128], func=AF.Identity, scale=ekcl[0:SUB, r, hh:hh + 1]),
                     reads=[psKt, ekcl_tl[r]], writes=[kht])
            for hh in range(4):
                p.op("dve", lambda e: e.tensor_scalar(out=G[l][:, hh * 130:(hh + 1) * 130], in0=G[l][:, hh * 130:(hh + 1) * 130], scalar1=wcb[:, r * 4 + hh:r * 4 + hh + 1], scalar2=None, op0=ALU.mult),
                     reads=[G_tl[l], wcb_tl], writes=[G_tl[l]])
            gbv, gbt = HP.get()
            p.op("act", lambda e: e.activation(out=gbv[:, 0:520], in_=G[l][:, :], func=AF.Copy), reads=[G_tl[l]], writes=[gbt])
            Hps = [PA.get(), PA.get()]
            for hh in range(4):
                hv, ht = Hps[hh // 2]
                o0 = (hh % 2) * 130
                p.op("pe", lambda e: e.matmul(hv[0:SUB, o0:o0 + 130], lhsT=U[:, 4 + hh, qs], rhs=gbv[:, hh * 130:(hh + 1) * 130], start=(hh % 2 == 0), stop=False, skip_group_check=True),
                     reads=[U_tl[4 + hh], gbt], writes=[ht])
                p.op("pe", lambda e: e.matmul(hv[0:SUB, o0:o0 + 130], lhsT=smv[0:SUB, hh * SUB:(hh + 1) * SUB], rhs=mv_t[0:SUB, r, hh * 130:(hh + 1) * 130], start=False, stop=True, skip_group_check=True),
                     reads=[smt, mv_tl[r]], writes=[ht])
            ddv, ddt = smalls.get()
            hmv, hmt = FP.get()
            for hp_ in range(2):
                hv, ht = Hps[hp_]
                den = hv[0:SUB, 0:260].rearrange("p (c d) -> p c d", c=2)[:, :, 128]
                p.op("dve", lambda e: e.tensor_scalar(out=ddv[0:SUB, 8 + hp_ * 2:10 + hp_ * 2], in0=den, scalar1=-1.0, scalar2=None, op0=ALU.mult), reads=[ht], writes=[ddt])
                p.op("dve", lambda e: e.tensor_tensor(out=ddv[0:SUB, 8 + hp_ * 2:10 + hp_ * 2], in0=den, in1=ddv[0:SUB, 8 + hp_ * 2:10 + hp_ * 2], op=ALU.max), reads=[ht, ddt], writes=[ddt])
                p.op("dve", lambda e: e.tensor_tensor(out=ddv[0:SUB, hp_ * 2:hp_ * 2 + 2], in0=ddv[0:SUB, 8 + hp_ * 2:10 + hp_ * 2], in1=ekcl[0:SUB, r, 4 + hp_ * 2:6 + hp_ * 2], op=ALU.max), reads=[ddt, ekcl_tl[r]], writes=[ddt])
            p.op("dve", lambda e: e.reciprocal(out=ddv[0:SUB, 0:4], in_=ddv[0:SUB, 0:4]), reads=[ddt], writes=[ddt])
            for hh in range(4):
                hv, ht = Hps[hh // 2]
                o0 = (hh % 2) * 130
                p.op("act", lambda e: e.activation(out=hmv[0:SUB, hh * 128:(hh + 1) * 128], in_=hv[0:SUB, o0:o0 + 128], func=AF.Identity, scale=ddv[0:SUB, hh:hh + 1]), reads=[ht, ddt], writes=[hmt])
            dbg('hm', hmv[0:SUB, 0:512], [hmt])
            dbg('ddv', ddv[0:SUB, 0:16], [ddt])
            dbg('smv', smv[0:SUB, 0:512], [smt])
            dbg('khv', khv[0:SUB, 0:512], [kht])
            dbg('gbv', gbv[:, 0:520], [gbt])
            dGs = [PA.get(), PA.get()]
            for hh in range(4):
                gv, gt_ = dGs[hh // 2]
                o0 = (hh % 2) * 130
                p.op("pe", lambda e: e.matmul(gv[:, o0:o0 + 130], lhsT=khv[0:SUB, hh * 128:(hh + 1) * 128], rhs=mv_t[0:SUB, r, hh * 130:(hh + 1) * 130], start=(hh % 2 == 0), stop=True, skip_group_check=True),
                     reads=[kht, mv_tl[r]], writes=[gt_])
            an_transposes()
            for hp_ in range(2):
                gv, gt_ = dGs[hp_]
                p.op("dve", lambda e: e.tensor_tensor(out=G[l][:, hp_ * 260:(hp_ + 1) * 260], in0=gv[:, 0:260], in1=G[l][:, hp_ * 260:(hp_ + 1) * 260], op=ALU.add), reads=[gt_, G_tl[l]], writes=[G_tl[l]])
            stv, stt = smalls.get()
            mvv, mvt = smalls.get()
            for hh in range(4):
                p.op("dve", lambda e: e.bn_stats(out=stv[0:SUB, hh * 6:(hh + 1) * 6], in_=hmv[0:SUB, hh * 128:(hh + 1) * 128]), reads=[hmt], writes=[stt])
                p.op("dve", lambda e: e.bn_aggr(out=mvv[0:SUB, hh * 2:(hh + 1) * 2], in_=stv[0:SUB, hh * 6:(hh + 1) * 6]), reads=[stt], writes=[mvt])
            p.op("dve", lambda e: e.tensor_scalar(out=mvv[0:SUB, 8:12], in0=mvv[0:SUB, 0:8].rearrange("p (h t) -> p h t", t=2)[:, :, 1], scalar1=LN_EPS, scalar2=None, op0=ALU.add), reads=[mvt], writes=[mvt])
            p.op("act", lambda e: e.activation(out=mvv[0:SUB, 8:12], in_=mvv[0:SUB, 8:12], func=AF.Ln), reads=[mvt], writes=[mvt])
            p.op("act", lambda e: e.activation(out=mvv[0:SUB, 8:12], in_=mvv[0:SUB, 8:12], func=AF.Exp, scale=-0.5), reads=[mvt], writes=[mvt])
            p.op("dve", lambda e: e.scalar_tensor_tensor(out=mvv[0:SUB, 12:16], in0=mvv[0:SUB, 0:8].rearrange("p (h t) -> p h t", t=2)[:, :, 0], scalar=-1.0, in1=mvv[0:SUB, 8:12], op0=ALU.mult, op1=ALU.mult),
                 reads=[mvt], writes=[mvt])
            mnv, mnt = MNP.get()
            for hh in range(4):
                p.op("act", lambda e: e.activation(out=mnv[0:SUB, hh * 128:(hh + 1) * 128], in_=hmv[0:SUB, hh * 128:(hh + 1) * 128], func=AF.Identity, scale=mvv[0:SUB, 8 + hh:9 + hh], bias=mvv[0:SUB, 12 + hh:13 + hh]),
                     reads=[hmt, mvt], writes=[mnt])

            def mn_transposes(mnv=mnv, mnt=mnt, qs=qs):
                pv, pt = PA.get()
                pvb = pv[:].bitcast(BF16)
                for hh in range(4):
                    p.op("pe", lambda e: e.transpose(out=pvb[:, hh * SUB:(hh + 1) * SUB], in_=mnv[0:SUB, hh * 128:(hh + 1) * 128], identity=identb[0:SUB, 0:SUB]), reads=[mnt, cb_tl], writes=[pt])
                for hh in range(4):
                    p.op("dve", lambda e: e.scalar_tensor_tensor(out=U[:, 20 + hh, qs], in0=pvb[:, hh * SUB:(hh + 1) * SUB], scalar=vcol(V_MN + hh), in1=U[:, 12 + hh, qs], op0=ALU.mult, op1=ALU.mult),
                         reads=[pt, vec_tl, U_tl[12 + hh]], writes=[U_tl[20 + hh]])
            deferred.append(mn_transposes)
        for fn_ in deferred:
            fn_()
        deferred.clear()

        dbg('anT', U[:, 16:20, 0:n], U_tl[16:20])
        dbg('mnT', U[:, 20:24, 0:n], U_tl[20:24])
        dbg('G', G[l][:, :], [G_tl[l]])
        _chk('d')
        issue_casts(1)
        p.tag = 'e-gate'
        for gi, c0 in enumerate((0, 512, 1024, 1536)):
            wv, wt = wload("w_gate", l, 0, 8, c0, 512)
            for cc in range(4):
                gj = gi * 4 + cc
                pv, pt = proj_fm(wv, wt, cc, hs, h_tl, 8)
                p.op("act", lambda e: e.activation(out=U[:, gj, 0:n], in_=pv[:, 0:n], func=AF.Sigmoid, bias=vcol(V_BGATE + gj)), reads=[pt, vec_tl], writes=[U_tl[gj]])
        p.tag = 'e-merge'
        wa, wat = wload("w_br_a", l, 0, 4, 0, 1024)
        wb, wbt = wload("w_br_b", l, 0, 4, 0, 1024)
        an_ch = [U[:, 16 + k, 0:n] for k in range(4)]
        mn_ch = [U[:, 20 + k, 0:n] for k in range(4)]
        for j in range(8):
            pva, pta = proj_fm(wa, wat, j, an_ch, U_tl[16:20], 4)
            pvb_, ptb = proj_fm(wb, wbt, j, mn_ch, U_tl[20:24], 4)
            t1, t1t = FP.get()
            t2, t2t = FP.get()
            p.op("dve", lambda e: e.tensor_tensor(out=t1[:, 0:n], in0=pva[:, 0:n], in1=U[:, j, 0:n], op=ALU.mult), reads=[pta, U_tl[j]], writes=[t1t])
            p.op("dve", lambda e: e.tensor_tensor(out=t2[:, 0:n], in0=pvb_[:, 0:n], in1=U[:, 8 + j, 0:n], op=ALU.mult), reads=[ptb, U_tl[8 + j]], writes=[t2t])
            p.op("pool", lambda e: e.tensor_tensor(out=mixin[:, j, 0:n], in0=t1[:, 0:n], in1=t2[:, 0:n], op=ALU.add), reads=[t1t, t2t], writes=[mix_tl[j]])

        dbg('mixin', mixin[:, :, 0:n], mix_tl)
        _chk('e')
        p.tag = 'f-wo'
        mix_ch = [mixin[:, k, 0:n] for k in range(8)]
        for half in range(2):
            wv, wt = wload("w_o", l, 0, 8, half * 512, 512)
            for cc in range(4):
                j = half * 4 + cc
                pv, pt = proj_fm(wv, wt, cc, mix_ch, mix_tl, 8)
                p.op("dve", lambda e: e.scalar_tensor_tensor(out=xs[j], in0=pv[:, 0:n], scalar=m2col(2, j), in1=xs[j], op0=ALU.mult, op1=ALU.add), reads=[pt, mod_tl, x_tl[j]], writes=[x_tl[j]])
        p.tag = 'f-ln'
        B1, B2 = layer_norm_stats(xs, x_tl, n, LN_EPS / (ALPHA * ALPHA))
        for j in range(8):
            ln_apply(xs[j], x_tl[j], xs[j], x_tl[j], n, B1, B2, vcol(V_LN1G + j), vcol(V_LN1B + j), [vec_tl])

        dbg('x1', x_t[:, :, 0:n], x_tl)
        _chk('f')
        p.tag = 'g-ln'
        B1, B2 = layer_norm_stats(xs, x_tl, n, LN_EPS)
        for j in range(8):
            ln_apply(xs[j], x_tl[j], h_t[:, j, 0:n], h_tl[j], n, B1, B2, m2col(4, j), mcol(3, j), [mod_tl])

        _chk('g')
        p.tag = 'h-gu'
        i0 = 0
        while i0 < NFF:
            nch = min(4, NFF - i0)
            wg_, wgt = wload("w_gu", l, 0, 8, i0 * 128, nch * 128)
            wu_, wut = wload("w_gu", l, 0, 8, DFF + i0 * 128, nch * 128)
            for cc in range(nch):
                i = i0 + cc
                pvg, ptg = proj_fm(wg_, wgt, cc, hs, h_tl, 8)
                pvu, ptu = proj_fm(wu_, wut, cc, hs, h_tl, 8)
                sgv, sgt = FP.get()
                p.op("act", lambda e: e.activation(out=sgv[:, 0:n], in_=pvg[:, 0:n], func=AF.Silu), reads=[ptg], writes=[sgt])
                p.op("dve", lambda e: e.tensor_tensor(out=U[:, i, 0:n], in0=pvu[:, 0:n], in1=sgv[:, 0:n], op=ALU.mult), reads=[ptu, sgt], writes=[U_tl[i]])
            i0 += nch
        p.tag = 'h-down'
        hid_ch = [U[:, i, 0:n] for i in range(NFF)]
        for cg in range(2):
            banks = [(PA if cg == 0 else PBC).get() for _ in range(4)]
            for (k0, nk) in ((0, 8), (8, 8), (16, 6)):
                wv, wt = wload("w_down", l, k0, nk, cg * 512, 512)
                for cc in range(4):
                    proj_fm(wv, wt, cc, hid_ch, U_tl, nk, k0=k0, first=(k0 == 0), last=(k0 == 16), pv=banks[cc][0], pt=banks[cc][1])
            for cc in range(4):
                j = cg * 4 + cc
                pv, pt = banks[cc]
                p.op("dve", lambda e: e.scalar_tensor_tensor(out=xs[j], in0=pv[:, 0:n], scalar=m2col(5, j), in1=xs[j], op0=ALU.mult, op1=ALU.add), reads=[pt, mod_tl, x_tl[j]], writes=[x_tl[j]])
        p.tag = 'h-ln'
        B1, B2 = layer_norm_stats(xs, x_tl, n, LN_EPS / (ALPHA * ALPHA))
        for j in range(8):
            ln_apply(xs[j], x_tl[j], xs[j], x_tl[j], n, B1, B2, vcol(V_LN2G + j), vcol(V_LN2B + j), [vec_tl])

    for l in range(DEPTH):
        lam_init = 0.8 - 0.6 * math.exp(-0.3 * l)
        p.op("dve", lambda e: e.tensor_scalar(out=vec_sb[:, l, V_DAN:V_DAN + 4], in0=vec_sb[:, l, V_DAN:V_DAN + 4], scalar1=1.0 - lam_init, scalar2=None, op0=ALU.mult),
             reads=[vec_tl], writes=[vec_tl])

    try:
        _chk('pro')
        for si in range(NP):
            run_sequence(si, False)
        if with_sample:
            run_sequence(NP, True)
    except _Stop:
        pass

    for key, sem in p.dsems.items():
        if p.dcnt[key] > 0:
            nc.sync.wait_ge(sem, p.dcnt[key])
    p.sbuf_left = nc.sbuf_bytes_remaining
    return nc, p


def _consts():
    ident = np.eye(128, dtype=np.float32)
    s_idx = np.arange(128)[:, None]
    t_idx = np.arange(128)[None, :]
    maskST = (s_idx <= t_idx).astype(np.float32)
    Dtab = np.zeros((128, 4, 128), np.float32)
    biasT = np.zeros((128, 4, 16), np.float32)
    kl = np.arange(128)[:, None].astype(np.float64)
    ql = np.arange(128)[None, :].astype(np.float64)
    vis = (kl // 64) <= (ql // 64)
    for h in range(4):
        s = SLOPES[h]
        d = np.where(kl <= ql, s * kl, s * (2 * ql - kl))
        Dtab[:, h, :] = np.where(vis, d, NEG)
        for dl in range(16):
            biasT[:, h, dl] = s * kl[:, 0] - s * 128.0 * dl
    c = np.concatenate([ident, maskST, Dtab.reshape(128, 512), biasT.reshape(128, 64)], axis=1).astype(np.float32)
    d4 = np.zeros((4, 4, 4), np.float32)
    for h in range(4):
        d4[h, :, h] = 1.0
    return np.ascontiguousarray(c), np.ascontiguousarray(d4.reshape(4, 16))


def _vecs(inp):
    out = np.zeros((DEPTH, 128, NV), np.float32)
    for l in range(DEPTH):
        out[l, :, V_BADA:V_BADA + 48] = inp["b_ada"][l].reshape(48, 128).T
        out[l, :, V_CONVW:V_CONVW + 32] = inp["conv_w"][l].reshape(4, 8, 128).transpose(2, 0, 1).reshape(128, 32)
        out[l, :, V_CONVB:V_CONVB + 8] = inp["conv_b"][l].reshape(8, 128).T
        out[l, :, V_DAN:V_DAN + 4] = inp["da_norm_w"][l].reshape(4, 128).T
        out[l, :, V_MN:V_MN + 4] = inp["m_norm_w"][l].reshape(4, 128).T
        out[l, :, V_BGATE:V_BGATE + 16] = inp["b_gate"][l].reshape(16, 128).T
        out[l, :, V_LN1G:V_LN1G + 8] = inp["ln1_g"][l].reshape(8, 128).T
        out[l, :, V_LN1B:V_LN1B + 8] = inp["ln1_b"][l].reshape(8, 128).T
        out[l, :, V_LN2G:V_LN2G + 8] = inp["ln2_g"][l].reshape(8, 128).T
        out[l, :, V_LN2B:V_LN2B + 8] = inp["ln2_b"][l].reshape(8, 128).T
    return out


_PROG_CACHE = {}


def run_cores(inp, ncores, NP, T, with_sample=True, trace=False):
    key = (NP, T, with_sample)
    if key not in _PROG_CACHE:
        _PROG_CACHE[key] = build_program(NP, T, with_sample)
    nc, p = _PROG_CACHE[key]
    f32 = lambda a: np.ascontiguousarray(np.asarray(a, dtype=np.float32))
    consts, d4 = _consts()
    vecs = _vecs(inp)
    bifh = f32(np.asarray(inp["b_if"]).reshape(DEPTH, 2, 4).transpose(0, 2, 1))
    lamp = f32(np.asarray(inp["lam_p"]).reshape(DEPTH, 1, 256))
    shared = {"vecs": vecs, "bif": bifh, "lamp": lamp, "consts": consts, "delta4": d4}
    for k in W_SHAPES:
        shared[k] = f32(inp[k])
    in_maps = []
    for c in range(ncores):
        m = dict(shared)
        xp = np.asarray(inp["x_prompt"])[c * NP:(c + 1) * NP]
        m["xT"] = f32(xp.transpose(0, 2, 1))
        cs = [np.asarray(inp["c_prompt"])[c * NP + i] for i in range(NP)]
        if with_sample:
            cs.append(np.asarray(inp["c_sample"])[c])
        cmat = np.stack(cs, 0)
        m["cT"] = f32(cmat.reshape(len(cs), 8, 128).transpose(2, 1, 0))
        if with_sample:
            m["xsT"] = f32(np.asarray(inp["x_sample"])[c].T)
            m["ckT"] = f32(np.asarray(inp["cache_attn_k"])[:, c].transpose(0, 2, 3, 1))
            m["cvv"] = f32(np.asarray(inp["cache_attn_v"])[:, c].reshape(DEPTH, PAST, 512))
            C = np.asarray(inp["state_mlstm_C"])[:, c]
            nn = np.asarray(inp["state_mlstm_n"])[:, c]
            sG = np.zeros((DEPTH, 128, 4, 130), np.float32)
            sG[:, :, :, 0:128] = C.transpose(0, 3, 1, 2)
            sG[:, :, :, 128] = nn.transpose(0, 2, 1)
            sG[:, :, :, 129] = nn.transpose(0, 2, 1)
            m["sG"] = f32(sG.reshape(DEPTH, 128, 520))
            m["sm"] = f32(np.asarray(inp["state_mlstm_m"])[:, c].reshape(DEPTH, 4, 1))
            cv = np.asarray(inp["state_mlstm_conv"])[:, c]
            m["sconv"] = f32(cv.reshape(DEPTH, 3, 8, 128).transpose(0, 3, 2, 1).reshape(DEPTH, 128, 24))
        in_maps.append(m)
    res = run_bass_kernel_spmd(nc, in_maps, core_ids=list(range(ncores)), trace=trace)
    return res


def assemble(results, ncores, NP, T, with_sample=True):
    B = ncores * NP
    y = np.zeros((B, T, D), np.float32)
    ak = np.zeros((DEPTH, B, T, 4, 128), np.float32)
    av = np.zeros((DEPTH, B, T, 4, 128), np.float32)
    Cp = np.zeros((DEPTH, B, 4, 128, 128), np.float32)
    npp = np.zeros((DEPTH, B, 4, 128), np.float32)
    mp = np.zeros((DEPTH, B, 4), np.float32)
    cvp = np.zeros((DEPTH, B, 3, 1024), np.float32)
    Bs = ncores
    ys = np.zeros((Bs, TS, D), np.float32)
    aks = np.zeros((DEPTH, Bs, TS, 4, 128), np.float32)
    avs = np.zeros((DEPTH, Bs, TS, 4, 128), np.float32)
    Cs = np.zeros((DEPTH, Bs, 4, 128, 128), np.float32)
    ns = np.zeros((DEPTH, Bs, 4, 128), np.float32)
    ms = np.zeros((DEPTH, Bs, 4), np.float32)
    cvs = np.zeros((DEPTH, Bs, 3, 1024), np.float32)

    def unG(g):
        g = g.reshape(g.shape[:-1] + (4, 130))
        Cc = np.moveaxis(g[..., 0:128], -3, -1)
        nn = np.moveaxis(g[..., 128], -2, -1)
        return Cc, nn

    def unconv(cv):
        cv = cv.reshape(cv.shape[:-1] + (8, 3))
        return np.moveaxis(cv, -1, -3).swapaxes(-1, -2).reshape(cv.shape[:-3] + (3, 1024))

    for c in range(ncores):
        r = results[c]
        sl = slice(c * NP, (c + 1) * NP)
        y[sl] = r["yT"].transpose(0, 2, 1)
        ak[:, sl] = r["okT"].transpose(0, 1, 3, 2).reshape(DEPTH, NP, T, 4, 128)
        av[:, sl] = r["ov"].reshape(DEPTH, NP, T, 4, 128)
        Cc, nn = unG(r["oG"])
        Cp[:, sl] = Cc
        npp[:, sl] = nn
        mp[:, sl] = r["om"][..., 0]
        cvp[:, sl] = unconv(r["oconv"])
        if with_sample:
            ys[c] = r["ysT"].T
            aks[:, c] = r["oksT"].transpose(0, 2, 1).reshape(DEPTH, TS, 4, 128)
            avs[:, c] = r["ovs"].reshape(DEPTH, TS, 4, 128)
            Cc, nn = unG(r["oGs"])
            Cs[:, c] = Cc
            ns[:, c] = nn
            ms[:, c] = r["oms"][..., 0]
            cvs[:, c] = unconv(r["oconvs"])
    return (y, ys, ak, av, aks, avs, Cp, npp, mp, cvp, Cs, ns, ms, cvs)


def kernel(**inputs):
    ncores = 8
    NP = 4
    T = 2048
    res = run_cores(inputs, ncores, NP, T, True)
    return assemble(res.results, ncores, NP, T, True)
```

```python
import math
from contextlib import ExitStack

import numpy as np
import concourse.bass as bass
import concourse.mybir as mybir
from concourse.bass_utils import run_bass_kernel_spmd

F32 = mybir.dt.float32
BF16 = mybir.dt.bfloat16
AF = mybir.ActivationFunctionType
ALU = mybir.AluOpType
AX = mybir.AxisListType

D = 1024
DEPTH = 2
NH = 4
DFF = 2816
NFF = 22
IN_COLS = 3592
LN_EPS = 1e-5
ALPHA = (2 * DEPTH) ** 0.25
SLOPES = [2.0 ** (-8.0 * (i + 1) / 4) for i in range(4)]
PAST = 1024
TS = 32
NEG = -30000.0
KSCALE = 128 ** -0.5

V_BADA = 0
V_CONVW = 48
V_CONVB = 80
V_DAN = 88
V_MN = 92
V_BGATE = 96
V_LN1G = 112
V_LN1B = 120
V_LN2G = 128
V_LN2B = 136
NV = 144

W_SHAPES = {
    "w_ada": (D, 6 * D), "w_in": (D, IN_COLS), "w_br_a": (512, D), "w_br_b": (512, D),
    "w_gate": (D, 2 * D), "w_o": (D, D), "w_gu": (D, 2 * DFF), "w_down": (DFF, D),
}


STOP = None
DEBUG = None


class _Stop(Exception):
    pass


def _chk(tag):
    if STOP == tag:
        raise _Stop()


class TT:
    __slots__ = ("w", "r", "excl", "small")

    def __init__(self, excl=False, small=False):
        self.w = None
        self.r = {}
        self.excl = excl
        self.small = small


class P:
    def __init__(self, nc, es):
        self.nc = nc
        self.es = es
        self.engs = {"pe": nc.tensor, "act": nc.scalar, "dve": nc.vector, "pool": nc.gpsimd, "sp": nc.sync}
        self.sems = {}
        self.cnt = {}
        self.seen = {e: {} for e in self.engs}
        for e in self.engs:
            self.sems[e] = es.enter_context(nc.semaphore("s_" + e))
            self.cnt[e] = 0
        self.dsems = {}
        self.dcnt = {}
        self.nwait = 0
        self.ninst = 0
        self.small_mode = False
        self.tag = ''
        self.pe_tags = []
        self.know = {}

    def _deps(self, reads, writes, eng=None):
        deps = {}
        same = 0
        sm = self.small_mode

        def add(k, v, small):
            nonlocal same
            if k == eng:
                if eng != "pe" and (small or sm) and v > same:
                    same = v
                return
            if deps.get(k, 0) < v:
                deps[k] = v
        for t in reads:
            if t.w is not None:
                add(t.w[0], t.w[1], t.small)
            if t.excl:
                for k, v in t.r.items():
                    add(k, v, t.small)
        for t in writes:
            if t.w is not None:
                add(t.w[0], t.w[1], t.small)
            for k, v in t.r.items():
                add(k, v, t.small)
        if same:
            deps[eng] = same
        return deps

    def _wait(self, eng, deps):
        seen = self.seen[eng]
        for k, v in deps.items():
            if seen.get(k, 0) >= v:
                continue
            sem = self.sems[k] if k in self.sems else self.dsems[k]
            self.engs[eng].wait_ge(sem, v)
            seen[k] = v
            self.nwait += 1
            kn = self.know.get((k, v))
            if kn:
                for k2, v2 in kn.items():
                    if seen.get(k2, 0) < v2:
                        seen[k2] = v2

    def _commit(self, ev, reads, writes):
        k, v = ev
        for t in writes:
            t.w = ev
            t.r = {}
        for t in reads:
            if t.excl:
                t.w = ev
                t.r = {}
            else:
                if t.r.get(k, 0) < v:
                    t.r[k] = v

    def op(self, eng, fn, reads=(), writes=()):
        deps = self._deps(reads, writes, eng)
        self._wait(eng, deps)
        inst = fn(self.engs[eng])
        if eng == 'pe':
            self.pe_tags.append(self.tag)
        self.cnt[eng] += 1
        inst.then_inc(self.sems[eng], 1)
        self.know[(eng, self.cnt[eng])] = dict(self.seen[eng])
        self._commit((eng, self.cnt[eng]), reads, writes)
        self.ninst += 1
        return inst

    def dma(self, q, key, out, in_, reads=(), writes=()):
        if key not in self.dsems:
            self.dsems[key] = self.es.enter_context(self.nc.semaphore("d_" + key))
            self.dcnt[key] = 0
        deps = self._deps(reads, writes)
        self._wait(q, deps)
        self.dcnt[key] += 16
        self.engs[q].dma_start(out=out, in_=in_).then_inc(self.dsems[key], 16)
        kn = dict(self.seen[q])
        kn[q] = max(kn.get(q, 0), self.cnt[q])
        self.know[(key, self.dcnt[key])] = kn
        self._commit((key, self.dcnt[key]), reads, writes)
        self.ninst += 1

    def finish(self, tiles):
        deps = self._deps(tiles, tiles)
        self._wait("sp", deps)


class Pool:
    def __init__(self, views, small=False):
        self.views = views
        self.tiles = [TT(small=small) for _ in views]
        self.i = 0

    def get(self):
        i = self.i
        self.i = (i + 1) % len(self.views)
        return self.views[i], self.tiles[i]


def build_program(NP, T, with_sample=True, TTK=512):
    nc = bass.Bass("TRN2", target_bir_lowering=False, dynamic_dma_scratch_size=4096)
    NSEQ = NP + (1 if with_sample else 0)
    NTILE = T // TTK
    NBLK = max(T // 128, 9)

    def din(name, shape, dt=F32):
        return nc.dram_tensor(name, list(shape), dt, kind="ExternalInput").ap()

    def dout(name, shape, dt=F32):
        return nc.dram_tensor(name, list(shape), dt, kind="ExternalOutput").ap()

    xT = din("xT", (NP, D, T))
    cT = din("cT", (128, 8, NSEQ))
    vecs = din("vecs", (DEPTH, 128, NV))
    bif = din("bif", (DEPTH, 4, 2))
    lamp = din("lamp", (DEPTH, 1, 256))
    consts = din("consts", (128, 128 * 2 + 4 * 128 + 64))
    delta4 = din("delta4", (4, 16))
    W = {k: din(k, (DEPTH,) + v) for k, v in W_SHAPES.items()}
    WB = {k: nc.dram_tensor(k + "_bf", [DEPTH] + list(v), BF16, kind="Internal").ap() for k, v in W_SHAPES.items()}
    yT = dout("yT", (NP, D, T))
    okT = dout("okT", (DEPTH, NP, 512, T))
    ov = dout("ov", (DEPTH, NP, T, 512))
    oG = dout("oG", (DEPTH, NP, 128, 4 * 130))
    om = dout("om", (DEPTH, NP, 4, 1))
    oconv = dout("oconv", (DEPTH, NP, 128, 24))
    if with_sample:
        xsT = din("xsT", (D, TS))
        ckT = din("ckT", (DEPTH, 4, 128, PAST))
        cvv = din("cvv", (DEPTH, PAST, 512))
        sG = din("sG", (DEPTH, 128, 4 * 130))
        sm = din("sm", (DEPTH, 4, 1))
        sconv = din("sconv", (DEPTH, 128, 24))
        ysT = dout("ysT", (D, TS))
        oksT = dout("oksT", (DEPTH, 512, TS))
        ovs = dout("ovs", (DEPTH, TS, 512))
        oGs = dout("oGs", (DEPTH, 128, 4 * 130))
        oms = dout("oms", (DEPTH, 4, 1))
        oconvs = dout("oconvs", (DEPTH, 128, 24))

    es = ExitStack()
    p = P(nc, es)
    dbg_count = [0]

    def dbg(name, ap, tiles, once=True):
        if DEBUG is None or name not in DEBUG:
            return
        if once and name in dbg_seen:
            return
        dbg_seen.add(name)
        shape = list(ap.shape)
        d = nc.dram_tensor("dbg_" + name, shape, ap.dtype, kind="ExternalOutput").ap()
        p.dma("act", "dbg_" + name, d, ap, reads=tiles)
    dbg_seen = set()

    def sb(name, shape, dt):
        return es.enter_context(nc.sbuf_tensor(name, list(shape), dt))

    x_t = sb("x_t", (128, 8, TTK), F32)
    x_tl = [TT() for _ in range(8)]
    h_t = sb("h_t", (128, 8, TTK), BF16)
    h_tl = [TT() for _ in range(8)]
    U = sb("U", (128, 28, TTK), BF16)
    U_tl = [TT() for _ in range(28)]
    mixin = sb("mixin", (128, 8, TTK), BF16)
    mix_tl = [TT() for _ in range(8)]
    Kc = [sb(f"Kc{l}", (128, 4, max(T, PAST + TS)), BF16) for l in range(DEPTH)]
    Kc_tl = [[TT() for _ in range(4)] for l in range(DEPTH)]
    Vc = [sb(f"Vc{l}", (128, NBLK, 4 * 130), BF16) for l in range(DEPTH)]
    Vc_tl = [[TT() for _ in range(NBLK)] for l in range(DEPTH)]
    G = [sb(f"G{l}", (128, 4 * 130), F32) for l in range(DEPTH)]
    G_tl = [TT() for l in range(DEPTH)]
    carry = [sb(f"carry{l}", (128, 24), F32) for l in range(DEPTH)]
    carry_tl = [TT(small=True) for l in range(DEPTH)]
    mv_t = sb("mv_t", (128, 4, 4 * 130), BF16)
    mv_tl = [TT() for _ in range(4)]
    wslots = Pool([sb(f"wslot{i}", (128, 8 * 512), BF16) for i in range(3)])
    NFP = 6
    FP = Pool([sb(f"fp{i}", (128, 520), F32) for i in range(NFP)])
    ANP = Pool([sb(f"anp{i}", (128, 512), BF16) for i in range(2)])
    MNP = Pool([sb(f"mnp{i}", (128, 512), BF16) for i in range(2)])
    AO = Pool([sb(f"ao{i}", (128, 512), F32) for i in range(2)])
    NHP = 8
    HP = Pool([sb(f"hp{i}", (128, 520), BF16) for i in range(NHP)])
    pre = Pool([sb(f"pre{i}", (128, 3 + TTK), F32) for i in range(2)])
    g_u = sb("g_u", (4, TTK), F32)
    g_lf = sb("g_lf", (4, TTK), F32)
    g_nb = sb("g_nb", (4, TTK), F32)
    g_ek = sb("g_ek", (4, TTK), F32)
    g_cl = sb("g_cl", (4, TTK), F32)
    g_tl = TT()
    g2_tl = TT()
    gsm = sb("gsm", (128, 64), F32)
    gsm_tl = TT(small=True)
    ekcl = sb("ekcl", (128, 4, 8), F32)
    ekcl_tl = [TT(small=True) for _ in range(4)]
    wcb = sb("wcb", (128, 16), F32)
    wcb_tl = TT(small=True)
    gst = [sb(f"gst{l}", (128, 4), F32) for l in range(DEPTH)]
    gst_tl = [TT(small=True) for l in range(DEPTH)]
    smalls = Pool([sb(f"sml{i}", (128, 32), F32) for i in range(8)], small=True)
    cst = sb("cst", (128, 128 * 2 + 4 * 128 + 64), F32)
    cst_tl = TT()
    identb = sb("identb", (128, 128), BF16)
    onesb = sb("onesb", (128, 128), BF16)
    maskb = sb("maskb", (128, 128), BF16)
    ones4 = sb("ones4", (4, 128), F32)
    d4 = sb("d4", (4, 16), F32)
    cb_tl = TT(small=True)
    vec_sb = sb("vec_sb", (128, DEPTH, NV), F32)
    vec_tl = TT(small=True)
    c_sb = sb("c_sb", (128, 8, NSEQ), F32)
    c_bf = sb("c_bf", (128, 8, NSEQ), BF16)
    c_tl = TT(small=True)
    mod = sb("mod", (128, DEPTH, 48, NSEQ), F32)
    mod_tl = TT(small=True)
    mod2 = sb("mod2", (128, DEPTH, 48, NSEQ), F32)
    lam_sb = sb("lam_sb", (128, DEPTH, 256), F32)
    lamv = sb("lamv", (128, DEPTH, 4), F32)
    lam_tl = TT(small=True)
    bif_sb = sb("bif_sb", (4, DEPTH, 4), F32)
    bif_tl = TT(small=True)

    ident_f = cst[:, 0:128]
    mask_f = cst[:, 128:256]
    Dtab = cst[:, 256:256 + 512]
    biasT = cst[:, 768:768 + 64]

    PS = [es.enter_context(nc.psum_tensor(f"ps{i}", [128, 512], F32)) for i in range(8)]
    PA = Pool(PS[0:4])
    PB = Pool(PS[0:6])
    PB.tiles[0:4] = PA.tiles[0:4]
    PC = Pool(PS[6:8])
    PBC = Pool(PS[4:8])
    PBC.tiles = [PB.tiles[4], PB.tiles[5], PC.tiles[0], PC.tiles[1]]
    for pl in (PA, PB, PC):
        for t in pl.tiles:
            t.excl = True

    WB_tl = {}
    order = ["w_in", "w_gate", "w_br_a", "w_br_b", "w_o", "w_gu", "w_down"]
    casts_issued = set()

    def issue_casts(l):
        if l in casts_issued:
            return
        casts_issued.add(l)
        for k in order:
            R = W_SHAPES[k][0]
            t = TT()
            WB_tl[(k, l)] = t
            nsplit = 4 if R * W_SHAPES[k][1] > 2 ** 21 else 1
            rs = R // nsplit
            key = f"wc_{k}{l}"
            if key not in p.dsems:
                p.dsems[key] = es.enter_context(nc.semaphore("d_" + key))
                p.dcnt[key] = 0
            for i in range(nsplit):
                nc.gpsimd.dma_start(out=WB[k][l, i * rs:(i + 1) * rs, :], in_=W[k][l, i * rs:(i + 1) * rs, :]).then_inc(p.dsems[key], 16)
                p.dcnt[key] += 16
            t.w = (key, p.dcnt[key])

    p.dma("act", "c_cst", cst[:], consts[:, :], writes=[cst_tl])
    p.dma("act", "c_vec", vec_sb[:], vecs.rearrange("l p n -> p l n"), writes=[vec_tl])
    p.dma("act", "c_c", c_sb[:], cT[:, :, :], writes=[c_tl])
    p.dma("act", "c_lam", lam_sb[:], lamp.rearrange("l o n -> o l n").broadcast_to([128, DEPTH, 256]), writes=[lam_tl])
    p.dma("act", "c_bif", bif_sb[:, :, 0:2], bif.rearrange("l h t -> h l t"), writes=[bif_tl])
    p.dma("act", "c_d4", d4[:], delta4[:, :], writes=[cb_tl])
    p.op("dve", lambda e: e.tensor_copy(out=identb[:], in_=ident_f), reads=[cst_tl], writes=[cb_tl])
    p.op("dve", lambda e: e.tensor_copy(out=maskb[:], in_=mask_f), reads=[cst_tl], writes=[cb_tl])
    p.op("dve", lambda e: e.memset(onesb[:], 1.0 / 1024.0), writes=[cb_tl])
    p.op("dve", lambda e: e.memset(ones4[:], 1.0), writes=[cb_tl])
    p.op("dve", lambda e: e.tensor_scalar(out=bif_sb[:, :, 2:3], in0=bif_sb[:, :, 1:2], scalar1=-1.0, scalar2=None, op0=ALU.mult),
         reads=[bif_tl], writes=[bif_tl])
    for l in range(DEPTH):
        lam_init = 0.8 - 0.6 * math.exp(-0.3 * l)
        fv, ft = FP.get()
        p.op("dve", lambda e: e.tensor_tensor(out=fv[:, 0:64], in0=lam_sb[:, l, 0:64], in1=lam_sb[:, l, 64:128], op=ALU.mult), reads=[lam_tl], writes=[ft])
        p.op("dve", lambda e: e.tensor_tensor(out=fv[:, 64:128], in0=lam_sb[:, l, 128:192], in1=lam_sb[:, l, 192:256], op=ALU.mult), reads=[lam_tl], writes=[ft])
        p.op("dve", lambda e: e.tensor_reduce(out=lamv[:, l, 1:3], in_=fv[:, 0:128].rearrange("p (a b) -> p a b", a=2), axis=AX.X, op=ALU.add), reads=[ft], writes=[lam_tl])
        p.op("act", lambda e: e.activation(out=lamv[:, l, 1:3], in_=lamv[:, l, 1:3], func=AF.Exp), reads=[lam_tl], writes=[lam_tl])
        p.op("dve", lambda e: e.scalar_tensor_tensor(out=lamv[:, l, 0:1], in0=lamv[:, l, 2:3], scalar=-lam_init, in1=lamv[:, l, 1:2], op0=ALU.add, op1=ALU.subtract),
             reads=[lam_tl], writes=[lam_tl])

    def wload(name, l, k0, nk, c0, ncol):
        view, tl = wslots.get()
        v3 = view[:, 0:nk * ncol].rearrange("p (k c) -> p k c", k=nk)
        src = WB[name][l].rearrange("(k p) c -> p k c", p=128)[:, k0:k0 + nk, c0:c0 + ncol]
        p.dma("sp", f"ws{wslots.views.index(view)}", v3, src, reads=[WB_tl[(name, l)]], writes=[tl])
        return v3, tl

    fv, ft = FP.get()
    p.op("act", lambda e: e.activation(out=c_sb[:].rearrange("p k s -> p (k s)"), in_=c_sb[:].rearrange("p k s -> p (k s)"), func=AF.Silu),
         reads=[c_tl], writes=[c_tl])
    mod_done = set()

    def compute_mod(l):
        if l in mod_done:
            return
        p.tag = 'mod'
        mod_done.add(l)
        sm_save = p.small_mode
        p.small_mode = False
        for g in range(24):
            view, wt = wslots.get()
            wv = view[:].bitcast(F32)[:, 0:8 * 256].rearrange("p (k c) -> p k c", k=8)
            p.dma("sp", f"ws{wslots.views.index(view)}", wv, W["w_ada"][l].rearrange("(k p) c -> p k c", p=128)[:, :, g * 256:(g + 1) * 256], writes=[wt])
            pv, pt = PA.get()
            for cc in range(2):
                for k in range(8):
                    p.op("pe", lambda e: e.matmul(pv[:, cc * NSEQ:(cc + 1) * NSEQ], lhsT=wv[:, k, cc * 128:(cc + 1) * 128], rhs=c_sb[:, k, :],
                                                  start=(k == 0 and cc == 0), stop=(k == 7), skip_group_check=True),
                         reads=[wt, c_tl], writes=[pt])
            p.op("dve", lambda e: e.tensor_tensor(out=mod[:, l, g * 2:(g + 1) * 2, :], in0=pv[:, 0:2 * NSEQ].rearrange("p (c s) -> p c s", c=2),
                                                  in1=vec_sb[:, l, V_BADA + g * 2:V_BADA + (g + 1) * 2].unsqueeze(2).broadcast_to([128, 2, NSEQ]), op=ALU.add),
                 reads=[pt, vec_tl], writes=[modl_tl[l]])
        for (a, mul) in ((8, 1.0), (16, 1.0 / ALPHA), (32, 1.0), (40, 1.0 / ALPHA)):
            p.op("dve", lambda e: e.tensor_scalar(out=mod2[:, l, a:a + 8, :], in0=mod[:, l, a:a + 8, :], scalar1=1.0, scalar2=mul, op0=ALU.add, op1=ALU.mult),
                 reads=[modl_tl[l]], writes=[modl_tl[l]])
        p.small_mode = sm_save

    modl_tl = [TT(small=True) for _ in range(DEPTH)]
    compute_mod(0)
    nc.gpsimd.wait_ge(p.dsems["ws2"], 16 * 6)
    issue_casts(0)
    issue_casts(1)

    def layer_norm_stats(src_chunks, src_tls, n, eps):
        mps, mt = PA.get()
        qps, qt = PA.get()
        for j in range(8):
            sqv, sqt = HP.get()
            xbv, xbt = HP.get()
            p.op("act", lambda e: e.activation(out=sqv[:, 0:n], in_=src_chunks[j], func=AF.Square), reads=[src_tls[j]], writes=[sqt])
            p.op("dve", lambda e: e.tensor_copy(out=xbv[:, 0:n], in_=src_chunks[j]), reads=[src_tls[j]], writes=[xbt])
            p.op("pe", lambda e: e.matmul(mps[:, 0:n], lhsT=onesb[:], rhs=xbv[:, 0:n], start=(j == 0), stop=(j == 7)), reads=[xbt, cb_tl], writes=[mt])
            p.op("pe", lambda e: e.matmul(qps[:, 0:n], lhsT=onesb[:], rhs=sqv[:, 0:n], start=(j == 0), stop=(j == 7)), reads=[sqt, cb_tl], writes=[qt])
        b1, b1t = FP.get()
        b2, b2t = FP.get()
        p.op("act", lambda e: e.activation(out=b2[:, 0:n], in_=mps[:, 0:n], func=AF.Square), reads=[mt], writes=[b2t])
        p.op("dve", lambda e: e.tensor_tensor(out=b1[:, 0:n], in0=qps[:, 0:n], in1=b2[:, 0:n], op=ALU.subtract), reads=[qt, b2t], writes=[b1t])
        p.op("dve", lambda e: e.tensor_scalar(out=b1[:, 0:n], in0=b1[:, 0:n], scalar1=0.0, scalar2=eps, op0=ALU.max, op1=ALU.add), reads=[b1t], writes=[b1t])
        p.op("act", lambda e: e.activation(out=b1[:, 0:n], in_=b1[:, 0:n], func=AF.Ln), reads=[b1t], writes=[b1t])
        p.op("act", lambda e: e.activation(out=b1[:, 0:n], in_=b1[:, 0:n], func=AF.Exp, scale=-0.5), reads=[b1t], writes=[b1t])
        p.op("dve", lambda e: e.scalar_tensor_tensor(out=mps[:, 0:n], in0=mps[:, 0:n], scalar=-1.0, in1=b1[:, 0:n], op0=ALU.mult, op1=ALU.mult),
             reads=[mt, b1t], writes=[mt])
        p.op("dve", lambda e: e.tensor_copy(out=qps[:, 0:n], in_=b1[:, 0:n]), reads=[b1t], writes=[qt])
        return (qps, qt), (mps, mt)

    def ln_apply(src, src_tl, dst, dst_tl, n, B1, B2, scale_ap, bias_ap, extra_reads):
        (b1, b1t), (b2, b2t) = B1, B2
        t1, t1t = FP.get()
        p.op("dve", lambda e: e.tensor_tensor(out=t1[:, 0:n], in0=b1[:, 0:n], in1=src, op=ALU.mult), reads=[src_tl, b1t], writes=[t1t])
        p.op("dve", lambda e: e.tensor_tensor(out=t1[:, 0:n], in0=b2[:, 0:n], in1=t1[:, 0:n], op=ALU.add), reads=[t1t, b2t], writes=[t1t])
        p.op("act", lambda e: e.activation(out=dst, in_=t1[:, 0:n], func=AF.Identity, scale=scale_ap, bias=bias_ap), reads=[t1t] + extra_reads, writes=[dst_tl])

    def run_sequence(si, is_sample):
        p.small_mode = is_sample
        Tq = TS if is_sample else T
        TTs = TS if is_sample else TTK
        SUB = TS if is_sample else 128
        NS = TTs // SUB
        ntile = 1 if is_sample else NTILE
        x_src = xsT if is_sample else xT[si]
        y_dst = ysT if is_sample else yT[si]

        for l in range(DEPTH):
            if is_sample:
                p.dma("act", f"g{l}", G[l][:], sG[l], writes=[G_tl[l]])
                p.dma("act", f"cr{l}", carry[l][:], sconv[l], writes=[carry_tl[l]])
                p.op("dve", lambda e: e.memset(gst[l][0:4, 0:1], 0.0), writes=[gst_tl[l]])
                p.dma("act", f"gs{l}", gst[l][0:4, 1:2], sm[l], writes=[gst_tl[l]])
                for hh in range(4):
                    for half in range(2):
                        fv_, ft_ = FP.get()
                        p.dma("act", f"fp{FP.views.index(fv_)}", fv_[:, 0:512], ckT[l, hh, :, half * 512:(half + 1) * 512], writes=[ft_])
                        p.op("pool", lambda e: e.tensor_copy(out=Kc[l][:, hh, half * 512:(half + 1) * 512], in_=fv_[:, 0:512]), reads=[ft_], writes=[Kc_tl[l][hh]])
                for b in range(8):
                    fv_, ft_ = FP.get()
                    p.dma("act", f"fp{FP.views.index(fv_)}", fv_[:, 0:512], cvv[l, b * 128:(b + 1) * 128, :], writes=[ft_])
                    p.op("pool", lambda e: e.memset(Vc[l][:, b, :], 1.0), writes=[Vc_tl[l][b]])
                    p.op("pool", lambda e: e.tensor_copy(out=Vc[l][:, b, :].rearrange("p (h d) -> p h d", h=4)[:, :, 0:128],
                                                         in_=fv_[:, 0:512].rearrange("p (h d) -> p h d", h=4)), reads=[ft_], writes=[Vc_tl[l][b]])
                p.op("pool", lambda e: e.memset(Vc[l][:, 8, :], 1.0), writes=[Vc_tl[l][8]])
            else:
                p.op("pool", lambda e: e.memset(G[l][:], 0.0), writes=[G_tl[l]])
                p.op("pool", lambda e: e.memset(carry[l][:], 0.0), writes=[carry_tl[l]])
                p.op("dve", lambda e: e.memset(gst[l][0:4, 0:2], 0.0), writes=[gst_tl[l]])
                for b in range(T // 128):
                    p.op("pool", lambda e: e.memset(Vc[l][:, b, :], 1.0), writes=[Vc_tl[l][b]])

        for it in range(ntile):
            tok0 = it * TTs
            p.dma("act", "xin", x_t[:, :, 0:TTs], x_src.rearrange("(k p) t -> p k t", p=128)[:, :, tok0:tok0 + TTs], writes=x_tl)
            for l in range(DEPTH):
                run_tile_layer(si, is_sample, l, it, tok0, TTs, SUB, NS)
            p.dma("act", "yout", y_dst.rearrange("(k p) t -> p k t", p=128)[:, :, tok0:tok0 + TTs], x_t[:, :, 0:TTs], reads=x_tl)

        for l in range(DEPTH):
            p.op("dve", lambda e: e.tensor_tensor(out=gst[l][0:4, 2:3], in0=gst[l][0:4, 1:2], in1=gst[l][0:4, 0:1], op=ALU.subtract), reads=[gst_tl[l]], writes=[gst_tl[l]])
            if is_sample:
                p.dma("act", f"g{l}", oGs[l], G[l][:], reads=[G_tl[l]])
                p.dma("act", f"cr{l}", oconvs[l], carry[l][:], reads=[carry_tl[l]])
                p.dma("act", f"gs{l}", oms[l], gst[l][0:4, 2:3], reads=[gst_tl[l]])
            else:
                p.dma("act", f"g{l}", oG[l, si], G[l][:], reads=[G_tl[l]])
                p.dma("act", f"cr{l}", oconv[l, si], carry[l][:], reads=[carry_tl[l]])
                p.dma("act", f"gs{l}", om[l, si], gst[l][0:4, 2:3], reads=[gst_tl[l]])

    def run_tile_layer(si, is_sample, l, it, tok0, n, SUB, NS):
        compute_mod(l)
        mod_tl = modl_tl[l]
        lam_init = 0.8 - 0.6 * math.exp(-0.3 * l)
        mcol = lambda piece, j: mod[:, l, piece * 8 + j, si:si + 1]
        m2col = lambda piece, j: mod2[:, l, piece * 8 + j, si:si + 1]
        vcol = lambda off: vec_sb[:, l, off:off + 1]
        kbase = PAST if is_sample else 0
        xs = [x_t[:, j, 0:n] for j in range(8)]

        p.tag = 'a-ln'
        B1, B2 = layer_norm_stats(xs, x_tl, n, LN_EPS)
        for j in range(8):
            ln_apply(xs[j], x_tl[j], h_t[:, j, 0:n], h_tl[j], n, B1, B2, m2col(1, j), mcol(0, j), [mod_tl])

        def proj_fm(wv, wt, ci, act_chunks, act_tls, nk, k0=0, first=True, last=True, pv=None, pt=None):
            if pv is None:
                pv, pt = PA.get()
            for k in range(nk):
                p.op("pe", lambda e: e.matmul(pv[:, 0:n], lhsT=wv[:, k, ci * 128:(ci + 1) * 128], rhs=act_chunks[k0 + k],
                                              start=(first and k == 0), stop=(last and k == nk - 1)),
                     reads=[wt, act_tls[k0 + k]], writes=[pt])
            return pv, pt

        dbg('B1', B1[0][:, 0:n], [B1[1]])
        dbg('B2', B2[0][:, 0:n], [B2[1]])
        dbg('h', h_t[:, :, 0:n], h_tl)
        _chk('a')
        hs = [h_t[:, k, 0:n] for k in range(8)]

        p.tag = 'b-aq'
        wv, wt = wload("w_in", l, 0, 8, 0, 512)
        aqb = [PA.get() for _ in range(4)]
        for k in range(8):
            for hh in range(4):
                p.op("pe", lambda e: e.matmul(aqb[hh][0][:, 0:n], lhsT=wv[:, k, hh * 128:(hh + 1) * 128], rhs=hs[k], start=(k == 0), stop=(k == 7)),
                     reads=[wt, h_tl[k]], writes=[aqb[hh][1]])
        for hh in range(4):
            pv, pt = aqb[hh]
            p.op("pool", lambda e: e.memset(U[64:128, hh, 0:n], 0.0), writes=[U_tl[hh]])
            p.op("pool", lambda e: e.memset(U[0:64, 24 + hh, 0:n], 0.0), writes=[U_tl[24 + hh]])
            p.op("act", lambda e: e.activation(out=U[0:64, hh, 0:n], in_=pv[0:64, 0:n], func=AF.Copy), reads=[pt], writes=[U_tl[hh]])
            p.op("act", lambda e: e.activation(out=U[64:128, 24 + hh, 0:n], in_=pv[64:128, 0:n], func=AF.Copy), reads=[pt], writes=[U_tl[24 + hh]])
        p.tag = 'b-ak'
        wv, wt = wload("w_in", l, 0, 8, 512, 512)
        for hh in range(4):
            pv, pt = proj_fm(wv, wt, hh, hs, h_tl, 8)
            fv, ft = FP.get()
            p.op("act", lambda e: e.activation(out=fv[:, 0:n], in_=pv[:, 0:n], func=AF.Copy), reads=[pt], writes=[ft])
            p.op("dve", lambda e: e.tensor_copy(out=Kc[l][:, hh, kbase + tok0:kbase + tok0 + n], in_=pv[:, 0:n]), reads=[pt], writes=[Kc_tl[l][hh]])
            dst = oksT[l, hh * 128:(hh + 1) * 128, :] if is_sample else okT[l, si, hh * 128:(hh + 1) * 128, tok0:tok0 + n]
            p.dma("act", f"fp{FP.views.index(fv)}", dst, fv[:, 0:n], reads=[ft])
        p.tag = 'b-av'
        wv, wt = wload("w_in", l, 0, 8, 1024, 512)
        for r in range(NS):
            pv, pt = PA.get()
            for k in range(8):
                p.op("pe", lambda e: e.matmul(pv[0:SUB, :], lhsT=h_t[:, k, r * SUB:(r + 1) * SUB], rhs=wv[:, k, :], start=(k == 0), stop=(k == 7)),
                     reads=[wt, h_tl[k]], writes=[pt])
            fv, ft = FP.get()
            blk = (kbase + tok0) // 128 + r
            p.op("act", lambda e: e.activation(out=fv[0:SUB, 0:512], in_=pv[0:SUB, :], func=AF.Copy), reads=[pt], writes=[ft])
            p.op("dve", lambda e: e.tensor_copy(out=Vc[l][0:SUB, blk, :].rearrange("p (h d) -> p h d", h=4)[:, :, 0:128],
                                                in_=pv[0:SUB, :].rearrange("p (h d) -> p h d", h=4)), reads=[pt], writes=[Vc_tl[l][blk]])
            dst = ovs[l] if is_sample else ov[l, si, tok0 + r * SUB:tok0 + (r + 1) * SUB, :]
            p.dma("act", f"fp{FP.views.index(fv)}", dst, fv[0:SUB, 0:512], reads=[ft])
        p.tag = 'b-mqk'
        for half in range(2):
            wv, wt = wload("w_in", l, 0, 8, 1536 + half * 512, 512)
            for cc in range(4):
                c = half * 4 + cc
                pv, pt = proj_fm(wv, wt, cc, hs, h_tl, 8)
                prv, prt = pre.get()
                p.op("pool", lambda e: e.tensor_copy(out=prv[:, 0:3], in_=carry[l][:, c * 3:c * 3 + 3]), reads=[carry_tl[l]], writes=[prt])
                p.op("act", lambda e: e.activation(out=prv[:, 3:3 + n], in_=pv[:, 0:n], func=AF.Copy), reads=[pt], writes=[prt])
                p.op("pool", lambda e: e.tensor_copy(out=carry[l][:, c * 3:c * 3 + 3], in_=prv[:, n:n + 3]), reads=[prt], writes=[carry_tl[l]])
                yv, yt = FP.get()
                p.op("dve", lambda e: e.tensor_scalar(out=yv[:, 0:n], in0=prv[:, 0:n], scalar1=vcol(V_CONVW + 0 * 8 + c), scalar2=vcol(V_CONVB + c), op0=ALU.mult, op1=ALU.add),
                     reads=[prt, vec_tl], writes=[yt])
                for tap in range(1, 4):
                    p.op("dve", lambda e: e.scalar_tensor_tensor(out=yv[:, 0:n], in0=prv[:, tap:tap + n], scalar=vcol(V_CONVW + tap * 8 + c), in1=yv[:, 0:n], op0=ALU.mult, op1=ALU.add),
                         reads=[prt, vec_tl, yt], writes=[yt])
                p.op("act", lambda e: e.activation(out=U[:, 4 + c, 0:n], in_=yv[:, 0:n], func=AF.Silu), reads=[yt], writes=[U_tl[4 + c]])
        p.tag = 'b-mv'
        wv, wt = wload("w_in", l, 0, 8, 2560, 512)
        for r in range(NS):
            pv, pt = PA.get()
            for k in range(8):
                p.op("pe", lambda e: e.matmul(pv[0:SUB, :], lhsT=h_t[:, k, r * SUB:(r + 1) * SUB], rhs=wv[:, k, :], start=(k == 0), stop=(k == 7)),
                     reads=[wt, h_tl[k]], writes=[pt])
            p.op("pool", lambda e: e.memset(mv_t[:, r, :], 1.0), writes=[mv_tl[r]])
            p.op("dve", lambda e: e.tensor_copy(out=mv_t[0:SUB, r, :].rearrange("p (h d) -> p h d", h=4)[:, :, 0:128],
                                                in_=pv[0:SUB, :].rearrange("p (h d) -> p h d", h=4)), reads=[pt], writes=[mv_tl[r]])
        p.tag = 'b-if'
        wv, wt = wload("w_in", l, 0, 8, 3072, 8)
        pvi, pti = PA.get()
        pvf, ptf = PA.get()
        for k in range(8):
            p.op("pe", lambda e: e.matmul(pvi[0:4, 0:n], lhsT=wv[:, k, 0:4], rhs=hs[k], start=(k == 0), stop=(k == 7)), reads=[wt, h_tl[k]], writes=[pti])
        for k in range(8):
            p.op("pe", lambda e: e.matmul(pvf[0:4, 0:n], lhsT=wv[:, k, 4:8], rhs=hs[k], start=(k == 0), stop=(k == 7)), reads=[wt, h_tl[k]], writes=[ptf])
        p.op("act", lambda e: e.activation(out=g_u[0:4, 0:n], in_=pvi[0:4, 0:n], func=AF.Identity, bias=bif_sb[:, l, 0:1]), reads=[pti, bif_tl], writes=[g_tl])
        p.op("act", lambda e: e.activation(out=g_lf[0:4, 0:n], in_=pvf[0:4, 0:n], func=AF.Exp, scale=-1.0, bias=bif_sb[:, l, 2:3]), reads=[ptf, bif_tl], writes=[g_tl])
        p.op("act", lambda e: e.activation(out=g_lf[0:4, 0:n], in_=g_lf[0:4, 0:n], func=AF.Ln, bias=1.0), reads=[g_tl], writes=[g_tl])
        p.tag = 'b-mo'
        wv, wt = wload("w_in", l, 0, 8, 3080, 512)
        for hh in range(4):
            pv, pt = proj_fm(wv, wt, hh, hs, h_tl, 8)
            p.op("act", lambda e: e.activation(out=U[:, 12 + hh, 0:n], in_=pv[:, 0:n], func=AF.Sigmoid), reads=[pt], writes=[U_tl[12 + hh]])

        _chk('b')
        p.tag = 'c'
        gs_ = gsm[0:4, :]
        p.op("dve", lambda e: e.tensor_tensor_scan(out=g_nb[0:4, 0:n], data0=g_lf[0:4, 0:n], data1=g_lf[0:4, 0:n], initial=gst[l][0:4, 0:1], op0=ALU.add, op1=ALU.max),
             reads=[g_tl, gst_tl[l]], writes=[g_tl])
        p.op("dve", lambda e: e.tensor_tensor(out=g_u[0:4, 0:n], in0=g_u[0:4, 0:n], in1=g_nb[0:4, 0:n], op=ALU.add), reads=[g_tl], writes=[g_tl])
        p.op("dve", lambda e: e.tensor_reduce(out=gs_[:, 0:NS], in_=g_u[0:4, 0:n].rearrange("p (r s) -> p r s", r=NS), axis=AX.X, op=ALU.max), reads=[g_tl], writes=[gsm_tl])
        p.op("dve", lambda e: e.tensor_tensor_scan(out=gs_[:, 8:8 + NS], data0=gs_[:, 0:NS], data1=gs_[:, 0:NS], initial=gst[l][0:4, 1:2], op0=ALU.max, op1=ALU.max),
             reads=[gsm_tl, gst_tl[l]], writes=[gsm_tl])
        p.op("dve", lambda e: e.tensor_copy(out=gs_[:, 16:17], in_=gst[l][0:4, 1:2]), reads=[gst_tl[l]], writes=[gsm_tl])
        if NS > 1:
            p.op("dve", lambda e: e.tensor_copy(out=gs_[:, 17:16 + NS], in_=gs_[:, 8:8 + NS - 1]), reads=[gsm_tl], writes=[gsm_tl])
        p.op("dve", lambda e: e.tensor_tensor(out=gs_[:, 24:24 + NS], in0=gs_[:, 16:16 + NS], in1=gs_[:, 8:8 + NS], op=ALU.subtract), reads=[gsm_tl], writes=[gsm_tl])
        p.op("act", lambda e: e.activation(out=gs_[:, 24:24 + NS], in_=gs_[:, 24:24 + NS], func=AF.Exp), reads=[gsm_tl], writes=[gsm_tl])
        p.op("dve", lambda e: e.tensor_scalar(out=gs_[:, 32:32 + NS], in0=gs_[:, 8:8 + NS], scalar1=-1.0, scalar2=None, op0=ALU.mult), reads=[gsm_tl], writes=[gsm_tl])
        p.op("dve", lambda e: e.tensor_scalar(out=gs_[:, 40:40 + NS], in0=gs_[:, 8:8 + NS], scalar1=-1.0, scalar2=math.log(KSCALE), op0=ALU.mult, op1=ALU.add), reads=[gsm_tl], writes=[gsm_tl])
        for r in range(NS):
            p.op("act", lambda e: e.activation(out=g_ek[0:4, r * SUB:(r + 1) * SUB], in_=g_u[0:4, r * SUB:(r + 1) * SUB], func=AF.Exp, bias=gs_[:, 40 + r:41 + r]),
                 reads=[g_tl, gsm_tl], writes=[g2_tl])
            p.op("act", lambda e: e.activation(out=g_cl[0:4, r * SUB:(r + 1) * SUB], in_=g_nb[0:4, r * SUB:(r + 1) * SUB], func=AF.Exp, bias=gs_[:, 32 + r:33 + r]),
                 reads=[g_tl, gsm_tl], writes=[g2_tl])
        p.op("dve", lambda e: e.tensor_copy(out=gst[l][0:4, 0:1], in_=g_nb[0:4, n - 1:n]), reads=[g_tl], writes=[gst_tl[l]])
        p.op("dve", lambda e: e.tensor_copy(out=gst[l][0:4, 1:2], in_=gs_[:, 8 + NS - 1:8 + NS]), reads=[gsm_tl], writes=[gst_tl[l]])
        for r in range(NS):
            pv, pt = PA.get()
            p.op("pe", lambda e: e.transpose(out=pv[0:SUB, 0:4], in_=g_ek[0:4, r * SUB:(r + 1) * SUB], identity=ident_f[0:4, 0:4]), reads=[g2_tl, cst_tl], writes=[pt])
            p.op("pe", lambda e: e.transpose(out=pv[0:SUB, 4:8], in_=g_cl[0:4, r * SUB:(r + 1) * SUB], identity=ident_f[0:4, 0:4]), reads=[g2_tl, cst_tl], writes=[pt])
            p.op("dve", lambda e: e.tensor_copy(out=ekcl[0:SUB, r, :], in_=pv[0:SUB, 0:8]), reads=[pt], writes=[ekcl_tl[r]])
        p.op("dve", lambda e: e.tensor_tensor(out=gs_[:, 48:48 + NS * 4].rearrange("p (r h) -> p r h", h=4), in0=gs_[:, 24:24 + NS].unsqueeze(2).broadcast_to([4, NS, 4]),
                                              in1=d4[:, 0:NS * 4].rearrange("p (r h) -> p r h", h=4), op=ALU.mult), reads=[gsm_tl, cb_tl], writes=[gsm_tl])
        pv, pt = PA.get()
        p.op("pe", lambda e: e.matmul(pv[:, 0:NS * 4], lhsT=ones4[:, :], rhs=gs_[:, 48:48 + NS * 4], start=True, stop=True), reads=[gsm_tl, cb_tl], writes=[pt])
        p.op("dve", lambda e: e.tensor_copy(out=wcb[:, 0:NS * 4], in_=pv[:, 0:NS * 4]), reads=[pt], writes=[wcb_tl])

        dbg('gu', g_u[0:4, 0:n], [g_tl])
        dbg('gnb', g_nb[0:4, 0:n], [g_tl])
        dbg('gek', g_ek[0:4, 0:n], [g2_tl])
        dbg('gcl', g_cl[0:4, 0:n], [g2_tl])
        dbg('gsm', gsm[0:4, :], [gsm_tl])
        dbg('wcb', wcb[:, :], [wcb_tl])
        dbg('ekcl', ekcl[:, :, :], ekcl_tl)
        dbg('mq', U[:, 4:8, 0:n], U_tl[4:8])
        dbg('mk', U[:, 8:12, 0:n], U_tl[8:12])
        dbg('mv', mv_t[:, :, :], mv_tl)
        _chk('c')
        deferred = []
        for r in range(NS):
            q0 = r * SUB
            qs = slice(q0, q0 + SUB)
            p.tag = 'd-attn'
            if is_sample:
                blocks = [(j, 8 - j, 128, j * 128, False) for j in range(8)] + [(8, 0, TS, PAST, True)]
            else:
                qi = (tok0 + q0) // 128
                blocks = [(j, qi - j, 128, j * 128, False) for j in range(qi)] + [(qi, 0, 128, qi * 128, True)]
            aov, aot = AO.get()
            def live(hh, bi):
                dl = blocks[bi][1]
                return dl == 0 or SLOPES[hh] * (128.0 * dl - 127.0) <= 80.0
            units = [(hh, bi) for hh in range(4) for bi in range(len(blocks)) if live(hh, bi)]
            first_bi = {hh: min(bi for (h2, bi) in units if h2 == hh) for hh in range(4)}
            nb = len(blocks)
            Ocur = {}

            def emit_qk(hh, bi):
                (vb, dl, kb, kc0, diag) = blocks[bi]
                sv, st = PB.get()
                p.op("pe", lambda e: e.matmul(sv[0:kb, 0:2 * SUB].rearrange("p (c s) -> p c s", c=2), lhsT=Kc[l][:, hh, kc0:kc0 + kb],
                                              rhs=U[:, hh:hh + 25:24, qs], start=True, stop=True),
                     reads=[Kc_tl[l][hh], U_tl[hh], U_tl[24 + hh]], writes=[st])
                ptv, ptt = HP.get()
                if not diag:
                    p.op("act", lambda e: e.activation(out=ptv[0:kb, 0:2 * SUB], in_=sv[0:kb, 0:2 * SUB], func=AF.Exp, scale=0.125, bias=biasT[0:kb, hh * 16 + dl:hh * 16 + dl + 1]),
                         reads=[st, cst_tl], writes=[ptt])
                else:
                    tv, tt_ = FP.get()
                    for c in range(2):
                        p.op("dve", lambda e: e.scalar_tensor_tensor(out=tv[0:kb, c * SUB:(c + 1) * SUB], in0=sv[0:kb, c * SUB:(c + 1) * SUB], scalar=0.125,
                                                                     in1=Dtab[0:kb, hh * 128:hh * 128 + SUB], op0=ALU.mult, op1=ALU.add),
                             reads=[st, cst_tl], writes=[tt_])
                    p.op("act", lambda e: e.activation(out=ptv[0:kb, 0:2 * SUB], in_=tv[0:kb, 0:2 * SUB], func=AF.Exp), reads=[tt_], writes=[ptt])
                return ptv, ptt

            def emit_pv(hh, bi, ptv, ptt):
                (vb, dl, kb, kc0, diag) = blocks[bi]
                if bi == first_bi[hh]:
                    Ocur[hh] = PC.get()
                Ov, Ot = Ocur[hh]
                for c in range(2):
                    p.op("pe", lambda e: e.matmul(Ov[0:SUB, c * 130:(c + 1) * 130], lhsT=ptv[0:kb, c * SUB:(c + 1) * SUB], rhs=Vc[l][0:kb, vb, hh * 130:(hh + 1) * 130],
                                                  start=(bi == first_bi[hh] and c == 0), stop=(bi == nb - 1), skip_group_check=True),
                         reads=[ptt, Vc_tl[l][vb]], writes=[Ot])
                if bi != nb - 1:
                    return
                rcv, rct = smalls.get()
                p.op("dve", lambda e: e.reciprocal(out=rcv[0:SUB, 0:1], in_=Ov[0:SUB, 128:129]), reads=[Ot], writes=[rct])
                p.op("dve", lambda e: e.reciprocal(out=rcv[0:SUB, 1:2], in_=Ov[0:SUB, 258:259]), reads=[Ot], writes=[rct])
                p.op("dve", lambda e: e.tensor_scalar(out=rcv[0:SUB, 1:2], in0=rcv[0:SUB, 1:2], scalar1=lamv[0:SUB, l, 0:1], scalar2=None, op0=ALU.mult), reads=[rct, lam_tl], writes=[rct])
                tv, tt_ = FP.get()
                p.op("dve", lambda e: e.tensor_scalar(out=tv[0:SUB, 0:128], in0=Ov[0:SUB, 130:258], scalar1=rcv[0:SUB, 1:2], scalar2=None, op0=ALU.mult), reads=[Ot, rct], writes=[tt_])
                p.op("dve", lambda e: e.scalar_tensor_tensor(out=aov[0:SUB, hh * 128:(hh + 1) * 128], in0=Ov[0:SUB, 0:128], scalar=rcv[0:SUB, 0:1], in1=tv[0:SUB, 0:128],
                                                             op0=ALU.mult, op1=ALU.add), reads=[Ot, rct, tt_], writes=[aot])

            pend = []
            for (hh, bi) in units:
                pt_ = emit_qk(hh, bi)
                pend.append((hh, bi) + pt_)
                if len(pend) > 5:
                    emit_pv(*pend.pop(0))
            while pend:
                emit_pv(*pend.pop(0))
            for fn_ in deferred:
                fn_()
            deferred.clear()
            dbg('ao', aov[0:SUB, :], [aot])
            ssv, sst = smalls.get()
            jv, jt = FP.get()
            p.op("act", lambda e: e.activation(out=jv[0:SUB, 0:512], in_=aov[0:SUB, 0:512], func=AF.Square), reads=[aot], writes=[jt])
            p.op("dve", lambda e: e.tensor_reduce(out=ssv[0:SUB, 0:4], in_=jv[0:SUB, 0:512].rearrange("p (h d) -> p h d", h=4), axis=AX.X, op=ALU.add), reads=[jt], writes=[sst])
            p.op("dve", lambda e: e.tensor_scalar(out=ssv[0:SUB, 0:4], in0=ssv[0:SUB, 0:4], scalar1=1.0 / 128.0, scalar2=LN_EPS, op0=ALU.mult, op1=ALU.add), reads=[sst], writes=[sst])
            p.op("act", lambda e: e.activation(out=ssv[0:SUB, 0:4], in_=ssv[0:SUB, 0:4], func=AF.Ln), reads=[sst], writes=[sst])
            p.op("act", lambda e: e.activation(out=ssv[0:SUB, 0:4], in_=ssv[0:SUB, 0:4], func=AF.Exp, scale=-0.5), reads=[sst], writes=[sst])
            _chk('d0e')
            anv, ant = ANP.get()
            for hh in range(4):
                p.op("act", lambda e: e.activation(out=anv[0:SUB, hh * 128:(hh + 1) * 128], in_=aov[0:SUB, hh * 128:(hh + 1) * 128], func=AF.Identity, scale=ssv[0:SUB, hh:hh + 1]),
                     reads=[aot, sst], writes=[ant])

            def an_transposes(anv=anv, ant=ant, qs=qs):
                pv, pt = PA.get()
                pvb = pv[:].bitcast(BF16)
                for hh in range(4):
                    p.op("pe", lambda e: e.transpose(out=pvb[:, hh * SUB:(hh + 1) * SUB], in_=anv[0:SUB, hh * 128:(hh + 1) * 128], identity=identb[0:SUB, 0:SUB]), reads=[ant, cb_tl], writes=[pt])
                for hh in range(4):
                    p.op("act", lambda e: e.activation(out=U[:, 16 + hh, qs], in_=pvb[:, hh * SUB:(hh + 1) * SUB], func=AF.Identity, scale=vcol(V_DAN + hh)), reads=[pt, vec_tl], writes=[U_tl[16 + hh]])

            _chk('d1')
            p.tag = 'd-mlstm'
            psS, psSt = PA.get()
            for hh in range(4):
                p.op("pe", lambda e: e.matmul(psS[0:SUB, hh * SUB:(hh + 1) * SUB], lhsT=U[:, 8 + hh, qs], rhs=U[:, 4 + hh, qs], start=True, stop=True),
                     reads=[U_tl[8 + hh], U_tl[4 + hh]], writes=[psSt])
            smv, smt = HP.get()
            for hh in range(4):
                p.op("dve", lambda e: e.scalar_tensor_tensor(out=smv[0:SUB, hh * SUB:(hh + 1) * SUB], in0=psS[0:SUB, hh * SUB:(hh + 1) * SUB], scalar=ekcl[0:SUB, r, hh:hh + 1],
                                                             in1=maskb[0:SUB, 0:SUB], op0=ALU.mult, op1=ALU.mult), reads=[psSt, ekcl_tl[r], cb_tl], writes=[smt])
            psK, psKt = PA.get()
            psKb = psK[:].bitcast(BF16)
            for hh in range(4):
                p.op("pe", lambda e: e.transpose(out=psKb[0:SUB, hh * 128:(hh + 1) * 128], in_=U[:, 8 + hh, qs], identity=identb[:, :]), reads=[U_tl[8 + hh], cb_tl], writes=[psKt])
            khv, kht = HP.get()
            for hh in range(4):
                p.op("act", lambda e: e.activation(out=khv[0:SUB, hh * 128:(hh + 1) * 128], in_=psKb[0:SUB, hh * 128:(hh + 1) * 128], func=AF.Identity, scale=ekcl[0:SUB, r, hh:hh + 1]),
                     reads=[psKt, ekcl_tl[r]], writes=[kht])
            for hh in range(4):
                p.op("dve", lambda e: e.tensor_scalar(out=G[l][:, hh * 130:(hh + 1) * 130], in0=G[l][:, hh * 130:(hh + 1) * 130], scalar1=wcb[:, r * 4 + hh:r * 4 + hh + 1], scalar2=None, op0=ALU.mult),
                     reads=[G_tl[l], wcb_tl], writes=[G_tl[l]])
            gbv, gbt = HP.get()
            p.op("act", lambda e: e.activation(out=gbv[:, 0:520], in_=G[l][:, :], func=AF.Copy), reads=[G_tl[l]], writes=[gbt])
            Hps = [PA.get(), PA.get()]
            for hh in range(4):
                hv, ht = Hps[hh // 2]
                o0 = (hh % 2) * 130
                p.op("pe", lambda e: e.matmul(hv[0:SUB, o0:o0 + 130], lhsT=U[:, 4 + hh, qs], rhs=gbv[:, hh * 130:(hh + 1) * 130], start=(hh % 2 == 0), stop=False, skip_group_check=True),
                     reads=[U_tl[4 + hh], gbt], writes=[ht])
                p.op("pe", lambda e: e.matmul(hv[0:SUB, o0:o0 + 130], lhsT=smv[0:SUB, hh * SUB:(hh + 1) * SUB], rhs=mv_t[0:SUB, r, hh * 130:(hh + 1) * 130], start=False, stop=True, skip_group_check=True),
                     reads=[smt, mv_tl[r]], writes=[ht])
            ddv, ddt = smalls.get()
            hmv, hmt = FP.get()
            for hp_ in range(2):
                hv, ht = Hps[hp_]
                den = hv[0:SUB, 0:260].rearrange("p (c d) -> p c d", c=2)[:, :, 128]
                p.op("dve", lambda e: e.tensor_scalar(out=ddv[0:SUB, 8 + hp_ * 2:10 + hp_ * 2], in0=den, scalar1=-1.0, scalar2=None, op0=ALU.mult), reads=[ht], writes=[ddt])
                p.op("dve", lambda e: e.tensor_tensor(out=ddv[0:SUB, 8 + hp_ * 2:10 + hp_ * 2], in0=den, in1=ddv[0:SUB, 8 + hp_ * 2:10 + hp_ * 2], op=ALU.max), reads=[ht, ddt], writes=[ddt])
                p.op("dve", lambda e: e.tensor_tensor(out=ddv[0:SUB, hp_ * 2:hp_ * 2 + 2], in0=ddv[0:SUB, 8 + hp_ * 2:10 + hp_ * 2], in1=ekcl[0:SUB, r, 4 + hp_ * 2:6 + hp_ * 2], op=ALU.max), reads=[ddt, ekcl_tl[r]], writes=[ddt])
            p.op("dve", lambda e: e.reciprocal(out=ddv[0:SUB, 0:4], in_=ddv[0:SUB, 0:4]), reads=[ddt], writes=[ddt])
            for hh in range(4):
                hv, ht = Hps[hh // 2]
                o0 = (hh % 2) * 130
                p.op("act", lambda e: e.activation(out=hmv[0:SUB, hh * 128:(hh + 1) * 128], in_=hv[0:SUB, o0:o0 + 128], func=AF.Identity, scale=ddv[0:SUB, hh:hh + 1]), reads=[ht, ddt], writes=[hmt])
            dbg('hm', hmv[0:SUB, 0:512], [hmt])
            dbg('ddv', ddv[0:SUB, 0:16], [ddt])
            dbg('smv', smv[0:SUB, 0:512], [smt])
            dbg('khv', khv[0:SUB, 0:512], [kht])
            dbg('gbv', gbv[:, 0:520], [gbt])
            dGs = [PA.get(), PA.get()]
            for hh in range(4):
                gv, gt_ = dGs[hh // 2]
                o0 = (hh % 2) * 130
                p.op("pe", lambda e: e.matmul(gv[:, o0:o0 + 130], lhsT=khv[0:SUB, hh * 128:(hh + 1) * 128], rhs=mv_t[0:SUB, r, hh * 130:(hh + 1) * 130], start=(hh % 2 == 0), stop=True, skip_group_check=True),
                     reads=[kht, mv_tl[r]], writes=[gt_])
            an_transposes()
            for hp_ in range(2):
                gv, gt_ = dGs[hp_]
                p.op("dve", lambda e: e.tensor_tensor(out=G[l][:, hp_ * 260:(hp_ + 1) * 260], in0=gv[:, 0:260], in1=G[l][:, hp_ * 260:(hp_ + 1) * 260], op=ALU.add), reads=[gt_, G_tl[l]], writes=[G_tl[l]])
            stv, stt = smalls.get()
            mvv, mvt = smalls.get()
            for hh in range(4):
                p.op("dve", lambda e: e.bn_stats(out=stv[0:SUB, hh * 6:(hh + 1) * 6], in_=hmv[0:SUB, hh * 128:(hh + 1) * 128]), reads=[hmt], writes=[stt])
                p.op("dve", lambda e: e.bn_aggr(out=mvv[0:SUB, hh * 2:(hh + 1) * 2], in_=stv[0:SUB, hh * 6:(hh + 1) * 6]), reads=[stt], writes=[mvt])
            p.op("dve", lambda e: e.tensor_scalar(out=mvv[0:SUB, 8:12], in0=mvv[0:SUB, 0:8].rearrange("p (h t) -> p h t", t=2)[:, :, 1], scalar1=LN_EPS, scalar2=None, op0=ALU.add), reads=[mvt], writes=[mvt])
            p.op("act", lambda e: e.activation(out=mvv[0:SUB, 8:12], in_=mvv[0:SUB, 8:12], func=AF.Ln), reads=[mvt], writes=[mvt])
            p.op("act", lambda e: e.activation(out=mvv[0:SUB, 8:12], in_=mvv[0:SUB, 8:12], func=AF.Exp, scale=-0.5), reads=[mvt], writes=[mvt])
            p.op("dve", lambda e: e.scalar_tensor_tensor(out=mvv[0:SUB, 12:16], in0=mvv[0:SUB, 0:8].rearrange("p (h t) -> p h t", t=2)[:, :, 0], scalar=-1.0, in1=mvv[0:SUB, 8:12], op0=ALU.mult, op1=ALU.mult),
                 reads=[mvt], writes=[mvt])
            mnv, mnt = MNP.get()
            for hh in range(4):
                p.op("act", lambda e: e.activation(out=mnv[0:SUB, hh * 128:(hh + 1) * 128], in_=hmv[0:SUB, hh * 128:(hh + 1) * 128], func=AF.Identity, scale=mvv[0:SUB, 8 + hh:9 + hh], bias=mvv[0:SUB, 12 + hh:13 + hh]),
                     reads=[hmt, mvt], writes=[mnt])

            def mn_transposes(mnv=mnv, mnt=mnt, qs=qs):
                pv, pt = PA.get()
                pvb = pv[:].bitcast(BF16)
                for hh in range(4):
                    p.op("pe", lambda e: e.transpose(out=pvb[:, hh * SUB:(hh + 1) * SUB], in_=mnv[0:SUB, hh * 128:(hh + 1) * 128], identity=identb[0:SUB, 0:SUB]), reads=[mnt, cb_tl], writes=[pt])
                for hh in range(4):
                    p.op("dve", lambda e: e.scalar_tensor_tensor(out=U[:, 20 + hh, qs], in0=pvb[:, hh * SUB:(hh + 1) * SUB], scalar=vcol(V_MN + hh), in1=U[:, 12 + hh, qs], op0=ALU.mult, op1=ALU.mult),
                         reads=[pt, vec_tl, U_tl[12 + hh]], writes=[U_tl[20 + hh]])
            deferred.append(mn_transposes)
        for fn_ in deferred:
            fn_()
        deferred.clear()

        dbg('anT', U[:, 16:20, 0:n], U_tl[16:20])
        dbg('mnT', U[:, 20:24, 0:n], U_tl[20:24])
        dbg('G', G[l][:, :], [G_tl[l]])
        _chk('d')
        issue_casts(1)
        p.tag = 'e-gate'
        for gi, c0 in enumerate((0, 512, 1024, 1536)):
            wv, wt = wload("w_gate", l, 0, 8, c0, 512)
            for cc in range(4):
                gj = gi * 4 + cc
                pv, pt = proj_fm(wv, wt, cc, hs, h_tl, 8)
                p.op("act", lambda e: e.activation(out=U[:, gj, 0:n], in_=pv[:, 0:n], func=AF.Sigmoid, bias=vcol(V_BGATE + gj)), reads=[pt, vec_tl], writes=[U_tl[gj]])
        p.tag = 'e-merge'
        wa, wat = wload("w_br_a", l, 0, 4, 0, 1024)
        wb, wbt = wload("w_br_b", l, 0, 4, 0, 1024)
        an_ch = [U[:, 16 + k, 0:n] for k in range(4)]
        mn_ch = [U[:, 20 + k, 0:n] for k in range(4)]
        for j in range(8):
            pva, pta = proj_fm(wa, wat, j, an_ch, U_tl[16:20], 4)
            pvb_, ptb = proj_fm(wb, wbt, j, mn_ch, U_tl[20:24], 4)
            t1, t1t = FP.get()
            t2, t2t = FP.get()
            p.op("dve", lambda e: e.tensor_tensor(out=t1[:, 0:n], in0=pva[:, 0:n], in1=U[:, j, 0:n], op=ALU.mult), reads=[pta, U_tl[j]], writes=[t1t])
            p.op("dve", lambda e: e.tensor_tensor(out=t2[:, 0:n], in0=pvb_[:, 0:n], in1=U[:, 8 + j, 0:n], op=ALU.mult), reads=[ptb, U_tl[8 + j]], writes=[t2t])
            p.op("pool", lambda e: e.tensor_tensor(out=mixin[:, j, 0:n], in0=t1[:, 0:n], in1=t2[:, 0:n], op=ALU.add), reads=[t1t, t2t], writes=[mix_tl[j]])

        dbg('mixin', mixin[:, :, 0:n], mix_tl)
        _chk('e')
        p.tag = 'f-wo'
        mix_ch = [mixin[:, k, 0:n] for k in range(8)]
        for half in range(2):
            wv, wt = wload("w_o", l, 0, 8, half * 512, 512)
            for cc in range(4):
                j = half * 4 + cc
                pv, pt = proj_fm(wv, wt, cc, mix_ch, mix_tl, 8)
                p.op("dve", lambda e: e.scalar_tensor_tensor(out=xs[j], in0=pv[:, 0:n], scalar=m2col(2, j), in1=xs[j], op0=ALU.mult, op1=ALU.add), reads=[pt, mod_tl, x_tl[j]], writes=[x_tl[j]])
        p.tag = 'f-ln'
        B1, B2 = layer_norm_stats(xs, x_tl, n, LN_EPS / (ALPHA * ALPHA))
        for j in range(8):
            ln_apply(xs[j], x_tl[j], xs[j], x_tl[j], n, B1, B2, vcol(V_LN1G + j), vcol(V_LN1B + j), [vec_tl])

        dbg('x1', x_t[:, :, 0:n], x_tl)
        _chk('f')
        p.tag = 'g-ln'
        B1, B2 = layer_norm_stats(xs, x_tl, n, LN_EPS)
        for j in range(8):
            ln_apply(xs[j], x_tl[j], h_t[:, j, 0:n], h_tl[j], n, B1, B2, m2col(4, j), mcol(3, j), [mod_tl])

        _chk('g')
        p.tag = 'h-gu'
        i0 = 0
        while i0 < NFF:
            nch = min(4, NFF - i0)
            wg_, wgt = wload("w_gu", l, 0, 8, i0 * 128, nch * 128)
            wu_, wut = wload("w_gu", l, 0, 8, DFF + i0 * 128, nch * 128)
            for cc in range(nch):
                i = i0 + cc
                pvg, ptg = proj_fm(wg_, wgt, cc, hs, h_tl, 8)
                pvu, ptu = proj_fm(wu_, wut, cc, hs, h_tl, 8)
                sgv, sgt = FP.get()
                p.op("act", lambda e: e.activation(out=sgv[:, 0:n], in_=pvg[:, 0:n], func=AF.Silu), reads=[ptg], writes=[sgt])
                p.op("dve", lambda e: e.tensor_tensor(out=U[:, i, 0:n], in0=pvu[:, 0:n], in1=sgv[:, 0:n], op=ALU.mult), reads=[ptu, sgt], writes=[U_tl[i]])
            i0 += nch
        p.tag = 'h-down'
        hid_ch = [U[:, i, 0:n] for i in range(NFF)]
        for cg in range(2):
            banks = [(PA if cg == 0 else PBC).get() for _ in range(4)]
            for (k0, nk) in ((0, 8), (8, 8), (16, 6)):
                wv, wt = wload("w_down", l, k0, nk, cg * 512, 512)
                for cc in range(4):
                    proj_fm(wv, wt, cc, hid_ch, U_tl, nk, k0=k0, first=(k0 == 0), last=(k0 == 16), pv=banks[cc][0], pt=banks[cc][1])
            for cc in range(4):
                j = cg * 4 + cc
                pv, pt = banks[cc]
                p.op("dve", lambda e: e.scalar_tensor_tensor(out=xs[j], in0=pv[:, 0:n], scalar=m2col(5, j), in1=xs[j], op0=ALU.mult, op1=ALU.add), reads=[pt, mod_tl, x_tl[j]], writes=[x_tl[j]])
        p.tag = 'h-ln'
        B1, B2 = layer_norm_stats(xs, x_tl, n, LN_EPS / (ALPHA * ALPHA))
        for j in range(8):
            ln_apply(xs[j], x_tl[j], xs[j], x_tl[j], n, B1, B2, vcol(V_LN2G + j), vcol(V_LN2B + j), [vec_tl])

    for l in range(DEPTH):
        lam_init = 0.8 - 0.6 * math.exp(-0.3 * l)
        p.op("dve", lambda e: e.tensor_scalar(out=vec_sb[:, l, V_DAN:V_DAN + 4], in0=vec_sb[:, l, V_DAN:V_DAN + 4], scalar1=1.0 - lam_init, scalar2=None, op0=ALU.mult),
             reads=[vec_tl], writes=[vec_tl])

    try:
        _chk('pro')
        for si in range(NP):
            run_sequence(si, False)
        if with_sample:
            run_sequence(NP, True)
    except _Stop:
        pass

    for key, sem in p.dsems.items():
        if p.dcnt[key] > 0:
            nc.sync.wait_ge(sem, p.dcnt[key])
    p.sbuf_left = nc.sbuf_bytes_remaining
    return nc, p


def _consts():
    ident = np.eye(128, dtype=np.float32)
    s_idx = np.arange(128)[:, None]
    t_idx = np.arange(128)[None, :]
    maskST = (s_idx <= t_idx).astype(np.float32)
    Dtab = np.zeros((128, 4, 128), np.float32)
    biasT = np.zeros((128, 4, 16), np.float32)
    kl = np.arange(128)[:, None].astype(np.float64)
    ql = np.arange(128)[None, :].astype(np.float64)
    vis = (kl // 64) <= (ql // 64)
    for h in range(4):
        s = SLOPES[h]
        d = np.where(kl <= ql, s * kl, s * (2 * ql - kl))
        Dtab[:, h, :] = np.where(vis, d, NEG)
        for dl in range(16):
            biasT[:, h, dl] = s * kl[:, 0] - s * 128.0 * dl
    c = np.concatenate([ident, maskST, Dtab.reshape(128, 512), biasT.reshape(128, 64)], axis=1).astype(np.float32)
    d4 = np.zeros((4, 4, 4), np.float32)
    for h in range(4):
        d4[h, :, h] = 1.0
    return np.ascontiguousarray(c), np.ascontiguousarray(d4.reshape(4, 16))


def _vecs(inp):
    out = np.zeros((DEPTH, 128, NV), np.float32)
    for l in range(DEPTH):
        out[l, :, V_BADA:V_BADA + 48] = inp["b_ada"][l].reshape(48, 128).T
        out[l, :, V_CONVW:V_CONVW + 32] = inp["conv_w"][l].reshape(4, 8, 128).transpose(2, 0, 1).reshape(128, 32)
        out[l, :, V_CONVB:V_CONVB + 8] = inp["conv_b"][l].reshape(8, 128).T
        out[l, :, V_DAN:V_DAN + 4] = inp["da_norm_w"][l].reshape(4, 128).T
        out[l, :, V_MN:V_MN + 4] = inp["m_norm_w"][l].reshape(4, 128).T
        out[l, :, V_BGATE:V_BGATE + 16] = inp["b_gate"][l].reshape(16, 128).T
        out[l, :, V_LN1G:V_LN1G + 8] = inp["ln1_g"][l].reshape(8, 128).T
        out[l, :, V_LN1B:V_LN1B + 8] = inp["ln1_b"][l].reshape(8, 128).T
        out[l, :, V_LN2G:V_LN2G + 8] = inp["ln2_g"][l].reshape(8, 128).T
        out[l, :, V_LN2B:V_LN2B + 8] = inp["ln2_b"][l].reshape(8, 128).T
    return out


_PROG_CACHE = {}


def run_cores(inp, ncores, NP, T, with_sample=True, trace=False):
    key = (NP, T, with_sample)
    if key not in _PROG_CACHE:
        _PROG_CACHE[key] = build_program(NP, T, with_sample)
    nc, p = _PROG_CACHE[key]
    f32 = lambda a: np.ascontiguousarray(np.asarray(a, dtype=np.float32))
    consts, d4 = _consts()
    vecs = _vecs(inp)
    bifh = f32(np.asarray(inp["b_if"]).reshape(DEPTH, 2, 4).transpose(0, 2, 1))
    lamp = f32(np.asarray(inp["lam_p"]).reshape(DEPTH, 1, 256))
    shared = {"vecs": vecs, "bif": bifh, "lamp": lamp, "consts": consts, "delta4": d4}
    for k in W_SHAPES:
        shared[k] = f32(inp[k])
    in_maps = []
    for c in range(ncores):
        m = dict(shared)
        xp = np.asarray(inp["x_prompt"])[c * NP:(c + 1) * NP]
        m["xT"] = f32(xp.transpose(0, 2, 1))
        cs = [np.asarray(inp["c_prompt"])[c * NP + i] for i in range(NP)]
        if with_sample:
            cs.append(np.asarray(inp["c_sample"])[c])
        cmat = np.stack(cs, 0)
        m["cT"] = f32(cmat.reshape(len(cs), 8, 128).transpose(2, 1, 0))
        if with_sample:
            m["xsT"] = f32(np.asarray(inp["x_sample"])[c].T)
            m["ckT"] = f32(np.asarray(inp["cache_attn_k"])[:, c].transpose(0, 2, 3, 1))
            m["cvv"] = f32(np.asarray(inp["cache_attn_v"])[:, c].reshape(DEPTH, PAST, 512))
            C = np.asarray(inp["state_mlstm_C"])[:, c]
            nn = np.asarray(inp["state_mlstm_n"])[:, c]
            sG = np.zeros((DEPTH, 128, 4, 130), np.float32)
            sG[:, :, :, 0:128] = C.transpose(0, 3, 1, 2)
            sG[:, :, :, 128] = nn.transpose(0, 2, 1)
            sG[:, :, :, 129] = nn.transpose(0, 2, 1)
            m["sG"] = f32(sG.reshape(DEPTH, 128, 520))
            m["sm"] = f32(np.asarray(inp["state_mlstm_m"])[:, c].reshape(DEPTH, 4, 1))
            cv = np.asarray(inp["state_mlstm_conv"])[:, c]
            m["sconv"] = f32(cv.reshape(DEPTH, 3, 8, 128).transpose(0, 3, 2, 1).reshape(DEPTH, 128, 24))
        in_maps.append(m)
    res = run_bass_kernel_spmd(nc, in_maps, core_ids=list(range(ncores)), trace=trace)
    return res


def assemble(results, ncores, NP, T, with_sample=True):
    B = ncores * NP
    y = np.zeros((B, T, D), np.float32)
    ak = np.zeros((DEPTH, B, T, 4, 128), np.float32)
    av = np.zeros((DEPTH, B, T, 4, 128), np.float32)
    Cp = np.zeros((DEPTH, B, 4, 128, 128), np.float32)
    npp = np.zeros((DEPTH, B, 4, 128), np.float32)
    mp = np.zeros((DEPTH, B, 4), np.float32)
    cvp = np.zeros((DEPTH, B, 3, 1024), np.float32)
    Bs = ncores
    ys = np.zeros((Bs, TS, D), np.float32)
    aks = np.zeros((DEPTH, Bs, TS, 4, 128), np.float32)
    avs = np.zeros((DEPTH, Bs, TS, 4, 128), np.float32)
    Cs = np.zeros((DEPTH, Bs, 4, 128, 128), np.float32)
    ns = np.zeros((DEPTH, Bs, 4, 128), np.float32)
    ms = np.zeros((DEPTH, Bs, 4), np.float32)
    cvs = np.zeros((DEPTH, Bs, 3, 1024), np.float32)

    def unG(g):
        g = g.reshape(g.shape[:-1] + (4, 130))
        Cc = np.moveaxis(g[..., 0:128], -3, -1)
        nn = np.moveaxis(g[..., 128], -2, -1)
        return Cc, nn

    def unconv(cv):
        cv = cv.reshape(cv.shape[:-1] + (8, 3))
        return np.moveaxis(cv, -1, -3).swapaxes(-1, -2).reshape(cv.shape[:-3] + (3, 1024))

    for c in range(ncores):
        r = results[c]
        sl = slice(c * NP, (c + 1) * NP)
        y[sl] = r["yT"].transpose(0, 2, 1)
        ak[:, sl] = r["okT"].transpose(0, 1, 3, 2).reshape(DEPTH, NP, T, 4, 128)
        av[:, sl] = r["ov"].reshape(DEPTH, NP, T, 4, 128)
        Cc, nn = unG(r["oG"])
        Cp[:, sl] = Cc
        npp[:, sl] = nn
        mp[:, sl] = r["om"][..., 0]
        cvp[:, sl] = unconv(r["oconv"])
        if with_sample:
            ys[c] = r["ysT"].T
            aks[:, c] = r["oksT"].transpose(0, 2, 1).reshape(DEPTH, TS, 4, 128)
            avs[:, c] = r["ovs"].reshape(DEPTH, TS, 4, 128)
            Cc, nn = unG(r["oGs"])
            Cs[:, c] = Cc
            ns[:, c] = nn
            ms[:, c] = r["oms"][..., 0]
            cvs[:, c] = unconv(r["oconvs"])
    return (y, ys, ak, av, aks, avs, Cp, npp, mp, cvp, Cs, ns, ms, cvs)


def kernel(**inputs):
    ncores = 8
    NP = 4
    T = 2048
    res = run_cores(inputs, ncores, NP, T, True)
    return assemble(res.results, ncores, NP, T, True)
```

```python
import math
from contextlib import ExitStack

import numpy as np
import concourse.bass as bass
import concourse.mybir as mybir
from concourse.bass_utils import run_bass_kernel_spmd

F32 = mybir.dt.float32
BF16 = mybir.dt.bfloat16
AF = mybir.ActivationFunctionType
ALU = mybir.AluOpType
AX = mybir.AxisListType

D = 1024
DEPTH = 2
NH = 4
DFF = 2816
NFF = 22
IN_COLS = 3592
LN_EPS = 1e-5
ALPHA = (2 * DEPTH) ** 0.25
SLOPES = [2.0 ** (-8.0 * (i + 1) / 4) for i in range(4)]
PAST = 1024
TS = 32
NEG = -30000.0
KSCALE = 128 ** -0.5

V_BADA = 0
V_CONVW = 48
V_CONVB = 80
V_DAN = 88
V_MN = 92
V_BGATE = 96
V_LN1G = 112
V_LN1B = 120
V_LN2G = 128
V_LN2B = 136
NV = 144

W_SHAPES = {
    "w_ada": (D, 6 * D), "w_in": (D, IN_COLS), "w_br_a": (512, D), "w_br_b": (512, D),
    "w_gate": (D, 2 * D), "w_o": (D, D), "w_gu": (D, 2 * DFF), "w_down": (DFF, D),
}


STOP = None
DEBUG = None


class _Stop(Exception):
    pass


def _chk(tag):
    if STOP == tag:
        raise _Stop()


class TT:
    __slots__ = ("w", "r", "excl", "small")

    def __init__(self, excl=False, small=False):
        self.w = None
        self.r = {}
        self.excl = excl
        self.small = small


class P:
    def __init__(self, nc, es):
        self.nc = nc
        self.es = es
        self.engs = {"pe": nc.tensor, "act": nc.scalar, "dve": nc.vector, "pool": nc.gpsimd, "sp": nc.sync}
        self.sems = {}
        self.cnt = {}
        self.seen = {e: {} for e in self.engs}
        for e in self.engs:
            self.sems[e] = es.enter_context(nc.semaphore("s_" + e))
            self.cnt[e] = 0
        self.dsems = {}
        self.dcnt = {}
        self.nwait = 0
        self.ninst = 0
        self.small_mode = False
        self.tag = ''
        self.pe_tags = []
        self.know = {}

    def _deps(self, reads, writes, eng=None):
        deps = {}
        same = 0
        sm = self.small_mode

        def add(k, v, small):
            nonlocal same
            if k == eng:
                if eng != "pe" and (small or sm) and v > same:
                    same = v
                return
            if deps.get(k, 0) < v:
                deps[k] = v
        for t in reads:
            if t.w is not None:
                add(t.w[0], t.w[1], t.small)
            if t.excl:
                for k, v in t.r.items():
                    add(k, v, t.small)
        for t in writes:
            if t.w is not None:
                add(t.w[0], t.w[1], t.small)
            for k, v in t.r.items():
                add(k, v, t.small)
        if same:
            deps[eng] = same
        return deps

    def _wait(self, eng, deps):
        seen = self.seen[eng]
        for k, v in deps.items():
            if seen.get(k, 0) >= v:
                continue
            sem = self.sems[k] if k in self.sems else self.dsems[k]
            self.engs[eng].wait_ge(sem, v)
            seen[k] = v
            self.nwait += 1
            kn = self.know.get((k, v))
            if kn:
                for k2, v2 in kn.items():
                    if seen.get(k2, 0) < v2:
                        seen[k2] = v2

    def _commit(self, ev, reads, writes):
        k, v = ev
        for t in writes:
            t.w = ev
            t.r = {}
        for t in reads:
            if t.excl:
                t.w = ev
                t.r = {}
            else:
                if t.r.get(k, 0) < v:
                    t.r[k] = v

    def op(self, eng, fn, reads=(), writes=()):
        deps = self._deps(reads, writes, eng)
        self._wait(eng, deps)
        inst = fn(self.engs[eng])
        if eng == 'pe':
            self.pe_tags.append(self.tag)
        self.cnt[eng] += 1
        inst.then_inc(self.sems[eng], 1)
        self.know[(eng, self.cnt[eng])] = dict(self.seen[eng])
        self._commit((eng, self.cnt[eng]), reads, writes)
        self.ninst += 1
        return inst

    def dma(self, q, key, out, in_, reads=(), writes=()):
        if key not in self.dsems:
            self.dsems[key] = self.es.enter_context(self.nc.semaphore("d_" + key))
            self.dcnt[key] = 0
        deps = self._deps(reads, writes)
        self._wait(q, deps)
        self.dcnt[key] += 16
        self.engs[q].dma_start(out=out, in_=in_).then_inc(self.dsems[key], 16)
        kn = dict(self.seen[q])
        kn[q] = max(kn.get(q, 0), self.cnt[q])
        self.know[(key, self.dcnt[key])] = kn
        self._commit((key, self.dcnt[key]), reads, writes)
        self.ninst += 1

    def finish(self, tiles):
        deps = self._deps(tiles, tiles)
        self._wait("sp", deps)


class Pool:
    def __init__(self, views, small=False):
        self.views = views
        self.tiles = [TT(small=small) for _ in views]
        self.i = 0

    def get(self):
        i = self.i
        self.i = (i + 1) % len(self.views)
        return self.views[i], self.tiles[i]


def build_program(NP, T, with_sample=True, TTK=512):
    nc = bass.Bass("TRN2", target_bir_lowering=False, dynamic_dma_scratch_size=4096)
    NSEQ = NP + (1 if with_sample else 0)
    NTILE = T // TTK
    NBLK = max(T // 128, 9)

    def din(name, shape, dt=F32):
        return nc.dram_tensor(name, list(shape), dt, kind="ExternalInput").ap()

    def dout(name, shape, dt=F32):
        return nc.dram_tensor(name, list(shape), dt, kind="ExternalOutput").ap()

    xT = din("xT", (NP, D, T))
    cT = din("cT", (128, 8, NSEQ))
    vecs = din("vecs", (DEPTH, 128, NV))
    bif = din("bif", (DEPTH, 4, 2))
    lamp = din("lamp", (DEPTH, 1, 256))
    consts = din("consts", (128, 128 * 2 + 4 * 128 + 64))
    delta4 = din("delta4", (4, 16))
    W = {k: din(k, (DEPTH,) + v) for k, v in W_SHAPES.items()}
    WB = {k: nc.dram_tensor(k + "_bf", [DEPTH] + list(v), BF16, kind="Internal").ap() for k, v in W_SHAPES.items()}
    yT = dout("yT", (NP, D, T))
    okT = dout("okT", (DEPTH, NP, 512, T))
    ov = dout("ov", (DEPTH, NP, T, 512))
    oG = dout("oG", (DEPTH, NP, 128, 4 * 130))
    om = dout("om", (DEPTH, NP, 4, 1))
    oconv = dout("oconv", (DEPTH, NP, 128, 24))
    if with_sample:
        xsT = din("xsT", (D, TS))
        ckT = din("ckT", (DEPTH, 4, 128, PAST))
        cvv = din("cvv", (DEPTH, PAST, 512))
        sG = din("sG", (DEPTH, 128, 4 * 130))
        sm = din("sm", (DEPTH, 4, 1))
        sconv = din("sconv", (DEPTH, 128, 24))
        ysT = dout("ysT", (D, TS))
        oksT = dout("oksT", (DEPTH, 512, TS))
        ovs = dout("ovs", (DEPTH, TS, 512))
        oGs = dout("oGs", (DEPTH, 128, 4 * 130))
        oms = dout("oms", (DEPTH, 4, 1))
        oconvs = dout("oconvs", (DEPTH, 128, 24))

    es = ExitStack()
    p = P(nc, es)
    dbg_count = [0]

    def dbg(name, ap, tiles, once=True):
        if DEBUG is None or name not in DEBUG:
            return
        if once and name in dbg_seen:
            return
        dbg_seen.add(name)
        shape = list(ap.shape)
        d = nc.dram_tensor("dbg_" + name, shape, ap.dtype, kind="ExternalOutput").ap()
        p.dma("act", "dbg_" + name, d, ap, reads=tiles)
    dbg_seen = set()

    def sb(name, shape, dt):
        return es.enter_context(nc.sbuf_tensor(name, list(shape), dt))

    x_t = sb("x_t", (128, 8, TTK), F32)
    x_tl = [TT() for _ in range(8)]
    h_t = sb("h_t", (128, 8, TTK), BF16)
    h_tl = [TT() for _ in range(8)]
    U = sb("U", (128, 28, TTK), BF16)
    U_tl = [TT() for _ in range(28)]
    mixin = sb("mixin", (128, 8, TTK), BF16)
    mix_tl = [TT() for _ in range(8)]
    Kc = [sb(f"Kc{l}", (128, 4, max(T, PAST + TS)), BF16) for l in range(DEPTH)]
    Kc_tl = [[TT() for _ in range(4)] for l in range(DEPTH)]
    Vc = [sb(f"Vc{l}", (128, NBLK, 4 * 130), BF16) for l in range(DEPTH)]
    Vc_tl = [[TT() for _ in range(NBLK)] for l in range(DEPTH)]
    G = [sb(f"G{l}", (128, 4 * 130), F32) for l in range(DEPTH)]
    G_tl = [TT() for l in range(DEPTH)]
    carry = [sb(f"carry{l}", (128, 24), F32) for l in range(DEPTH)]
    carry_tl = [TT(small=True) for l in range(DEPTH)]
    mv_t = sb("mv_t", (128, 4, 4 * 130), BF16)
    mv_tl = [TT() for _ in range(4)]
    wslots = Pool([sb(f"wslot{i}", (128, 8 * 512), BF16) for i in range(3)])
    NFP = 6
    FP = Pool([sb(f"fp{i}", (128, 520), F32) for i in range(NFP)])
    ANP = Pool([sb(f"anp{i}", (128, 512), BF16) for i in range(2)])
    MNP = Pool([sb(f"mnp{i}", (128, 512), BF16) for i in range(2)])
    AO = Pool([sb(f"ao{i}", (128, 512), F32) for i in range(2)])
    NHP = 8
    HP = Pool([sb(f"hp{i}", (128, 520), BF16) for i in range(NHP)])
    pre = Pool([sb(f"pre{i}", (128, 3 + TTK), F32) for i in range(2)])
    g_u = sb("g_u", (4, TTK), F32)
    g_lf = sb("g_lf", (4, TTK), F32)
    g_nb = sb("g_nb", (4, TTK), F32)
    g_ek = sb("g_ek", (4, TTK), F32)
    g_cl = sb("g_cl", (4, TTK), F32)
    g_tl = TT()
    g2_tl = TT()
    gsm = sb("gsm", (128, 64), F32)
    gsm_tl = TT(small=True)
    ekcl = sb("ekcl", (128, 4, 8), F32)
    ekcl_tl = [TT(small=True) for _ in range(4)]
    wcb = sb("wcb", (128, 16), F32)
    wcb_tl = TT(small=True)
    gst = [sb(f"gst{l}", (128, 4), F32) for l in range(DEPTH)]
    gst_tl = [TT(small=True) for l in range(DEPTH)]
    smalls = Pool([sb(f"sml{i}", (128, 32), F32) for i in range(8)], small=True)
    cst = sb("cst", (128, 128 * 2 + 4 * 128 + 64), F32)
    cst_tl = TT()
    identb = sb("identb", (128, 128), BF16)
    onesb = sb("onesb", (128, 128), BF16)
    maskb = sb("maskb", (128, 128), BF16)
    ones4 = sb("ones4", (4, 128), F32)
    d4 = sb("d4", (4, 16), F32)
    cb_tl = TT(small=True)
    vec_sb = sb("vec_sb", (128, DEPTH, NV), F32)
    vec_tl = TT(small=True)
    c_sb = sb("c_sb", (128, 8, NSEQ), F32)
    c_bf = sb("c_bf", (128, 8, NSEQ), BF16)
    c_tl = TT(small=True)
    mod = sb("mod", (128, DEPTH, 48, NSEQ), F32)
    mod_tl = TT(small=True)
    mod2 = sb("mod2", (128, DEPTH, 48, NSEQ), F32)
    lam_sb = sb("lam_sb", (128, DEPTH, 256), F32)
    lamv = sb("lamv", (128, DEPTH, 4), F32)
    lam_tl = TT(small=True)
    bif_sb = sb("bif_sb", (4, DEPTH, 4), F32)
    bif_tl = TT(small=True)

    ident_f = cst[:, 0:128]
    mask_f = cst[:, 128:256]
    Dtab = cst[:, 256:256 + 512]
    biasT = cst[:, 768:768 + 64]

    PS = [es.enter_context(nc.psum_tensor(f"ps{i}", [128, 512], F32)) for i in range(8)]
    PA = Pool(PS[0:4])
    PB = Pool(PS[0:6])
    PB.tiles[0:4] = PA.tiles[0:4]
    PC = Pool(PS[6:8])
    PBC = Pool(PS[4:8])
    PBC.tiles = [PB.tiles[4], PB.tiles[5], PC.tiles[0], PC.tiles[1]]
    for pl in (PA, PB, PC):
        for t in pl.tiles:
            t.excl = True

    WB_tl = {}
    order = ["w_in", "w_gate", "w_br_a", "w_br_b", "w_o", "w_gu", "w_down"]
    for l in range(DEPTH):
        for k in order:
            R = W_SHAPES[k][0]
            t = TT()
            WB_tl[(k, l)] = t
            nsplit = 4 if R * W_SHAPES[k][1] > 2 ** 21 else 1
            rs = R // nsplit
            key = f"wc_{k}{l}"
            if key not in p.dsems:
                p.dsems[key] = es.enter_context(nc.semaphore("d_" + key))
                p.dcnt[key] = 0
            for i in range(nsplit):
                nc.gpsimd.dma_start(out=WB[k][l, i * rs:(i + 1) * rs, :], in_=W[k][l, i * rs:(i + 1) * rs, :]).then_inc(p.dsems[key], 16)
                p.dcnt[key] += 16
            t.w = (key, p.dcnt[key])

    p.dma("act", "c_cst", cst[:], consts[:, :], writes=[cst_tl])
    p.dma("act", "c_vec", vec_sb[:], vecs.rearrange("l p n -> p l n"), writes=[vec_tl])
    p.dma("act", "c_c", c_sb[:], cT[:, :, :], writes=[c_tl])
    p.dma("act", "c_lam", lam_sb[:], lamp.rearrange("l o n -> o l n").broadcast_to([128, DEPTH, 256]), writes=[lam_tl])
    p.dma("act", "c_bif", bif_sb[:, :, 0:2], bif.rearrange("l h t -> h l t"), writes=[bif_tl])
    p.dma("act", "c_d4", d4[:], delta4[:, :], writes=[cb_tl])
    p.op("dve", lambda e: e.tensor_copy(out=identb[:], in_=ident_f), reads=[cst_tl], writes=[cb_tl])
    p.op("dve", lambda e: e.tensor_copy(out=maskb[:], in_=mask_f), reads=[cst_tl], writes=[cb_tl])
    p.op("dve", lambda e: e.memset(onesb[:], 1.0 / 1024.0), writes=[cb_tl])
    p.op("dve", lambda e: e.memset(ones4[:], 1.0), writes=[cb_tl])
    p.op("dve", lambda e: e.tensor_scalar(out=bif_sb[:, :, 2:3], in0=bif_sb[:, :, 1:2], scalar1=-1.0, scalar2=None, op0=ALU.mult),
         reads=[bif_tl], writes=[bif_tl])
    for l in range(DEPTH):
        lam_init = 0.8 - 0.6 * math.exp(-0.3 * l)
        fv, ft = FP.get()
        p.op("dve", lambda e: e.tensor_tensor(out=fv[:, 0:64], in0=lam_sb[:, l, 0:64], in1=lam_sb[:, l, 64:128], op=ALU.mult), reads=[lam_tl], writes=[ft])
        p.op("dve", lambda e: e.tensor_tensor(out=fv[:, 64:128], in0=lam_sb[:, l, 128:192], in1=lam_sb[:, l, 192:256], op=ALU.mult), reads=[lam_tl], writes=[ft])
        p.op("dve", lambda e: e.tensor_reduce(out=lamv[:, l, 1:3], in_=fv[:, 0:128].rearrange("p (a b) -> p a b", a=2), axis=AX.X, op=ALU.add), reads=[ft], writes=[lam_tl])
        p.op("act", lambda e: e.activation(out=lamv[:, l, 1:3], in_=lamv[:, l, 1:3], func=AF.Exp), reads=[lam_tl], writes=[lam_tl])
        p.op("dve", lambda e: e.scalar_tensor_tensor(out=lamv[:, l, 0:1], in0=lamv[:, l, 2:3], scalar=-lam_init, in1=lamv[:, l, 1:2], op0=ALU.add, op1=ALU.subtract),
             reads=[lam_tl], writes=[lam_tl])

    def wload(name, l, k0, nk, c0, ncol):
        view, tl = wslots.get()
        v3 = view[:, 0:nk * ncol].rearrange("p (k c) -> p k c", k=nk)
        src = WB[name][l].rearrange("(k p) c -> p k c", p=128)[:, k0:k0 + nk, c0:c0 + ncol]
        p.dma("sp", f"ws{wslots.views.index(view)}", v3, src, reads=[WB_tl[(name, l)]], writes=[tl])
        return v3, tl

    fv, ft = FP.get()
    p.op("act", lambda e: e.activation(out=c_sb[:].rearrange("p k s -> p (k s)"), in_=c_sb[:].rearrange("p k s -> p (k s)"), func=AF.Silu),
         reads=[c_tl], writes=[c_tl])
    mod_done = set()

    def compute_mod(l):
        if l in mod_done:
            return
        p.tag = 'mod'
        mod_done.add(l)
        sm_save = p.small_mode
        p.small_mode = False
        for g in range(24):
            view, wt = wslots.get()
            wv = view[:].bitcast(F32)[:, 0:8 * 256].rearrange("p (k c) -> p k c", k=8)
            p.dma("sp", f"ws{wslots.views.index(view)}", wv, W["w_ada"][l].rearrange("(k p) c -> p k c", p=128)[:, :, g * 256:(g + 1) * 256], writes=[wt])
            pv, pt = PA.get()
            for cc in range(2):
                for k in range(8):
                    p.op("pe", lambda e: e.matmul(pv[:, cc * NSEQ:(cc + 1) * NSEQ], lhsT=wv[:, k, cc * 128:(cc + 1) * 128], rhs=c_sb[:, k, :],
                                                  start=(k == 0 and cc == 0), stop=(k == 7), skip_group_check=True),
                         reads=[wt, c_tl], writes=[pt])
            p.op("dve", lambda e: e.tensor_tensor(out=mod[:, l, g * 2:(g + 1) * 2, :], in0=pv[:, 0:2 * NSEQ].rearrange("p (c s) -> p c s", c=2),
                                                  in1=vec_sb[:, l, V_BADA + g * 2:V_BADA + (g + 1) * 2].unsqueeze(2).broadcast_to([128, 2, NSEQ]), op=ALU.add),
                 reads=[pt, vec_tl], writes=[modl_tl[l]])
        for (a, mul) in ((8, 1.0), (16, 1.0 / ALPHA), (32, 1.0), (40, 1.0 / ALPHA)):
            p.op("dve", lambda e: e.tensor_scalar(out=mod2[:, l, a:a + 8, :], in0=mod[:, l, a:a + 8, :], scalar1=1.0, scalar2=mul, op0=ALU.add, op1=ALU.mult),
                 reads=[modl_tl[l]], writes=[modl_tl[l]])
        p.small_mode = sm_save

    modl_tl = [TT(small=True) for _ in range(DEPTH)]
    compute_mod(0)

    def ln_begin():
        mps, mt = PA.get()
        qps, qt = PA.get()
        return (mps, mt, qps, qt)

    def ln_chunk(ctx, j, src_ap, src_tl, n):
        mps, mt, qps, qt = ctx
        sqv, sqt = HP.get()
        xbv, xbt = HP.get()
        p.op("act", lambda e: e.activation(out=sqv[:, 0:n], in_=src_ap, func=AF.Square), reads=[src_tl], writes=[sqt])
        p.op("dve", lambda e: e.tensor_copy(out=xbv[:, 0:n], in_=src_ap), reads=[src_tl], writes=[xbt])
        p.op("pe", lambda e: e.matmul(mps[:, 0:n], lhsT=onesb[:], rhs=xbv[:, 0:n], start=(j == 0), stop=(j == 7)), reads=[xbt, cb_tl], writes=[mt])
        p.op("pe", lambda e: e.matmul(qps[:, 0:n], lhsT=onesb[:], rhs=sqv[:, 0:n], start=(j == 0), stop=(j == 7)), reads=[sqt, cb_tl], writes=[qt])

    def layer_norm_stats(src_chunks, src_tls, n, eps):
        ctx = ln_begin()
        for j in range(8):
            ln_chunk(ctx, j, src_chunks[j], src_tls[j], n)
        return ln_finish(ctx, n, eps)

    def ln_finish(ctx, n, eps):
        mps, mt, qps, qt = ctx
        b1, b1t = FP.get()
        b2, b2t = FP.get()
        p.op("act", lambda e: e.activation(out=b2[:, 0:n], in_=mps[:, 0:n], func=AF.Square), reads=[mt], writes=[b2t])
        p.op("dve", lambda e: e.tensor_tensor(out=b1[:, 0:n], in0=qps[:, 0:n], in1=b2[:, 0:n], op=ALU.subtract), reads=[qt, b2t], writes=[b1t])
        p.op("dve", lambda e: e.tensor_scalar(out=b1[:, 0:n], in0=b1[:, 0:n], scalar1=0.0, scalar2=eps, op0=ALU.max, op1=ALU.add), reads=[b1t], writes=[b1t])
        p.op("act", lambda e: e.activation(out=b1[:, 0:n], in_=b1[:, 0:n], func=AF.Ln), reads=[b1t], writes=[b1t])
        p.op("act", lambda e: e.activation(out=b1[:, 0:n], in_=b1[:, 0:n], func=AF.Exp, scale=-0.5), reads=[b1t], writes=[b1t])
        p.op("dve", lambda e: e.scalar_tensor_tensor(out=mps[:, 0:n], in0=mps[:, 0:n], scalar=-1.0, in1=b1[:, 0:n], op0=ALU.mult, op1=ALU.mult),
             reads=[mt, b1t], writes=[mt])
        p.op("dve", lambda e: e.tensor_copy(out=qps[:, 0:n], in_=b1[:, 0:n]), reads=[b1t], writes=[qt])
        return (qps, qt), (mps, mt)

    def ln_apply(src, src_tl, dst, dst_tl, n, B1, B2, scale_ap, bias_ap, extra_reads):
        (b1, b1t), (b2, b2t) = B1, B2
        t1, t1t = FP.get()
        p.op("dve", lambda e: e.tensor_tensor(out=t1[:, 0:n], in0=b1[:, 0:n], in1=src, op=ALU.mult), reads=[src_tl, b1t], writes=[t1t])
        p.op("dve", lambda e: e.tensor_tensor(out=t1[:, 0:n], in0=b2[:, 0:n], in1=t1[:, 0:n], op=ALU.add), reads=[t1t, b2t], writes=[t1t])
        p.op("act", lambda e: e.activation(out=dst, in_=t1[:, 0:n], func=AF.Identity, scale=scale_ap, bias=bias_ap), reads=[t1t] + extra_reads, writes=[dst_tl])

    def run_sequence(si, is_sample):
        p.small_mode = is_sample
        Tq = TS if is_sample else T
        TTs = TS if is_sample else TTK
        SUB = TS if is_sample else 128
        NS = TTs // SUB
        ntile = 1 if is_sample else NTILE
        x_src = xsT if is_sample else xT[si]
        y_dst = ysT if is_sample else yT[si]

        for l in range(DEPTH):
            if is_sample:
                p.dma("act", f"g{l}", G[l][:], sG[l], writes=[G_tl[l]])
                p.dma("act", f"cr{l}", carry[l][:], sconv[l], writes=[carry_tl[l]])
                p.op("dve", lambda e: e.memset(gst[l][0:4, 0:1], 0.0), writes=[gst_tl[l]])
                p.dma("act", f"gs{l}", gst[l][0:4, 1:2], sm[l], writes=[gst_tl[l]])
                for hh in range(4):
                    for half in range(2):
                        fv_, ft_ = FP.get()
                        p.dma("act", f"fp{FP.views.index(fv_)}", fv_[:, 0:512], ckT[l, hh, :, half * 512:(half + 1) * 512], writes=[ft_])
                        p.op("pool", lambda e: e.tensor_copy(out=Kc[l][:, hh, half * 512:(half + 1) * 512], in_=fv_[:, 0:512]), reads=[ft_], writes=[Kc_tl[l][hh]])
                for b in range(8):
                    fv_, ft_ = FP.get()
                    p.dma("act", f"fp{FP.views.index(fv_)}", fv_[:, 0:512], cvv[l, b * 128:(b + 1) * 128, :], writes=[ft_])
                    p.op("pool", lambda e: e.memset(Vc[l][:, b, :], 1.0), writes=[Vc_tl[l][b]])
                    p.op("pool", lambda e: e.tensor_copy(out=Vc[l][:, b, :].rearrange("p (h d) -> p h d", h=4)[:, :, 0:128],
                                                         in_=fv_[:, 0:512].rearrange("p (h d) -> p h d", h=4)), reads=[ft_], writes=[Vc_tl[l][b]])
                p.op("pool", lambda e: e.memset(Vc[l][:, 8, :], 1.0), writes=[Vc_tl[l][8]])
            else:
                p.op("pool", lambda e: e.memset(G[l][:], 0.0), writes=[G_tl[l]])
                p.op("pool", lambda e: e.memset(carry[l][:], 0.0), writes=[carry_tl[l]])
                p.op("dve", lambda e: e.memset(gst[l][0:4, 0:2], 0.0), writes=[gst_tl[l]])
                for b in range(T // 128):
                    p.op("pool", lambda e: e.memset(Vc[l][:, b, :], 1.0), writes=[Vc_tl[l][b]])

        for it in range(ntile):
            tok0 = it * TTs
            p.dma("act", "xin", x_t[:, :, 0:TTs], x_src.rearrange("(k p) t -> p k t", p=128)[:, :, tok0:tok0 + TTs], writes=x_tl)
            for l in range(DEPTH):
                run_tile_layer(si, is_sample, l, it, tok0, TTs, SUB, NS)
            p.dma("act", "yout", y_dst.rearrange("(k p) t -> p k t", p=128)[:, :, tok0:tok0 + TTs], x_t[:, :, 0:TTs], reads=x_tl)

        for l in range(DEPTH):
            p.op("dve", lambda e: e.tensor_tensor(out=gst[l][0:4, 2:3], in0=gst[l][0:4, 1:2], in1=gst[l][0:4, 0:1], op=ALU.subtract), reads=[gst_tl[l]], writes=[gst_tl[l]])
            if is_sample:
                p.dma("act", f"g{l}", oGs[l], G[l][:], reads=[G_tl[l]])
                p.dma("act", f"cr{l}", oconvs[l], carry[l][:], reads=[carry_tl[l]])
                p.dma("act", f"gs{l}", oms[l], gst[l][0:4, 2:3], reads=[gst_tl[l]])
            else:
                p.dma("act", f"g{l}", oG[l, si], G[l][:], reads=[G_tl[l]])
                p.dma("act", f"cr{l}", oconv[l, si], carry[l][:], reads=[carry_tl[l]])
                p.dma("act", f"gs{l}", om[l, si], gst[l][0:4, 2:3], reads=[gst_tl[l]])

    def run_tile_layer(si, is_sample, l, it, tok0, n, SUB, NS):
        compute_mod(l)
        mod_tl = modl_tl[l]
        lam_init = 0.8 - 0.6 * math.exp(-0.3 * l)
        mcol = lambda piece, j: mod[:, l, piece * 8 + j, si:si + 1]
        m2col = lambda piece, j: mod2[:, l, piece * 8 + j, si:si + 1]
        vcol = lambda off: vec_sb[:, l, off:off + 1]
        kbase = PAST if is_sample else 0
        xs = [x_t[:, j, 0:n] for j in range(8)]

        p.tag = 'a-ln'
        B1, B2 = layer_norm_stats(xs, x_tl, n, LN_EPS)
        for j in range(8):
            ln_apply(xs[j], x_tl[j], h_t[:, j, 0:n], h_tl[j], n, B1, B2, m2col(1, j), mcol(0, j), [mod_tl])

        def proj_fm(wv, wt, ci, act_chunks, act_tls, nk, k0=0, first=True, last=True, pv=None, pt=None):
            if pv is None:
                pv, pt = PA.get()
            for k in range(nk):
                p.op("pe", lambda e: e.matmul(pv[:, 0:n], lhsT=wv[:, k, ci * 128:(ci + 1) * 128], rhs=act_chunks[k0 + k],
                                              start=(first and k == 0), stop=(last and k == nk - 1)),
                     reads=[wt, act_tls[k0 + k]], writes=[pt])
            return pv, pt

        dbg('B1', B1[0][:, 0:n], [B1[1]])
        dbg('B2', B2[0][:, 0:n], [B2[1]])
        dbg('h', h_t[:, :, 0:n], h_tl)
        _chk('a')
        hs = [h_t[:, k, 0:n] for k in range(8)]

        p.tag = 'b-aq'
        wv, wt = wload("w_in", l, 0, 8, 0, 512)
        aqb = [PA.get() for _ in range(4)]
        for k in range(8):
            for hh in range(4):
                p.op("pe", lambda e: e.matmul(aqb[hh][0][:, 0:n], lhsT=wv[:, k, hh * 128:(hh + 1) * 128], rhs=hs[k], start=(k == 0), stop=(k == 7)),
                     reads=[wt, h_tl[k]], writes=[aqb[hh][1]])
        for hh in range(4):
            pv, pt = aqb[hh]
            p.op("pool", lambda e: e.memset(U[64:128, hh, 0:n], 0.0), writes=[U_tl[hh]])
            p.op("pool", lambda e: e.memset(U[0:64, 24 + hh, 0:n], 0.0), writes=[U_tl[24 + hh]])
            p.op("act", lambda e: e.activation(out=U[0:64, hh, 0:n], in_=pv[0:64, 0:n], func=AF.Copy), reads=[pt], writes=[U_tl[hh]])
            p.op("act", lambda e: e.activation(out=U[64:128, 24 + hh, 0:n], in_=pv[64:128, 0:n], func=AF.Copy), reads=[pt], writes=[U_tl[24 + hh]])
        p.tag = 'b-ak'
        wv, wt = wload("w_in", l, 0, 8, 512, 512)
        for hh in range(4):
            pv, pt = proj_fm(wv, wt, hh, hs, h_tl, 8)
            fv, ft = FP.get()
            p.op("act", lambda e: e.activation(out=fv[:, 0:n], in_=pv[:, 0:n], func=AF.Copy), reads=[pt], writes=[ft])
            p.op("dve", lambda e: e.tensor_copy(out=Kc[l][:, hh, kbase + tok0:kbase + tok0 + n], in_=pv[:, 0:n]), reads=[pt], writes=[Kc_tl[l][hh]])
            dst = oksT[l, hh * 128:(hh + 1) * 128, :] if is_sample else okT[l, si, hh * 128:(hh + 1) * 128, tok0:tok0 + n]
            p.dma("act", f"fp{FP.views.index(fv)}", dst, fv[:, 0:n], reads=[ft])
        p.tag = 'b-av'
        wv, wt = wload("w_in", l, 0, 8, 1024, 512)
        for r in range(NS):
            pv, pt = PA.get()
            for k in range(8):
                p.op("pe", lambda e: e.matmul(pv[0:SUB, :], lhsT=h_t[:, k, r * SUB:(r + 1) * SUB], rhs=wv[:, k, :], start=(k == 0), stop=(k == 7)),
                     reads=[wt, h_tl[k]], writes=[pt])
            fv, ft = FP.get()
            blk = (kbase + tok0) // 128 + r
            p.op("act", lambda e: e.activation(out=fv[0:SUB, 0:512], in_=pv[0:SUB, :], func=AF.Copy), reads=[pt], writes=[ft])
            p.op("dve", lambda e: e.tensor_copy(out=Vc[l][0:SUB, blk, :].rearrange("p (h d) -> p h d", h=4)[:, :, 0:128],
                                                in_=pv[0:SUB, :].rearrange("p (h d) -> p h d", h=4)), reads=[pt], writes=[Vc_tl[l][blk]])
            dst = ovs[l] if is_sample else ov[l, si, tok0 + r * SUB:tok0 + (r + 1) * SUB, :]
            p.dma("act", f"fp{FP.views.index(fv)}", dst, fv[0:SUB, 0:512], reads=[ft])
        p.tag = 'b-mqk'
        for half in range(2):
            wv, wt = wload("w_in", l, 0, 8, 1536 + half * 512, 512)
            for cc in range(4):
                c = half * 4 + cc
                pv, pt = proj_fm(wv, wt, cc, hs, h_tl, 8)
                prv, prt = pre.get()
                p.op("pool", lambda e: e.tensor_copy(out=prv[:, 0:3], in_=carry[l][:, c * 3:c * 3 + 3]), reads=[carry_tl[l]], writes=[prt])
                p.op("act", lambda e: e.activation(out=prv[:, 3:3 + n], in_=pv[:, 0:n], func=AF.Copy), reads=[pt], writes=[prt])
                p.op("pool", lambda e: e.tensor_copy(out=carry[l][:, c * 3:c * 3 + 3], in_=prv[:, n:n + 3]), reads=[prt], writes=[carry_tl[l]])
                yv, yt = FP.get()
                p.op("dve", lambda e: e.tensor_scalar(out=yv[:, 0:n], in0=prv[:, 0:n], scalar1=vcol(V_CONVW + 0 * 8 + c), scalar2=vcol(V_CONVB + c), op0=ALU.mult, op1=ALU.add),
                     reads=[prt, vec_tl], writes=[yt])
                for tap in range(1, 4):
                    p.op("dve", lambda e: e.scalar_tensor_tensor(out=yv[:, 0:n], in0=prv[:, tap:tap + n], scalar=vcol(V_CONVW + tap * 8 + c), in1=yv[:, 0:n], op0=ALU.mult, op1=ALU.add),
                         reads=[prt, vec_tl, yt], writes=[yt])
                p.op("act", lambda e: e.activation(out=U[:, 4 + c, 0:n], in_=yv[:, 0:n], func=AF.Silu), reads=[yt], writes=[U_tl[4 + c]])
        p.tag = 'b-mv'
        wv, wt = wload("w_in", l, 0, 8, 2560, 512)
        for r in range(NS):
            pv, pt = PA.get()
            for k in range(8):
                p.op("pe", lambda e: e.matmul(pv[0:SUB, :], lhsT=h_t[:, k, r * SUB:(r + 1) * SUB], rhs=wv[:, k, :], start=(k == 0), stop=(k == 7)),
                     reads=[wt, h_tl[k]], writes=[pt])
            p.op("pool", lambda e: e.memset(mv_t[:, r, :], 1.0), writes=[mv_tl[r]])
            p.op("dve", lambda e: e.tensor_copy(out=mv_t[0:SUB, r, :].rearrange("p (h d) -> p h d", h=4)[:, :, 0:128],
                                                in_=pv[0:SUB, :].rearrange("p (h d) -> p h d", h=4)), reads=[pt], writes=[mv_tl[r]])
        p.tag = 'b-if'
        wv, wt = wload("w_in", l, 0, 8, 3072, 8)
        pvi, pti = PA.get()
        pvf, ptf = PA.get()
        for k in range(8):
            p.op("pe", lambda e: e.matmul(pvi[0:4, 0:n], lhsT=wv[:, k, 0:4], rhs=hs[k], start=(k == 0), stop=(k == 7)), reads=[wt, h_tl[k]], writes=[pti])
        for k in range(8):
            p.op("pe", lambda e: e.matmul(pvf[0:4, 0:n], lhsT=wv[:, k, 4:8], rhs=hs[k], start=(k == 0), stop=(k == 7)), reads=[wt, h_tl[k]], writes=[ptf])
        p.op("act", lambda e: e.activation(out=g_u[0:4, 0:n], in_=pvi[0:4, 0:n], func=AF.Identity, bias=bif_sb[:, l, 0:1]), reads=[pti, bif_tl], writes=[g_tl])
        p.op("act", lambda e: e.activation(out=g_lf[0:4, 0:n], in_=pvf[0:4, 0:n], func=AF.Exp, scale=-1.0, bias=bif_sb[:, l, 2:3]), reads=[ptf, bif_tl], writes=[g_tl])
        p.op("act", lambda e: e.activation(out=g_lf[0:4, 0:n], in_=g_lf[0:4, 0:n], func=AF.Ln, bias=1.0), reads=[g_tl], writes=[g_tl])
        p.tag = 'b-mo'
        wv, wt = wload("w_in", l, 0, 8, 3080, 512)
        for hh in range(4):
            pv, pt = proj_fm(wv, wt, hh, hs, h_tl, 8)
            p.op("act", lambda e: e.activation(out=U[:, 12 + hh, 0:n], in_=pv[:, 0:n], func=AF.Sigmoid), reads=[pt], writes=[U_tl[12 + hh]])

        _chk('b')
        p.tag = 'c'
        gs_ = gsm[0:4, :]
        p.op("dve", lambda e: e.tensor_tensor_scan(out=g_nb[0:4, 0:n], data0=g_lf[0:4, 0:n], data1=g_lf[0:4, 0:n], initial=gst[l][0:4, 0:1], op0=ALU.add, op1=ALU.max),
             reads=[g_tl, gst_tl[l]], writes=[g_tl])
        p.op("dve", lambda e: e.tensor_tensor(out=g_u[0:4, 0:n], in0=g_u[0:4, 0:n], in1=g_nb[0:4, 0:n], op=ALU.add), reads=[g_tl], writes=[g_tl])
        p.op("dve", lambda e: e.tensor_reduce(out=gs_[:, 0:NS], in_=g_u[0:4, 0:n].rearrange("p (r s) -> p r s", r=NS), axis=AX.X, op=ALU.max), reads=[g_tl], writes=[gsm_tl])
        p.op("dve", lambda e: e.tensor_tensor_scan(out=gs_[:, 8:8 + NS], data0=gs_[:, 0:NS], data1=gs_[:, 0:NS], initial=gst[l][0:4, 1:2], op0=ALU.max, op1=ALU.max),
             reads=[gsm_tl, gst_tl[l]], writes=[gsm_tl])
        p.op("dve", lambda e: e.tensor_copy(out=gs_[:, 16:17], in_=gst[l][0:4, 1:2]), reads=[gst_tl[l]], writes=[gsm_tl])
        if NS > 1:
            p.op("dve", lambda e: e.tensor_copy(out=gs_[:, 17:16 + NS], in_=gs_[:, 8:8 + NS - 1]), reads=[gsm_tl], writes=[gsm_tl])
        p.op("dve", lambda e: e.tensor_tensor(out=gs_[:, 24:24 + NS], in0=gs_[:, 16:16 + NS], in1=gs_[:, 8:8 + NS], op=ALU.subtract), reads=[gsm_tl], writes=[gsm_tl])
        p.op("act", lambda e: e.activation(out=gs_[:, 24:24 + NS], in_=gs_[:, 24:24 + NS], func=AF.Exp), reads=[gsm_tl], writes=[gsm_tl])
        p.op("dve", lambda e: e.tensor_scalar(out=gs_[:, 32:32 + NS], in0=gs_[:, 8:8 + NS], scalar1=-1.0, scalar2=None, op0=ALU.mult), reads=[gsm_tl], writes=[gsm_tl])
        p.op("dve", lambda e: e.tensor_scalar(out=gs_[:, 40:40 + NS], in0=gs_[:, 8:8 + NS], scalar1=-1.0, scalar2=math.log(KSCALE), op0=ALU.mult, op1=ALU.add), reads=[gsm_tl], writes=[gsm_tl])
        for r in range(NS):
            p.op("act", lambda e: e.activation(out=g_ek[0:4, r * SUB:(r + 1) * SUB], in_=g_u[0:4, r * SUB:(r + 1) * SUB], func=AF.Exp, bias=gs_[:, 40 + r:41 + r]),
                 reads=[g_tl, gsm_tl], writes=[g2_tl])
            p.op("act", lambda e: e.activation(out=g_cl[0:4, r * SUB:(r + 1) * SUB], in_=g_nb[0:4, r * SUB:(r + 1) * SUB], func=AF.Exp, bias=gs_[:, 32 + r:33 + r]),
                 reads=[g_tl, gsm_tl], writes=[g2_tl])
        p.op("dve", lambda e: e.tensor_copy(out=gst[l][0:4, 0:1], in_=g_nb[0:4, n - 1:n]), reads=[g_tl], writes=[gst_tl[l]])
        p.op("dve", lambda e: e.tensor_copy(out=gst[l][0:4, 1:2], in_=gs_[:, 8 + NS - 1:8 + NS]), reads=[gsm_tl], writes=[gst_tl[l]])
        for r in range(NS):
            pv, pt = PA.get()
            p.op("pe", lambda e: e.transpose(out=pv[0:SUB, 0:4], in_=g_ek[0:4, r * SUB:(r + 1) * SUB], identity=ident_f[0:4, 0:4]), reads=[g2_tl, cst_tl], writes=[pt])
            p.op("pe", lambda e: e.transpose(out=pv[0:SUB, 4:8], in_=g_cl[0:4, r * SUB:(r + 1) * SUB], identity=ident_f[0:4, 0:4]), reads=[g2_tl, cst_tl], writes=[pt])
            p.op("dve", lambda e: e.tensor_copy(out=ekcl[0:SUB, r, :], in_=pv[0:SUB, 0:8]), reads=[pt], writes=[ekcl_tl[r]])
        p.op("dve", lambda e: e.tensor_tensor(out=gs_[:, 48:48 + NS * 4].rearrange("p (r h) -> p r h", h=4), in0=gs_[:, 24:24 + NS].unsqueeze(2).broadcast_to([4, NS, 4]),
                                              in1=d4[:, 0:NS * 4].rearrange("p (r h) -> p r h", h=4), op=ALU.mult), reads=[gsm_tl, cb_tl], writes=[gsm_tl])
        pv, pt = PA.get()
        p.op("pe", lambda e: e.matmul(pv[:, 0:NS * 4], lhsT=ones4[:, :], rhs=gs_[:, 48:48 + NS * 4], start=True, stop=True), reads=[gsm_tl, cb_tl], writes=[pt])
        p.op("dve", lambda e: e.tensor_copy(out=wcb[:, 0:NS * 4], in_=pv[:, 0:NS * 4]), reads=[pt], writes=[wcb_tl])

        dbg('gu', g_u[0:4, 0:n], [g_tl])
        dbg('gnb', g_nb[0:4, 0:n], [g_tl])
        dbg('gek', g_ek[0:4, 0:n], [g2_tl])
        dbg('gcl', g_cl[0:4, 0:n], [g2_tl])
        dbg('gsm', gsm[0:4, :], [gsm_tl])
        dbg('wcb', wcb[:, :], [wcb_tl])
        dbg('ekcl', ekcl[:, :, :], ekcl_tl)
        dbg('mq', U[:, 4:8, 0:n], U_tl[4:8])
        dbg('mk', U[:, 8:12, 0:n], U_tl[8:12])
        dbg('mv', mv_t[:, :, :], mv_tl)
        _chk('c')
        deferred = []
        for r in range(NS):
            q0 = r * SUB
            qs = slice(q0, q0 + SUB)
            p.tag = 'd-attn'
            if is_sample:
                blocks = [(j, 8 - j, 128, j * 128, False) for j in range(8)] + [(8, 0, TS, PAST, True)]
            else:
                qi = (tok0 + q0) // 128
                blocks = [(j, qi - j, 128, j * 128, False) for j in range(qi)] + [(qi, 0, 128, qi * 128, True)]
            aov, aot = AO.get()
            def live(hh, bi):
                dl = blocks[bi][1]
                return dl == 0 or SLOPES[hh] * (128.0 * dl - 127.0) <= 80.0
            units = [(hh, bi) for hh in range(4) for bi in range(len(blocks)) if live(hh, bi)]
            first_bi = {hh: min(bi for (h2, bi) in units if h2 == hh) for hh in range(4)}
            nb = len(blocks)
            Ocur = {}

            def emit_qk(hh, bi):
                (vb, dl, kb, kc0, diag) = blocks[bi]
                sv, st = PB.get()
                p.op("pe", lambda e: e.matmul(sv[0:kb, 0:2 * SUB].rearrange("p (c s) -> p c s", c=2), lhsT=Kc[l][:, hh, kc0:kc0 + kb],
                                              rhs=U[:, hh:hh + 25:24, qs], start=True, stop=True),
                     reads=[Kc_tl[l][hh], U_tl[hh], U_tl[24 + hh]], writes=[st])
                ptv, ptt = HP.get()
                if not diag:
                    p.op("act", lambda e: e.activation(out=ptv[0:kb, 0:2 * SUB], in_=sv[0:kb, 0:2 * SUB], func=AF.Exp, scale=0.125, bias=biasT[0:kb, hh * 16 + dl:hh * 16 + dl + 1]),
                         reads=[st, cst_tl], writes=[ptt])
                else:
                    tv, tt_ = FP.get()
                    for c in range(2):
                        p.op("dve", lambda e: e.scalar_tensor_tensor(out=tv[0:kb, c * SUB:(c + 1) * SUB], in0=sv[0:kb, c * SUB:(c + 1) * SUB], scalar=0.125,
                                                                     in1=Dtab[0:kb, hh * 128:hh * 128 + SUB], op0=ALU.mult, op1=ALU.add),
                             reads=[st, cst_tl], writes=[tt_])
                    p.op("act", lambda e: e.activation(out=ptv[0:kb, 0:2 * SUB], in_=tv[0:kb, 0:2 * SUB], func=AF.Exp), reads=[tt_], writes=[ptt])
                return ptv, ptt

            def emit_pv(hh, bi, ptv, ptt):
                (vb, dl, kb, kc0, diag) = blocks[bi]
                if bi == first_bi[hh]:
                    Ocur[hh] = PC.get()
                Ov, Ot = Ocur[hh]
                for c in range(2):
                    p.op("pe", lambda e: e.matmul(Ov[0:SUB, c * 130:(c + 1) * 130], lhsT=ptv[0:kb, c * SUB:(c + 1) * SUB], rhs=Vc[l][0:kb, vb, hh * 130:(hh + 1) * 130],
                                                  start=(bi == first_bi[hh] and c == 0), stop=(bi == nb - 1), skip_group_check=True),
                         reads=[ptt, Vc_tl[l][vb]], writes=[Ot])
                if bi != nb - 1:
                    return
                rcv, rct = smalls.get()
                p.op("dve", lambda e: e.reciprocal(out=rcv[0:SUB, 0:1], in_=Ov[0:SUB, 128:129]), reads=[Ot], writes=[rct])
                p.op("dve", lambda e: e.reciprocal(out=rcv[0:SUB, 1:2], in_=Ov[0:SUB, 258:259]), reads=[Ot], writes=[rct])
                p.op("dve", lambda e: e.tensor_scalar(out=rcv[0:SUB, 1:2], in0=rcv[0:SUB, 1:2], scalar1=lamv[0:SUB, l, 0:1], scalar2=None, op0=ALU.mult), reads=[rct, lam_tl], writes=[rct])
                tv, tt_ = FP.get()
                p.op("dve", lambda e: e.tensor_scalar(out=tv[0:SUB, 0:128], in0=Ov[0:SUB, 130:258], scalar1=rcv[0:SUB, 1:2], scalar2=None, op0=ALU.mult), reads=[Ot, rct], writes=[tt_])
                p.op("dve", lambda e: e.scalar_tensor_tensor(out=aov[0:SUB, hh * 128:(hh + 1) * 128], in0=Ov[0:SUB, 0:128], scalar=rcv[0:SUB, 0:1], in1=tv[0:SUB, 0:128],
                                                             op0=ALU.mult, op1=ALU.add), reads=[Ot, rct, tt_], writes=[aot])

            pend = []
            for (hh, bi) in units:
                pt_ = emit_qk(hh, bi)
                pend.append((hh, bi) + pt_)
                if len(pend) > 5:
                    emit_pv(*pend.pop(0))
            while pend:
                emit_pv(*pend.pop(0))
            for fn_ in deferred:
                fn_()
            deferred.clear()
            dbg('ao', aov[0:SUB, :], [aot])
            ssv, sst = smalls.get()
            jv, jt = FP.get()
            p.op("act", lambda e: e.activation(out=jv[0:SUB, 0:512], in_=aov[0:SUB, 0:512], func=AF.Square), reads=[aot], writes=[jt])
            p.op("dve", lambda e: e.tensor_reduce(out=ssv[0:SUB, 0:4], in_=jv[0:SUB, 0:512].rearrange("p (h d) -> p h d", h=4), axis=AX.X, op=ALU.add), reads=[jt], writes=[sst])
            p.op("dve", lambda e: e.tensor_scalar(out=ssv[0:SUB, 0:4], in0=ssv[0:SUB, 0:4], scalar1=1.0 / 128.0, scalar2=LN_EPS, op0=ALU.mult, op1=ALU.add), reads=[sst], writes=[sst])
            p.op("act", lambda e: e.activation(out=ssv[0:SUB, 0:4], in_=ssv[0:SUB, 0:4], func=AF.Ln), reads=[sst], writes=[sst])
            p.op("act", lambda e: e.activation(out=ssv[0:SUB, 0:4], in_=ssv[0:SUB, 0:4], func=AF.Exp, scale=-0.5), reads=[sst], writes=[sst])
            _chk('d0e')
            anv, ant = ANP.get()
            for hh in range(4):
                p.op("act", lambda e: e.activation(out=anv[0:SUB, hh * 128:(hh + 1) * 128], in_=aov[0:SUB, hh * 128:(hh + 1) * 128], func=AF.Identity, scale=ssv[0:SUB, hh:hh + 1]),
                     reads=[aot, sst], writes=[ant])

            def an_transposes(anv=anv, ant=ant, qs=qs):
                pv, pt = PA.get()
                pvb = pv[:].bitcast(BF16)
                for hh in range(4):
                    p.op("pe", lambda e: e.transpose(out=pvb[:, hh * SUB:(hh + 1) * SUB], in_=anv[0:SUB, hh * 128:(hh + 1) * 128], identity=identb[0:SUB, 0:SUB]), reads=[ant, cb_tl], writes=[pt])
                for hh in range(4):
                    p.op("act", lambda e: e.activation(out=U[:, 16 + hh, qs], in_=pvb[:, hh * SUB:(hh + 1) * SUB], func=AF.Identity, scale=vcol(V_DAN + hh)), reads=[pt, vec_tl], writes=[U_tl[16 + hh]])

            _chk('d1')
            p.tag = 'd-mlstm'
            psS, psSt = PA.get()
            for hh in range(4):
                p.op("pe", lambda e: e.matmul(psS[0:SUB, hh * SUB:(hh + 1) * SUB], lhsT=U[:, 8 + hh, qs], rhs=U[:, 4 + hh, qs], start=True, stop=True),
                     reads=[U_tl[8 + hh], U_tl[4 + hh]], writes=[psSt])
            smv, smt = HP.get()
            for hh in range(4):
                p.op("dve", lambda e: e.scalar_tensor_tensor(out=smv[0:SUB, hh * SUB:(hh + 1) * SUB], in0=psS[0:SUB, hh * SUB:(hh + 1) * SUB], scalar=ekcl[0:SUB, r, hh:hh + 1],
                                                             in1=maskb[0:SUB, 0:SUB], op0=ALU.mult, op1=ALU.mult), reads=[psSt, ekcl_tl[r], cb_tl], writes=[smt])
            psK, psKt = PA.get()
            psKb = psK[:].bitcast(BF16)
            for hh in range(4):
                p.op("pe", lambda e: e.transpose(out=psKb[0:SUB, hh * 128:(hh + 1) * 128], in_=U[:, 8 + hh, qs], identity=identb[:, :]), reads=[U_tl[8 + hh], cb_tl], writes=[psKt])
            khv, kht = HP.get()
            for hh in range(4):
                p.op("act", lambda e: e.activation(out=khv[0:SUB, hh * 128:(hh + 1) * 128], in_=psKb[0:SUB, hh * 128:(hh + 1) * 128], func=AF.Identity, scale=ekcl[0:SUB, r, hh:hh + 1]),
                     reads=[psKt, ekcl_tl[r]], writes=[kht])
            for hh in range(4):
                p.op("dve", lambda e: e.tensor_scalar(out=G[l][:, hh * 130:(hh + 1) * 130], in0=G[l][:, hh * 130:(hh + 1) * 130], scalar1=wcb[:, r * 4 + hh:r * 4 + hh + 1], scalar2=None, op0=ALU.mult),
                     reads=[G_tl[l], wcb_tl], writes=[G_tl[l]])
            gbv, gbt = HP.get()
            p.op("act", lambda e: e.activation(out=gbv[:, 0:520], in_=G[l][:, :], func=AF.Copy), reads=[G_tl[l]], writes=[gbt])
            Hps = [PA.get(), PA.get()]
            for hh in range(4):
                hv, ht = Hps[hh // 2]
                o0 = (hh % 2) * 130
                p.op("pe", lambda e: e.matmul(hv[0:SUB, o0:o0 + 130], lhsT=U[:, 4 + hh, qs], rhs=gbv[:, hh * 130:(hh + 1) * 130], start=(hh % 2 == 0), stop=False, skip_group_check=True),
                     reads=[U_tl[4 + hh], gbt], writes=[ht])
                p.op("pe", lambda e: e.matmul(hv[0:SUB, o0:o0 + 130], lhsT=smv[0:SUB, hh * SUB:(hh + 1) * SUB], rhs=mv_t[0:SUB, r, hh * 130:(hh + 1) * 130], start=False, stop=True, skip_group_check=True),
                     reads=[smt, mv_tl[r]], writes=[ht])
            ddv, ddt = smalls.get()
            hmv, hmt = FP.get()
            for hp_ in range(2):
                hv, ht = Hps[hp_]
                den = hv[0:SUB, 0:260].rearrange("p (c d) -> p c d", c=2)[:, :, 128]
                p.op("dve", lambda e: e.tensor_scalar(out=ddv[0:SUB, 8 + hp_ * 2:10 + hp_ * 2], in0=den, scalar1=-1.0, scalar2=None, op0=ALU.mult), reads=[ht], writes=[ddt])
                p.op("dve", lambda e: e.tensor_tensor(out=ddv[0:SUB, 8 + hp_ * 2:10 + hp_ * 2], in0=den, in1=ddv[0:SUB, 8 + hp_ * 2:10 + hp_ * 2], op=ALU.max), reads=[ht, ddt], writes=[ddt])
                p.op("dve", lambda e: e.tensor_tensor(out=ddv[0:SUB, hp_ * 2:hp_ * 2 + 2], in0=ddv[0:SUB, 8 + hp_ * 2:10 + hp_ * 2], in1=ekcl[0:SUB, r, 4 + hp_ * 2:6 + hp_ * 2], op=ALU.max), reads=[ddt, ekcl_tl[r]], writes=[ddt])
            p.op("dve", lambda e: e.reciprocal(out=ddv[0:SUB, 0:4], in_=ddv[0:SUB, 0:4]), reads=[ddt], writes=[ddt])
            for hh in range(4):
                hv, ht = Hps[hh // 2]
                o0 = (hh % 2) * 130
                p.op("act", lambda e: e.activation(out=hmv[0:SUB, hh * 128:(hh + 1) * 128], in_=hv[0:SUB, o0:o0 + 128], func=AF.Identity, scale=ddv[0:SUB, hh:hh + 1]), reads=[ht, ddt], writes=[hmt])
            dbg('hm', hmv[0:SUB, 0:512], [hmt])
            dbg('ddv', ddv[0:SUB, 0:16], [ddt])
            dbg('smv', smv[0:SUB, 0:512], [smt])
            dbg('khv', khv[0:SUB, 0:512], [kht])
            dbg('gbv', gbv[:, 0:520], [gbt])
            dGs = [PA.get(), PA.get()]
            for hh in range(4):
                gv, gt_ = dGs[hh // 2]
                o0 = (hh % 2) * 130
                p.op("pe", lambda e: e.matmul(gv[:, o0:o0 + 130], lhsT=khv[0:SUB, hh * 128:(hh + 1) * 128], rhs=mv_t[0:SUB, r, hh * 130:(hh + 1) * 130], start=(hh % 2 == 0), stop=True, skip_group_check=True),
                     reads=[kht, mv_tl[r]], writes=[gt_])
            an_transposes()
            for hp_ in range(2):
                gv, gt_ = dGs[hp_]
                p.op("dve", lambda e: e.tensor_tensor(out=G[l][:, hp_ * 260:(hp_ + 1) * 260], in0=gv[:, 0:260], in1=G[l][:, hp_ * 260:(hp_ + 1) * 260], op=ALU.add), reads=[gt_, G_tl[l]], writes=[G_tl[l]])
            stv, stt = smalls.get()
            mvv, mvt = smalls.get()
            for hh in range(4):
                p.op("dve", lambda e: e.bn_stats(out=stv[0:SUB, hh * 6:(hh + 1) * 6], in_=hmv[0:SUB, hh * 128:(hh + 1) * 128]), reads=[hmt], writes=[stt])
                p.op("dve", lambda e: e.bn_aggr(out=mvv[0:SUB, hh * 2:(hh + 1) * 2], in_=stv[0:SUB, hh * 6:(hh + 1) * 6]), reads=[stt], writes=[mvt])
            p.op("dve", lambda e: e.tensor_scalar(out=mvv[0:SUB, 8:12], in0=mvv[0:SUB, 0:8].rearrange("p (h t) -> p h t", t=2)[:, :, 1], scalar1=LN_EPS, scalar2=None, op0=ALU.add), reads=[mvt], writes=[mvt])
            p.op("act", lambda e: e.activation(out=mvv[0:SUB, 8:12], in_=mvv[0:SUB, 8:12], func=AF.Ln), reads=[mvt], writes=[mvt])
            p.op("act", lambda e: e.activation(out=mvv[0:SUB, 8:12], in_=mvv[0:SUB, 8:12], func=AF.Exp, scale=-0.5), reads=[mvt], writes=[mvt])
            p.op("dve", lambda e: e.scalar_tensor_tensor(out=mvv[0:SUB, 12:16], in0=mvv[0:SUB, 0:8].rearrange("p (h t) -> p h t", t=2)[:, :, 0], scalar=-1.0, in1=mvv[0:SUB, 8:12], op0=ALU.mult, op1=ALU.mult),
                 reads=[mvt], writes=[mvt])
            mnv, mnt = MNP.get()
            for hh in range(4):
                p.op("act", lambda e: e.activation(out=mnv[0:SUB, hh * 128:(hh + 1) * 128], in_=hmv[0:SUB, hh * 128:(hh + 1) * 128], func=AF.Identity, scale=mvv[0:SUB, 8 + hh:9 + hh], bias=mvv[0:SUB, 12 + hh:13 + hh]),
                     reads=[hmt, mvt], writes=[mnt])

            def mn_transposes(mnv=mnv, mnt=mnt, qs=qs):
                pv, pt = PA.get()
                pvb = pv[:].bitcast(BF16)
                for hh in range(4):
                    p.op("pe", lambda e: e.transpose(out=pvb[:, hh * SUB:(hh + 1) * SUB], in_=mnv[0:SUB, hh * 128:(hh + 1) * 128], identity=identb[0:SUB, 0:SUB]), reads=[mnt, cb_tl], writes=[pt])
                for hh in range(4):
                    p.op("dve", lambda e: e.scalar_tensor_tensor(out=U[:, 20 + hh, qs], in0=pvb[:, hh * SUB:(hh + 1) * SUB], scalar=vcol(V_MN + hh), in1=U[:, 12 + hh, qs], op0=ALU.mult, op1=ALU.mult),
                         reads=[pt, vec_tl, U_tl[12 + hh]], writes=[U_tl[20 + hh]])
            deferred.append(mn_transposes)
        for fn_ in deferred:
            fn_()
        deferred.clear()

        dbg('anT', U[:, 16:20, 0:n], U_tl[16:20])
        dbg('mnT', U[:, 20:24, 0:n], U_tl[20:24])
        dbg('G', G[l][:, :], [G_tl[l]])
        _chk('d')
        p.tag = 'e-gate'
        for gi, c0 in enumerate((0, 512, 1024, 1536)):
            wv, wt = wload("w_gate", l, 0, 8, c0, 512)
            for cc in range(4):
                gj = gi * 4 + cc
                pv, pt = proj_fm(wv, wt, cc, hs, h_tl, 8)
                p.op("act", lambda e: e.activation(out=U[:, gj, 0:n], in_=pv[:, 0:n], func=AF.Sigmoid, bias=vcol(V_BGATE + gj)), reads=[pt, vec_tl], writes=[U_tl[gj]])
        p.tag = 'e-merge'
        wa, wat = wload("w_br_a", l, 0, 4, 0, 1024)
        wb, wbt = wload("w_br_b", l, 0, 4, 0, 1024)
        an_ch = [U[:, 16 + k, 0:n] for k in range(4)]
        mn_ch = [U[:, 20 + k, 0:n] for k in range(4)]
        for j in range(8):
            pva, pta = proj_fm(wa, wat, j, an_ch, U_tl[16:20], 4)
            pvb_, ptb = proj_fm(wb, wbt, j, mn_ch, U_tl[20:24], 4)
            t1, t1t = FP.get()
            t2, t2t = FP.get()
            p.op("dve", lambda e: e.tensor_tensor(out=t1[:, 0:n], in0=pva[:, 0:n], in1=U[:, j, 0:n], op=ALU.mult), reads=[pta, U_tl[j]], writes=[t1t])
            p.op("dve", lambda e: e.tensor_tensor(out=t2[:, 0:n], in0=pvb_[:, 0:n], in1=U[:, 8 + j, 0:n], op=ALU.mult), reads=[ptb, U_tl[8 + j]], writes=[t2t])
            p.op("pool", lambda e: e.tensor_tensor(out=mixin[:, j, 0:n], in0=t1[:, 0:n], in1=t2[:, 0:n], op=ALU.add), reads=[t1t, t2t], writes=[mix_tl[j]])

        dbg('mixin', mixin[:, :, 0:n], mix_tl)
        _chk('e')
        p.tag = 'f-wo'
        mix_ch = [mixin[:, k, 0:n] for k in range(8)]
        for half in range(2):
            wv, wt = wload("w_o", l, 0, 8, half * 512, 512)
            for cc in range(4):
                j = half * 4 + cc
                pv, pt = proj_fm(wv, wt, cc, mix_ch, mix_tl, 8)
                p.op("dve", lambda e: e.scalar_tensor_tensor(out=xs[j], in0=pv[:, 0:n], scalar=m2col(2, j), in1=xs[j], op0=ALU.mult, op1=ALU.add), reads=[pt, mod_tl, x_tl[j]], writes=[x_tl[j]])
        p.tag = 'f-ln'
        B1, B2 = layer_norm_stats(xs, x_tl, n, LN_EPS / (ALPHA * ALPHA))
        ctx_g = ln_begin()
        for j in range(8):
            ln_apply(xs[j], x_tl[j], xs[j], x_tl[j], n, B1, B2, vcol(V_LN1G + j), vcol(V_LN1B + j), [vec_tl])
            ln_chunk(ctx_g, j, xs[j], x_tl[j], n)

        dbg('x1', x_t[:, :, 0:n], x_tl)
        _chk('f')
        p.tag = 'g-ln'
        B1, B2 = ln_finish(ctx_g, n, LN_EPS)
        for j in range(8):
            ln_apply(xs[j], x_tl[j], h_t[:, j, 0:n], h_tl[j], n, B1, B2, m2col(4, j), mcol(3, j), [mod_tl])

        _chk('g')
        p.tag = 'h-gu'
        i0 = 0
        while i0 < NFF:
            nch = min(4, NFF - i0)
            wg_, wgt = wload("w_gu", l, 0, 8, i0 * 128, nch * 128)
            wu_, wut = wload("w_gu", l, 0, 8, DFF + i0 * 128, nch * 128)
            for cc in range(nch):
                i = i0 + cc
                pvg, ptg = proj_fm(wg_, wgt, cc, hs, h_tl, 8)
                pvu, ptu = proj_fm(wu_, wut, cc, hs, h_tl, 8)
                sgv, sgt = FP.get()
                p.op("act", lambda e: e.activation(out=sgv[:, 0:n], in_=pvg[:, 0:n], func=AF.Silu), reads=[ptg], writes=[sgt])
                p.op("dve", lambda e: e.tensor_tensor(out=U[:, i, 0:n], in0=pvu[:, 0:n], in1=sgv[:, 0:n], op=ALU.mult), reads=[ptu, sgt], writes=[U_tl[i]])
            i0 += nch
        p.tag = 'h-down'
        hid_ch = [U[:, i, 0:n] for i in range(NFF)]
        for cg in range(2):
            banks = [(PA if cg == 0 else PBC).get() for _ in range(4)]
            for (k0, nk) in ((0, 8), (8, 8), (16, 6)):
                wv, wt = wload("w_down", l, k0, nk, cg * 512, 512)
                for cc in range(4):
                    proj_fm(wv, wt, cc, hid_ch, U_tl, nk, k0=k0, first=(k0 == 0), last=(k0 == 16), pv=banks[cc][0], pt=banks[cc][1])
            for cc in range(4):
                j = cg * 4 + cc
                pv, pt = banks[cc]
                p.op("dve", lambda e: e.scalar_tensor_tensor(out=xs[j], in0=pv[:, 0:n], scalar=m2col(5, j), in1=xs[j], op0=ALU.mult, op1=ALU.add), reads=[pt, mod_tl, x_tl[j]], writes=[x_tl[j]])
        p.tag = 'h-ln'
        B1, B2 = layer_norm_stats(xs, x_tl, n, LN_EPS / (ALPHA * ALPHA))
        for j in range(8):
            ln_apply(xs[j], x_tl[j], xs[j], x_tl[j], n, B1, B2, vcol(V_LN2G + j), vcol(V_LN2B + j), [vec_tl])

    for l in range(DEPTH):
        lam_init = 0.8 - 0.6 * math.exp(-0.3 * l)
        p.op("dve", lambda e: e.tensor_scalar(out=vec_sb[:, l, V_DAN:V_DAN + 4], in0=vec_sb[:, l, V_DAN:V_DAN + 4], scalar1=1.0 - lam_init, scalar2=None, op0=ALU.mult),
             reads=[vec_tl], writes=[vec_tl])

    try:
        _chk('pro')
        for si in range(NP):
            run_sequence(si, False)
        if with_sample:
            run_sequence(NP, True)
    except _Stop:
        pass

    for key, sem in p.dsems.items():
        if p.dcnt[key] > 0:
            nc.sync.wait_ge(sem, p.dcnt[key])
    p.sbuf_left = nc.sbuf_bytes_remaining
    return nc, p


def _consts():
    ident = np.eye(128, dtype=np.float32)
    s_idx = np.arange(128)[:, None]
    t_idx = np.arange(128)[None, :]
    maskST = (s_idx <= t_idx).astype(np.float32)
    Dtab = np.zeros((128, 4, 128), np.float32)
    biasT = np.zeros((128, 4, 16), np.float32)
    kl = np.arange(128)[:, None].astype(np.float64)
    ql = np.arange(128)[None, :].astype(np.float64)
    vis = (kl // 64) <= (ql // 64)
    for h in range(4):
        s = SLOPES[h]
        d = np.where(kl <= ql, s * kl, s * (2 * ql - kl))
        Dtab[:, h, :] = np.where(vis, d, NEG)
        for dl in range(16):
            biasT[:, h, dl] = s * kl[:, 0] - s * 128.0 * dl
    c = np.concatenate([ident, maskST, Dtab.reshape(128, 512), biasT.reshape(128, 64)], axis=1).astype(np.float32)
    d4 = np.zeros((4, 4, 4), np.float32)
    for h in range(4):
        d4[h, :, h] = 1.0
    return np.ascontiguousarray(c), np.ascontiguousarray(d4.reshape(4, 16))


def _vecs(inp):
    out = np.zeros((DEPTH, 128, NV), np.float32)
    for l in range(DEPTH):
        out[l, :, V_BADA:V_BADA + 48] = inp["b_ada"][l].reshape(48, 128).T
        out[l, :, V_CONVW:V_CONVW + 32] = inp["conv_w"][l].reshape(4, 8, 128).transpose(2, 0, 1).reshape(128, 32)
        out[l, :, V_CONVB:V_CONVB + 8] = inp["conv_b"][l].reshape(8, 128).T
        out[l, :, V_DAN:V_DAN + 4] = inp["da_norm_w"][l].reshape(4, 128).T
        out[l, :, V_MN:V_MN + 4] = inp["m_norm_w"][l].reshape(4, 128).T
        out[l, :, V_BGATE:V_BGATE + 16] = inp["b_gate"][l].reshape(16, 128).T
        out[l, :, V_LN1G:V_LN1G + 8] = inp["ln1_g"][l].reshape(8, 128).T
        out[l, :, V_LN1B:V_LN1B + 8] = inp["ln1_b"][l].reshape(8, 128).T
        out[l, :, V_LN2G:V_LN2G + 8] = inp["ln2_g"][l].reshape(8, 128).T
        out[l, :, V_LN2B:V_LN2B + 8] = inp["ln2_b"][l].reshape(8, 128).T
    return out


_PROG_CACHE = {}


def run_cores(inp, ncores, NP, T, with_sample=True, trace=False):
    key = (NP, T, with_sample)
    if key not in _PROG_CACHE:
        _PROG_CACHE[key] = build_program(NP, T, with_sample)
    nc, p = _PROG_CACHE[key]
    f32 = lambda a: np.ascontiguousarray(np.asarray(a, dtype=np.float32))
    consts, d4 = _consts()
    vecs = _vecs(inp)
    bifh = f32(np.asarray(inp["b_if"]).reshape(DEPTH, 2, 4).transpose(0, 2, 1))
    lamp = f32(np.asarray(inp["lam_p"]).reshape(DEPTH, 1, 256))
    shared = {"vecs": vecs, "bif": bifh, "lamp": lamp, "consts": consts, "delta4": d4}
    for k in W_SHAPES:
        shared[k] = f32(inp[k])
    in_maps = []
    for c in range(ncores):
        m = dict(shared)
        xp = np.asarray(inp["x_prompt"])[c * NP:(c + 1) * NP]
        m["xT"] = f32(xp.transpose(0, 2, 1))
        cs = [np.asarray(inp["c_prompt"])[c * NP + i] for i in range(NP)]
        if with_sample:
            cs.append(np.asarray(inp["c_sample"])[c])
        cmat = np.stack(cs, 0)
        m["cT"] = f32(cmat.reshape(len(cs), 8, 128).transpose(2, 1, 0))
        if with_sample:
            m["xsT"] = f32(np.asarray(inp["x_sample"])[c].T)
            m["ckT"] = f32(np.asarray(inp["cache_attn_k"])[:, c].transpose(0, 2, 3, 1))
            m["cvv"] = f32(np.asarray(inp["cache_attn_v"])[:, c].reshape(DEPTH, PAST, 512))
            C = np.asarray(inp["state_mlstm_C"])[:, c]
            nn = np.asarray(inp["state_mlstm_n"])[:, c]
            sG = np.zeros((DEPTH, 128, 4, 130), np.float32)
            sG[:, :, :, 0:128] = C.transpose(0, 3, 1, 2)
            sG[:, :, :, 128] = nn.transpose(0, 2, 1)
            sG[:, :, :, 129] = nn.transpose(0, 2, 1)
            m["sG"] = f32(sG.reshape(DEPTH, 128, 520))
            m["sm"] = f32(np.asarray(inp["state_mlstm_m"])[:, c].reshape(DEPTH, 4, 1))
            cv = np.asarray(inp["state_mlstm_conv"])[:, c]
            m["sconv"] = f32(cv.reshape(DEPTH, 3, 8, 128).transpose(0, 3, 2, 1).reshape(DEPTH, 128, 24))
        in_maps.append(m)
    res = run_bass_kernel_spmd(nc, in_maps, core_ids=list(range(ncores)), trace=trace)
    return res


def assemble(results, ncores, NP, T, with_sample=True):
    B = ncores * NP
    y = np.zeros((B, T, D), np.float32)
    ak = np.zeros((DEPTH, B, T, 4, 128), np.float32)
    av = np.zeros((DEPTH, B, T, 4, 128), np.float32)
    Cp = np.zeros((DEPTH, B, 4, 128, 128), np.float32)
    npp = np.zeros((DEPTH, B, 4, 128), np.float32)
    mp = np.zeros((DEPTH, B, 4), np.float32)
    cvp = np.zeros((DEPTH, B, 3, 1024), np.float32)
    Bs = ncores
    ys = np.zeros((Bs, TS, D), np.float32)
    aks = np.zeros((DEPTH, Bs, TS, 4, 128), np.float32)
    avs = np.zeros((DEPTH, Bs, TS, 4, 128), np.float32)
    Cs = np.zeros((DEPTH, Bs, 4, 128, 128), np.float32)
    ns = np.zeros((DEPTH, Bs, 4, 128), np.float32)
    ms = np.zeros((DEPTH, Bs, 4), np.float32)
    cvs = np.zeros((DEPTH, Bs, 3, 1024), np.float32)

    def unG(g):
        g = g.reshape(g.shape[:-1] + (4, 130))
        Cc = np.moveaxis(g[..., 0:128], -3, -1)
        nn = np.moveaxis(g[..., 128], -2, -1)
        return Cc, nn

    def unconv(cv):
        cv = cv.reshape(cv.shape[:-1] + (8, 3))
        return np.moveaxis(cv, -1, -3).swapaxes(-1, -2).reshape(cv.shape[:-3] + (3, 1024))

    for c in range(ncores):
        r = results[c]
        sl = slice(c * NP, (c + 1) * NP)
        y[sl] = r["yT"].transpose(0, 2, 1)
        ak[:, sl] = r["okT"].transpose(0, 1, 3, 2).reshape(DEPTH, NP, T, 4, 128)
        av[:, sl] = r["ov"].reshape(DEPTH, NP, T, 4, 128)
        Cc, nn = unG(r["oG"])
        Cp[:, sl] = Cc
        npp[:, sl] = nn
        mp[:, sl] = r["om"][..., 0]
        cvp[:, sl] = unconv(r["oconv"])
        if with_sample:
            ys[c] = r["ysT"].T
            aks[:, c] = r["oksT"].transpose(0, 2, 1).reshape(DEPTH, TS, 4, 128)
            avs[:, c] = r["ovs"].reshape(DEPTH, TS, 4, 128)
            Cc, nn = unG(r["oGs"])
            Cs[:, c] = Cc
            ns[:, c] = nn
            ms[:, c] = r["oms"][..., 0]
            cvs[:, c] = unconv(r["oconvs"])
    return (y, ys, ak, av, aks, avs, Cp, npp, mp, cvp, Cs, ns, ms, cvs)


def kernel(**inputs):
    ncores = 8
    NP = 4
    T = 2048
    res = run_cores(inputs, ncores, NP, T, True)
    return assemble(res.results, ncores, NP, T, True)
```

```python
import math
from contextlib import ExitStack

import numpy as np
import concourse.bass as bass
import concourse.mybir as mybir
from concourse.bass_utils import run_bass_kernel_spmd

F32 = mybir.dt.float32
BF16 = mybir.dt.bfloat16
AF = mybir.ActivationFunctionType
ALU = mybir.AluOpType
AX = mybir.AxisListType

D = 1024
DEPTH = 2
NH = 4
DFF = 2816
NFF = 22
IN_COLS = 3592
LN_EPS = 1e-5
ALPHA = (2 * DEPTH) ** 0.25
SLOPES = [2.0 ** (-8.0 * (i + 1) / 4) for i in range(4)]
PAST = 1024
TS = 32
NEG = -30000.0
KSCALE = 128 ** -0.5

V_BADA = 0
V_CONVW = 48
V_CONVB = 80
V_DAN = 88
V_MN = 92
V_BGATE = 96
V_LN1G = 112
V_LN1B = 120
V_LN2G = 128
V_LN2B = 136
NV = 144

W_SHAPES = {
    "w_ada": (D, 6 * D), "w_in": (D, IN_COLS), "w_br_a": (512, D), "w_br_b": (512, D),
    "w_gate": (D, 2 * D), "w_o": (D, D), "w_gu": (D, 2 * DFF), "w_down": (DFF, D),
}


STOP = None
DEBUG = None


class _Stop(Exception):
    pass


def _chk(tag):
    if STOP == tag:
        raise _Stop()


class TT:
    __slots__ = ("w", "r", "excl", "small")

    def __init__(self, excl=False, small=False):
        self.w = None
        self.r = {}
        self.excl = excl
        self.small = small


class P:
    def __init__(self, nc, es):
        self.nc = nc
        self.es = es
        self.engs = {"pe": nc.tensor, "act": nc.scalar, "dve": nc.vector, "pool": nc.gpsimd, "sp": nc.sync}
        self.sems = {}
        self.cnt = {}
        self.seen = {e: {} for e in self.engs}
        for e in self.engs:
            self.sems[e] = es.enter_context(nc.semaphore("s_" + e))
            self.cnt[e] = 0
        self.dsems = {}
        self.dcnt = {}
        self.nwait = 0
        self.ninst = 0
        self.small_mode = False
        self.tag = ''
        self.pe_tags = []
        self.know = {}

    def _deps(self, reads, writes, eng=None):
        deps = {}
        same = 0
        sm = self.small_mode

        def add(k, v, small):
            nonlocal same
            if k == eng:
                if eng != "pe" and (small or sm) and v > same:
                    same = v
                return
            if deps.get(k, 0) < v:
                deps[k] = v
        for t in reads:
            if t.w is not None:
                add(t.w[0], t.w[1], t.small)
            if t.excl:
                for k, v in t.r.items():
                    add(k, v, t.small)
        for t in writes:
            if t.w is not None:
                add(t.w[0], t.w[1], t.small)
            for k, v in t.r.items():
                add(k, v, t.small)
        if same:
            deps[eng] = same
        return deps

    def _wait(self, eng, deps):
        seen = self.seen[eng]
        for k, v in deps.items():
            if seen.get(k, 0) >= v:
                continue
            sem = self.sems[k] if k in self.sems else self.dsems[k]
            self.engs[eng].wait_ge(sem, v)
            seen[k] = v
            self.nwait += 1
            kn = self.know.get((k, v))
            if kn:
                for k2, v2 in kn.items():
                    if seen.get(k2, 0) < v2:
                        seen[k2] = v2

    def _commit(self, ev, reads, writes):
        k, v = ev
        for t in writes:
            t.w = ev
            t.r = {}
        for t in reads:
            if t.excl:
                t.w = ev
                t.r = {}
            else:
                if t.r.get(k, 0) < v:
                    t.r[k] = v

    def op(self, eng, fn, reads=(), writes=()):
        deps = self._deps(reads, writes, eng)
        self._wait(eng, deps)
        inst = fn(self.engs[eng])
        if eng == 'pe':
            self.pe_tags.append(self.tag)
        self.cnt[eng] += 1
        inst.then_inc(self.sems[eng], 1)
        self.know[(eng, self.cnt[eng])] = dict(self.seen[eng])
        self._commit((eng, self.cnt[eng]), reads, writes)
        self.ninst += 1
        return inst

    def dma(self, q, key, out, in_, reads=(), writes=()):
        if key not in self.dsems:
            self.dsems[key] = self.es.enter_context(self.nc.semaphore("d_" + key))
            self.dcnt[key] = 0
        deps = self._deps(reads, writes)
        self._wait(q, deps)
        self.dcnt[key] += 16
        self.engs[q].dma_start(out=out, in_=in_).then_inc(self.dsems[key], 16)
        kn = dict(self.seen[q])
        kn[q] = max(kn.get(q, 0), self.cnt[q])
        self.know[(key, self.dcnt[key])] = kn
        self._commit((key, self.dcnt[key]), reads, writes)
        self.ninst += 1

    def finish(self, tiles):
        deps = self._deps(tiles, tiles)
        self._wait("sp", deps)


class Pool:
    def __init__(self, views, small=False):
        self.views = views
        self.tiles = [TT(small=small) for _ in views]
        self.i = 0

    def get(self):
        i = self.i
        self.i = (i + 1) % len(self.views)
        return self.views[i], self.tiles[i]


def build_program(NP, T, with_sample=True, TTK=512):
    nc = bass.Bass("TRN2", target_bir_lowering=False, dynamic_dma_scratch_size=4096)
    NSEQ = NP + (1 if with_sample else 0)
    NTILE = T // TTK
    NBLK = max(T // 128, 9)

    def din(name, shape, dt=F32):
        return nc.dram_tensor(name, list(shape), dt, kind="ExternalInput").ap()

    def dout(name, shape, dt=F32):
        return nc.dram_tensor(name, list(shape), dt, kind="ExternalOutput").ap()

    xT = din("xT", (NP, D, T))
    cT = din("cT", (128, 8, NSEQ))
    vecs = din("vecs", (DEPTH, 128, NV))
    bif = din("bif", (DEPTH, 4, 2))
    lamp = din("lamp", (DEPTH, 1, 256))
    consts = din("consts", (128, 128 * 2 + 4 * 128 + 64))
    delta4 = din("delta4", (4, 16))
    W = {k: din(k, (DEPTH,) + v) for k, v in W_SHAPES.items()}
    WB = {k: nc.dram_tensor(k + "_bf", [DEPTH] + list(v), BF16, kind="Internal").ap() for k, v in W_SHAPES.items()}
    yT = dout("yT", (NP, D, T))
    okT = dout("okT", (DEPTH, NP, 512, T))
    ov = dout("ov", (DEPTH, NP, T, 512))
    oG = dout("oG", (DEPTH, NP, 128, 4 * 130))
    om = dout("om", (DEPTH, NP, 4, 1))
    oconv = dout("oconv", (DEPTH, NP, 128, 24))
    if with_sample:
        xsT = din("xsT", (D, TS))
        ckT = din("ckT", (DEPTH, 4, 128, PAST))
        cvv = din("cvv", (DEPTH, PAST, 512))
        sG = din("sG", (DEPTH, 128, 4 * 130))
        sm = din("sm", (DEPTH, 4, 1))
        sconv = din("sconv", (DEPTH, 128, 24))
        ysT = dout("ysT", (D, TS))
        oksT = dout("oksT", (DEPTH, 512, TS))
        ovs = dout("ovs", (DEPTH, TS, 512))
        oGs = dout("oGs", (DEPTH, 128, 4 * 130))
        oms = dout("oms", (DEPTH, 4, 1))
        oconvs = dout("oconvs", (DEPTH, 128, 24))

    es = ExitStack()
    p = P(nc, es)
    dbg_count = [0]

    def dbg(name, ap, tiles, once=True):
        if DEBUG is None or name not in DEBUG:
            return
        if once and name in dbg_seen:
            return
        dbg_seen.add(name)
        shape = list(ap.shape)
        d = nc.dram_tensor("dbg_" + name, shape, ap.dtype, kind="ExternalOutput").ap()
        p.dma("act", "dbg_" + name, d, ap, reads=tiles)
    dbg_seen = set()

    def sb(name, shape, dt):
        return es.enter_context(nc.sbuf_tensor(name, list(shape), dt))

    x_t = sb("x_t", (128, 8, TTK), F32)
    x_tl = [TT() for _ in range(8)]
    h_t = sb("h_t", (128, 8, TTK), BF16)
    h_tl = [TT() for _ in range(8)]
    U = sb("U", (128, 28, TTK), BF16)
    U_tl = [TT() for _ in range(28)]
    mixin = sb("mixin", (128, 8, TTK), BF16)
    mix_tl = [TT() for _ in range(8)]
    Kc = [sb(f"Kc{l}", (128, 4, max(T, PAST + TS)), BF16) for l in range(DEPTH)]
    Kc_tl = [[TT() for _ in range(4)] for l in range(DEPTH)]
    Vc = [sb(f"Vc{l}", (128, NBLK, 4 * 130), BF16) for l in range(DEPTH)]
    Vc_tl = [[TT() for _ in range(NBLK)] for l in range(DEPTH)]
    G = [sb(f"G{l}", (128, 4 * 130), F32) for l in range(DEPTH)]
    G_tl = [TT() for l in range(DEPTH)]
    carry = [sb(f"carry{l}", (128, 24), F32) for l in range(DEPTH)]
    carry_tl = [TT(small=True) for l in range(DEPTH)]
    mv_t = sb("mv_t", (128, 4, 4 * 130), BF16)
    mv_tl = [TT() for _ in range(4)]
    wslots = Pool([sb(f"wslot{i}", (128, 8 * 512), BF16) for i in range(3)])
    NFP = 6
    FP = Pool([sb(f"fp{i}", (128, 520), F32) for i in range(NFP)])
    ANP = Pool([sb(f"anp{i}", (128, 512), BF16) for i in range(2)])
    MNP = Pool([sb(f"mnp{i}", (128, 512), BF16) for i in range(2)])
    AO = Pool([sb(f"ao{i}", (128, 512), F32) for i in range(2)])
    NHP = 8
    HP = Pool([sb(f"hp{i}", (128, 520), BF16) for i in range(NHP)])
    pre = Pool([sb(f"pre{i}", (128, 3 + TTK), F32) for i in range(2)])
    g_u = sb("g_u", (4, TTK), F32)
    g_lf = sb("g_lf", (4, TTK), F32)
    g_nb = sb("g_nb", (4, TTK), F32)
    g_ek = sb("g_ek", (4, TTK), F32)
    g_cl = sb("g_cl", (4, TTK), F32)
    g_tl = TT()
    g2_tl = TT()
    gsm = sb("gsm", (128, 64), F32)
    gsm_tl = TT(small=True)
    ekcl = sb("ekcl", (128, 4, 8), F32)
    ekcl_tl = [TT(small=True) for _ in range(4)]
    wcb = sb("wcb", (128, 16), F32)
    wcb_tl = TT(small=True)
    gst = [sb(f"gst{l}", (128, 4), F32) for l in range(DEPTH)]
    gst_tl = [TT(small=True) for l in range(DEPTH)]
    smalls = Pool([sb(f"sml{i}", (128, 32), F32) for i in range(8)], small=True)
    cst = sb("cst", (128, 128 * 2 + 4 * 128 + 64), F32)
    cst_tl = TT()
    identb = sb("identb", (128, 128), BF16)
    onesb = sb("onesb", (128, 128), BF16)
    maskb = sb("maskb", (128, 128), BF16)
    ones4 = sb("ones4", (4, 128), F32)
    d4 = sb("d4", (4, 16), F32)
    cb_tl = TT(small=True)
    vec_sb = sb("vec_sb", (128, DEPTH, NV), F32)
    vec_tl = TT(small=True)
    c_sb = sb("c_sb", (128, 8, NSEQ), F32)
    c_bf = sb("c_bf", (128, 8, NSEQ), BF16)
    c_tl = TT(small=True)
    mod = sb("mod", (128, DEPTH, 48, NSEQ), F32)
    mod_tl = TT(small=True)
    mod2 = sb("mod2", (128, DEPTH, 48, NSEQ), F32)
    lam_sb = sb("lam_sb", (128, DEPTH, 256), F32)
    lamv = sb("lamv", (128, DEPTH, 4), F32)
    lam_tl = TT(small=True)
    bif_sb = sb("bif_sb", (4, DEPTH, 4), F32)
    bif_tl = TT(small=True)

    ident_f = cst[:, 0:128]
    mask_f = cst[:, 128:256]
    Dtab = cst[:, 256:256 + 512]
    biasT = cst[:, 768:768 + 64]

    PS = [es.enter_context(nc.psum_tensor(f"ps{i}", [128, 512], F32)) for i in range(8)]
    PA = Pool(PS[0:4])
    PB = Pool(PS[0:6])
    PB.tiles[0:4] = PA.tiles[0:4]
    PC = Pool(PS[6:8])
    PBC = Pool(PS[4:8])
    PBC.tiles = [PB.tiles[4], PB.tiles[5], PC.tiles[0], PC.tiles[1]]
    for pl in (PA, PB, PC):
        for t in pl.tiles:
            t.excl = True

    WB_tl = {}
    order = ["w_in", "w_gate", "w_br_a", "w_br_b", "w_o", "w_gu", "w_down"]
    for l in range(DEPTH):
        for k in order:
            R = W_SHAPES[k][0]
            t = TT()
            WB_tl[(k, l)] = t
            nsplit = 4 if R * W_SHAPES[k][1] > 2 ** 21 else 1
            rs = R // nsplit
            key = f"wc_{k}{l}"
            if key not in p.dsems:
                p.dsems[key] = es.enter_context(nc.semaphore("d_" + key))
                p.dcnt[key] = 0
            for i in range(nsplit):
                nc.gpsimd.dma_start(out=WB[k][l, i * rs:(i + 1) * rs, :], in_=W[k][l, i * rs:(i + 1) * rs, :]).then_inc(p.dsems[key], 16)
                p.dcnt[key] += 16
            t.w = (key, p.dcnt[key])

    p.dma("act", "c_cst", cst[:], consts[:, :], writes=[cst_tl])
    p.dma("act", "c_vec", vec_sb[:], vecs.rearrange("l p n -> p l n"), writes=[vec_tl])
    p.dma("act", "c_c", c_sb[:], cT[:, :, :], writes=[c_tl])
    p.dma("act", "c_lam", lam_sb[:], lamp.rearrange("l o n -> o l n").broadcast_to([128, DEPTH, 256]), writes=[lam_tl])
    p.dma("act", "c_bif", bif_sb[:, :, 0:2], bif.rearrange("l h t -> h l t"), writes=[bif_tl])
    p.dma("act", "c_d4", d4[:], delta4[:, :], writes=[cb_tl])
    p.op("dve", lambda e: e.tensor_copy(out=identb[:], in_=ident_f), reads=[cst_tl], writes=[cb_tl])
    p.op("dve", lambda e: e.tensor_copy(out=maskb[:], in_=mask_f), reads=[cst_tl], writes=[cb_tl])
    p.op("dve", lambda e: e.memset(onesb[:], 1.0 / 1024.0), writes=[cb_tl])
    p.op("dve", lambda e: e.memset(ones4[:], 1.0), writes=[cb_tl])
    p.op("dve", lambda e: e.tensor_scalar(out=bif_sb[:, :, 2:3], in0=bif_sb[:, :, 1:2], scalar1=-1.0, scalar2=None, op0=ALU.mult),
         reads=[bif_tl], writes=[bif_tl])
    for l in range(DEPTH):
        lam_init = 0.8 - 0.6 * math.exp(-0.3 * l)
        fv, ft = FP.get()
        p.op("dve", lambda e: e.tensor_tensor(out=fv[:, 0:64], in0=lam_sb[:, l, 0:64], in1=lam_sb[:, l, 64:128], op=ALU.mult), reads=[lam_tl], writes=[ft])
        p.op("dve", lambda e: e.tensor_tensor(out=fv[:, 64:128], in0=lam_sb[:, l, 128:192], in1=lam_sb[:, l, 192:256], op=ALU.mult), reads=[lam_tl], writes=[ft])
        p.op("dve", lambda e: e.tensor_reduce(out=lamv[:, l, 1:3], in_=fv[:, 0:128].rearrange("p (a b) -> p a b", a=2), axis=AX.X, op=ALU.add), reads=[ft], writes=[lam_tl])
        p.op("act", lambda e: e.activation(out=lamv[:, l, 1:3], in_=lamv[:, l, 1:3], func=AF.Exp), reads=[lam_tl], writes=[lam_tl])
        p.op("dve", lambda e: e.scalar_tensor_tensor(out=lamv[:, l, 0:1], in0=lamv[:, l, 2:3], scalar=-lam_init, in1=lamv[:, l, 1:2], op0=ALU.add, op1=ALU.subtract),
             reads=[lam_tl], writes=[lam_tl])

    def wload(name, l, k0, nk, c0, ncol):
        view, tl = wslots.get()
        v3 = view[:, 0:nk * ncol].rearrange("p (k c) -> p k c", k=nk)
        src = WB[name][l].rearrange("(k p) c -> p k c", p=128)[:, k0:k0 + nk, c0:c0 + ncol]
        p.dma("sp", f"ws{wslots.views.index(view)}", v3, src, reads=[WB_tl[(name, l)]], writes=[tl])
        return v3, tl

    fv, ft = FP.get()
    p.op("act", lambda e: e.activation(out=c_sb[:].rearrange("p k s -> p (k s)"), in_=c_sb[:].rearrange("p k s -> p (k s)"), func=AF.Silu),
         reads=[c_tl], writes=[c_tl])
    mod_done = set()

    def compute_mod(l):
        if l in mod_done:
            return
        p.tag = 'mod'
        mod_done.add(l)
        sm_save = p.small_mode
        p.small_mode = False
        for g in range(24):
            view, wt = wslots.get()
            wv = view[:].bitcast(F32)[:, 0:8 * 256].rearrange("p (k c) -> p k c", k=8)
            p.dma("sp", f"ws{wslots.views.index(view)}", wv, W["w_ada"][l].rearrange("(k p) c -> p k c", p=128)[:, :, g * 256:(g + 1) * 256], writes=[wt])
            pv, pt = PA.get()
            for cc in range(2):
                for k in range(8):
                    p.op("pe", lambda e: e.matmul(pv[:, cc * NSEQ:(cc + 1) * NSEQ], lhsT=wv[:, k, cc * 128:(cc + 1) * 128], rhs=c_sb[:, k, :],
                                                  start=(k == 0 and cc == 0), stop=(k == 7), skip_group_check=True),
                         reads=[wt, c_tl], writes=[pt])
            p.op("dve", lambda e: e.tensor_tensor(out=mod[:, l, g * 2:(g + 1) * 2, :], in0=pv[:, 0:2 * NSEQ].rearrange("p (c s) -> p c s", c=2),
                                                  in1=vec_sb[:, l, V_BADA + g * 2:V_BADA + (g + 1) * 2].unsqueeze(2).broadcast_to([128, 2, NSEQ]), op=ALU.add),
                 reads=[pt, vec_tl], writes=[modl_tl[l]])
        for (a, mul) in ((8, 1.0), (16, 1.0 / ALPHA), (32, 1.0), (40, 1.0 / ALPHA)):
            p.op("dve", lambda e: e.tensor_scalar(out=mod2[:, l, a:a + 8, :], in0=mod[:, l, a:a + 8, :], scalar1=1.0, scalar2=mul, op0=ALU.add, op1=ALU.mult),
                 reads=[modl_tl[l]], writes=[modl_tl[l]])
        p.small_mode = sm_save

    modl_tl = [TT(small=True) for _ in range(DEPTH)]
    compute_mod(0)

    def layer_norm_stats(src_chunks, src_tls, n, eps):
        mps, mt = PA.get()
        qps, qt = PA.get()
        for j in range(8):
            sqv, sqt = HP.get()
            xbv, xbt = HP.get()
            p.op("act", lambda e: e.activation(out=sqv[:, 0:n], in_=src_chunks[j], func=AF.Square), reads=[src_tls[j]], writes=[sqt])
            p.op("dve", lambda e: e.tensor_copy(out=xbv[:, 0:n], in_=src_chunks[j]), reads=[src_tls[j]], writes=[xbt])
            p.op("pe", lambda e: e.matmul(mps[:, 0:n], lhsT=onesb[:], rhs=xbv[:, 0:n], start=(j == 0), stop=(j == 7)), reads=[xbt, cb_tl], writes=[mt])
            p.op("pe", lambda e: e.matmul(qps[:, 0:n], lhsT=onesb[:], rhs=sqv[:, 0:n], start=(j == 0), stop=(j == 7)), reads=[sqt, cb_tl], writes=[qt])
        b1, b1t = FP.get()
        b2, b2t = FP.get()
        p.op("act", lambda e: e.activation(out=b2[:, 0:n], in_=mps[:, 0:n], func=AF.Square), reads=[mt], writes=[b2t])
        p.op("dve", lambda e: e.tensor_tensor(out=b1[:, 0:n], in0=qps[:, 0:n], in1=b2[:, 0:n], op=ALU.subtract), reads=[qt, b2t], writes=[b1t])
        p.op("dve", lambda e: e.tensor_scalar(out=b1[:, 0:n], in0=b1[:, 0:n], scalar1=0.0, scalar2=eps, op0=ALU.max, op1=ALU.add), reads=[b1t], writes=[b1t])
        p.op("act", lambda e: e.activation(out=b1[:, 0:n], in_=b1[:, 0:n], func=AF.Ln), reads=[b1t], writes=[b1t])
        p.op("act", lambda e: e.activation(out=b1[:, 0:n], in_=b1[:, 0:n], func=AF.Exp, scale=-0.5), reads=[b1t], writes=[b1t])
        p.op("dve", lambda e: e.scalar_tensor_tensor(out=mps[:, 0:n], in0=mps[:, 0:n], scalar=-1.0, in1=b1[:, 0:n], op0=ALU.mult, op1=ALU.mult),
             reads=[mt, b1t], writes=[mt])
        p.op("dve", lambda e: e.tensor_copy(out=qps[:, 0:n], in_=b1[:, 0:n]), reads=[b1t], writes=[qt])
        return (qps, qt), (mps, mt)

    def ln_apply(src, src_tl, dst, dst_tl, n, B1, B2, scale_ap, bias_ap, extra_reads):
        (b1, b1t), (b2, b2t) = B1, B2
        t1, t1t = FP.get()
        p.op("dve", lambda e: e.tensor_tensor(out=t1[:, 0:n], in0=b1[:, 0:n], in1=src, op=ALU.mult), reads=[src_tl, b1t], writes=[t1t])
        p.op("dve", lambda e: e.tensor_tensor(out=t1[:, 0:n], in0=b2[:, 0:n], in1=t1[:, 0:n], op=ALU.add), reads=[t1t, b2t], writes=[t1t])
        p.op("act", lambda e: e.activation(out=dst, in_=t1[:, 0:n], func=AF.Identity, scale=scale_ap, bias=bias_ap), reads=[t1t] + extra_reads, writes=[dst_tl])

    def run_sequence(si, is_sample):
        p.small_mode = is_sample
        Tq = TS if is_sample else T
        TTs = TS if is_sample else TTK
        SUB = TS if is_sample else 128
        NS = TTs // SUB
        ntile = 1 if is_sample else NTILE
        x_src = xsT if is_sample else xT[si]
        y_dst = ysT if is_sample else yT[si]

        for l in range(DEPTH):
            if is_sample:
                p.dma("act", f"g{l}", G[l][:], sG[l], writes=[G_tl[l]])
                p.dma("act", f"cr{l}", carry[l][:], sconv[l], writes=[carry_tl[l]])
                p.op("dve", lambda e: e.memset(gst[l][0:4, 0:1], 0.0), writes=[gst_tl[l]])
                p.dma("act", f"gs{l}", gst[l][0:4, 1:2], sm[l], writes=[gst_tl[l]])
                for hh in range(4):
                    for half in range(2):
                        fv_, ft_ = FP.get()
                        p.dma("act", f"fp{FP.views.index(fv_)}", fv_[:, 0:512], ckT[l, hh, :, half * 512:(half + 1) * 512], writes=[ft_])
                        p.op("pool", lambda e: e.tensor_copy(out=Kc[l][:, hh, half * 512:(half + 1) * 512], in_=fv_[:, 0:512]), reads=[ft_], writes=[Kc_tl[l][hh]])
                for b in range(8):
                    fv_, ft_ = FP.get()
                    p.dma("act", f"fp{FP.views.index(fv_)}", fv_[:, 0:512], cvv[l, b * 128:(b + 1) * 128, :], writes=[ft_])
                    p.op("pool", lambda e: e.memset(Vc[l][:, b, :], 1.0), writes=[Vc_tl[l][b]])
                    p.op("pool", lambda e: e.tensor_copy(out=Vc[l][:, b, :].rearrange("p (h d) -> p h d", h=4)[:, :, 0:128],
                                                         in_=fv_[:, 0:512].rearrange("p (h d) -> p h d", h=4)), reads=[ft_], writes=[Vc_tl[l][b]])
                p.op("pool", lambda e: e.memset(Vc[l][:, 8, :], 1.0), writes=[Vc_tl[l][8]])
            else:
                p.op("pool", lambda e: e.memset(G[l][:], 0.0), writes=[G_tl[l]])
                p.op("pool", lambda e: e.memset(carry[l][:], 0.0), writes=[carry_tl[l]])
                p.op("dve", lambda e: e.memset(gst[l][0:4, 0:2], 0.0), writes=[gst_tl[l]])
                for b in range(T // 128):
                    p.op("pool", lambda e: e.memset(Vc[l][:, b, :], 1.0), writes=[Vc_tl[l][b]])

        for it in range(ntile):
            tok0 = it * TTs
            p.dma("act", "xin", x_t[:, :, 0:TTs], x_src.rearrange("(k p) t -> p k t", p=128)[:, :, tok0:tok0 + TTs], writes=x_tl)
            for l in range(DEPTH):
                run_tile_layer(si, is_sample, l, it, tok0, TTs, SUB, NS)
            p.dma("act", "yout", y_dst.rearrange("(k p) t -> p k t", p=128)[:, :, tok0:tok0 + TTs], x_t[:, :, 0:TTs], reads=x_tl)

        for l in range(DEPTH):
            p.op("dve", lambda e: e.tensor_tensor(out=gst[l][0:4, 2:3], in0=gst[l][0:4, 1:2], in1=gst[l][0:4, 0:1], op=ALU.subtract), reads=[gst_tl[l]], writes=[gst_tl[l]])
            if is_sample:
                p.dma("act", f"g{l}", oGs[l], G[l][:], reads=[G_tl[l]])
                p.dma("act", f"cr{l}", oconvs[l], carry[l][:], reads=[carry_tl[l]])
                p.dma("act", f"gs{l}", oms[l], gst[l][0:4, 2:3], reads=[gst_tl[l]])
            else:
                p.dma("act", f"g{l}", oG[l, si], G[l][:], reads=[G_tl[l]])
                p.dma("act", f"cr{l}", oconv[l, si], carry[l][:], reads=[carry_tl[l]])
                p.dma("act", f"gs{l}", om[l, si], gst[l][0:4, 2:3], reads=[gst_tl[l]])

    def run_tile_layer(si, is_sample, l, it, tok0, n, SUB, NS):
        compute_mod(l)
        mod_tl = modl_tl[l]
        lam_init = 0.8 - 0.6 * math.exp(-0.3 * l)
        mcol = lambda piece, j: mod[:, l, piece * 8 + j, si:si + 1]
        m2col = lambda piece, j: mod2[:, l, piece * 8 + j, si:si + 1]
        vcol = lambda off: vec_sb[:, l, off:off + 1]
        kbase = PAST if is_sample else 0
        xs = [x_t[:, j, 0:n] for j in range(8)]

        p.tag = 'a-ln'
        B1, B2 = layer_norm_stats(xs, x_tl, n, LN_EPS)
        for j in range(8):
            ln_apply(xs[j], x_tl[j], h_t[:, j, 0:n], h_tl[j], n, B1, B2, m2col(1, j), mcol(0, j), [mod_tl])

        def proj_fm(wv, wt, ci, act_chunks, act_tls, nk, k0=0, first=True, last=True, pv=None, pt=None):
            if pv is None:
                pv, pt = PA.get()
            for k in range(nk):
                p.op("pe", lambda e: e.matmul(pv[:, 0:n], lhsT=wv[:, k, ci * 128:(ci + 1) * 128], rhs=act_chunks[k0 + k],
                                              start=(first and k == 0), stop=(last and k == nk - 1)),
                     reads=[wt, act_tls[k0 + k]], writes=[pt])
            return pv, pt

        dbg('B1', B1[0][:, 0:n], [B1[1]])
        dbg('B2', B2[0][:, 0:n], [B2[1]])
        dbg('h', h_t[:, :, 0:n], h_tl)
        _chk('a')
        hs = [h_t[:, k, 0:n] for k in range(8)]

        p.tag = 'b-aq'
        wv, wt = wload("w_in", l, 0, 8, 0, 512)
        aqb = [PA.get() for _ in range(4)]
        for k in range(8):
            for hh in range(4):
                p.op("pe", lambda e: e.matmul(aqb[hh][0][:, 0:n], lhsT=wv[:, k, hh * 128:(hh + 1) * 128], rhs=hs[k], start=(k == 0), stop=(k == 7)),
                     reads=[wt, h_tl[k]], writes=[aqb[hh][1]])
        for hh in range(4):
            pv, pt = aqb[hh]
            p.op("pool", lambda e: e.memset(U[64:128, hh, 0:n], 0.0), writes=[U_tl[hh]])
            p.op("pool", lambda e: e.memset(U[0:64, 24 + hh, 0:n], 0.0), writes=[U_tl[24 + hh]])
            p.op("act", lambda e: e.activation(out=U[0:64, hh, 0:n], in_=pv[0:64, 0:n], func=AF.Copy), reads=[pt], writes=[U_tl[hh]])
            p.op("act", lambda e: e.activation(out=U[64:128, 24 + hh, 0:n], in_=pv[64:128, 0:n], func=AF.Copy), reads=[pt], writes=[U_tl[24 + hh]])
        p.tag = 'b-ak'
        wv, wt = wload("w_in", l, 0, 8, 512, 512)
        for hh in range(4):
            pv, pt = proj_fm(wv, wt, hh, hs, h_tl, 8)
            fv, ft = FP.get()
            p.op("act", lambda e: e.activation(out=fv[:, 0:n], in_=pv[:, 0:n], func=AF.Copy), reads=[pt], writes=[ft])
            p.op("dve", lambda e: e.tensor_copy(out=Kc[l][:, hh, kbase + tok0:kbase + tok0 + n], in_=pv[:, 0:n]), reads=[pt], writes=[Kc_tl[l][hh]])
            dst = oksT[l, hh * 128:(hh + 1) * 128, :] if is_sample else okT[l, si, hh * 128:(hh + 1) * 128, tok0:tok0 + n]
            p.dma("act", f"fp{FP.views.index(fv)}", dst, fv[:, 0:n], reads=[ft])
        p.tag = 'b-av'
        wv, wt = wload("w_in", l, 0, 8, 1024, 512)
        for r in range(NS):
            pv, pt = PA.get()
            for k in range(8):
                p.op("pe", lambda e: e.matmul(pv[0:SUB, :], lhsT=h_t[:, k, r * SUB:(r + 1) * SUB], rhs=wv[:, k, :], start=(k == 0), stop=(k == 7)),
                     reads=[wt, h_tl[k]], writes=[pt])
            fv, ft = FP.get()
            blk = (kbase + tok0) // 128 + r
            p.op("act", lambda e: e.activation(out=fv[0:SUB, 0:512], in_=pv[0:SUB, :], func=AF.Copy), reads=[pt], writes=[ft])
            p.op("dve", lambda e: e.tensor_copy(out=Vc[l][0:SUB, blk, :].rearrange("p (h d) -> p h d", h=4)[:, :, 0:128],
                                                in_=pv[0:SUB, :].rearrange("p (h d) -> p h d", h=4)), reads=[pt], writes=[Vc_tl[l][blk]])
            dst = ovs[l] if is_sample else ov[l, si, tok0 + r * SUB:tok0 + (r + 1) * SUB, :]
            p.dma("act", f"fp{FP.views.index(fv)}", dst, fv[0:SUB, 0:512], reads=[ft])
        p.tag = 'b-mqk'
        for half in range(2):
            wv, wt = wload("w_in", l, 0, 8, 1536 + half * 512, 512)
            for cc in range(4):
                c = half * 4 + cc
                pv, pt = proj_fm(wv, wt, cc, hs, h_tl, 8)
                prv, prt = pre.get()
                p.op("pool", lambda e: e.tensor_copy(out=prv[:, 0:3], in_=carry[l][:, c * 3:c * 3 + 3]), reads=[carry_tl[l]], writes=[prt])
                p.op("act", lambda e: e.activation(out=prv[:, 3:3 + n], in_=pv[:, 0:n], func=AF.Copy), reads=[pt], writes=[prt])
                p.op("pool", lambda e: e.tensor_copy(out=carry[l][:, c * 3:c * 3 + 3], in_=prv[:, n:n + 3]), reads=[prt], writes=[carry_tl[l]])
                yv, yt = FP.get()
                p.op("dve", lambda e: e.tensor_scalar(out=yv[:, 0:n], in0=prv[:, 0:n], scalar1=vcol(V_CONVW + 0 * 8 + c), scalar2=vcol(V_CONVB + c), op0=ALU.mult, op1=ALU.add),
                     reads=[prt, vec_tl], writes=[yt])
                for tap in range(1, 4):
                    p.op("dve", lambda e: e.scalar_tensor_tensor(out=yv[:, 0:n], in0=prv[:, tap:tap + n], scalar=vcol(V_CONVW + tap * 8 + c), in1=yv[:, 0:n], op0=ALU.mult, op1=ALU.add),
                         reads=[prt, vec_tl, yt], writes=[yt])
                p.op("act", lambda e: e.activation(out=U[:, 4 + c, 0:n], in_=yv[:, 0:n], func=AF.Silu), reads=[yt], writes=[U_tl[4 + c]])
        p.tag = 'b-mv'
        wv, wt = wload("w_in", l, 0, 8, 2560, 512)
        for r in range(NS):
            pv, pt = PA.get()
            for k in range(8):
                p.op("pe", lambda e: e.matmul(pv[0:SUB, :], lhsT=h_t[:, k, r * SUB:(r + 1) * SUB], rhs=wv[:, k, :], start=(k == 0), stop=(k == 7)),
                     reads=[wt, h_tl[k]], writes=[pt])
            p.op("pool", lambda e: e.memset(mv_t[:, r, :], 1.0), writes=[mv_tl[r]])
            p.op("dve", lambda e: e.tensor_copy(out=mv_t[0:SUB, r, :].rearrange("p (h d) -> p h d", h=4)[:, :, 0:128],
                                                in_=pv[0:SUB, :].rearrange("p (h d) -> p h d", h=4)), reads=[pt], writes=[mv_tl[r]])
        p.tag = 'b-if'
        wv, wt = wload("w_in", l, 0, 8, 3072, 8)
        pvi, pti = PA.get()
        pvf, ptf = PA.get()
        for k in range(8):
            p.op("pe", lambda e: e.matmul(pvi[0:4, 0:n], lhsT=wv[:, k, 0:4], rhs=hs[k], start=(k == 0), stop=(k == 7)), reads=[wt, h_tl[k]], writes=[pti])
        for k in range(8):
            p.op("pe", lambda e: e.matmul(pvf[0:4, 0:n], lhsT=wv[:, k, 4:8], rhs=hs[k], start=(k == 0), stop=(k == 7)), reads=[wt, h_tl[k]], writes=[ptf])
        p.op("act", lambda e: e.activation(out=g_u[0:4, 0:n], in_=pvi[0:4, 0:n], func=AF.Identity, bias=bif_sb[:, l, 0:1]), reads=[pti, bif_tl], writes=[g_tl])
        p.op("act", lambda e: e.activation(out=g_lf[0:4, 0:n], in_=pvf[0:4, 0:n], func=AF.Exp, scale=-1.0, bias=bif_sb[:, l, 2:3]), reads=[ptf, bif_tl], writes=[g_tl])
        p.op("act", lambda e: e.activation(out=g_lf[0:4, 0:n], in_=g_lf[0:4, 0:n], func=AF.Ln, bias=1.0), reads=[g_tl], writes=[g_tl])
        p.tag = 'b-mo'
        wv, wt = wload("w_in", l, 0, 8, 3080, 512)
        for hh in range(4):
            pv, pt = proj_fm(wv, wt, hh, hs, h_tl, 8)
            p.op("act", lambda e: e.activation(out=U[:, 12 + hh, 0:n], in_=pv[:, 0:n], func=AF.Sigmoid), reads=[pt], writes=[U_tl[12 + hh]])

        _chk('b')
        p.tag = 'c'
        gs_ = gsm[0:4, :]
        p.op("dve", lambda e: e.tensor_tensor_scan(out=g_nb[0:4, 0:n], data0=g_lf[0:4, 0:n], data1=g_lf[0:4, 0:n], initial=gst[l][0:4, 0:1], op0=ALU.add, op1=ALU.max),
             reads=[g_tl, gst_tl[l]], writes=[g_tl])
        p.op("dve", lambda e: e.tensor_tensor(out=g_u[0:4, 0:n], in0=g_u[0:4, 0:n], in1=g_nb[0:4, 0:n], op=ALU.add), reads=[g_tl], writes=[g_tl])
        p.op("dve", lambda e: e.tensor_reduce(out=gs_[:, 0:NS], in_=g_u[0:4, 0:n].rearrange("p (r s) -> p r s", r=NS), axis=AX.X, op=ALU.max), reads=[g_tl], writes=[gsm_tl])
        p.op("dve", lambda e: e.tensor_tensor_scan(out=gs_[:, 8:8 + NS], data0=gs_[:, 0:NS], data1=gs_[:, 0:NS], initial=gst[l][0:4, 1:2], op0=ALU.max, op1=ALU.max),
             reads=[gsm_tl, gst_tl[l]], writes=[gsm_tl])
        p.op("dve", lambda e: e.tensor_copy(out=gs_[:, 16:17], in_=gst[l][0:4, 1:2]), reads=[gst_tl[l]], writes=[gsm_tl])
        if NS > 1:
            p.op("dve", lambda e: e.tensor_copy(out=gs_[:, 17:16 + NS], in_=gs_[:, 8:8 + NS - 1]), reads=[gsm_tl], writes=[gsm_tl])
        p.op("dve", lambda e: e.tensor_tensor(out=gs_[:, 24:24 + NS], in0=gs_[:, 16:16 + NS], in1=gs_[:, 8:8 + NS], op=ALU.subtract), reads=[gsm_tl], writes=[gsm_tl])
        p.op("act", lambda e: e.activation(out=gs_[:, 24:24 + NS], in_=gs_[:, 24:24 + NS], func=AF.Exp), reads=[gsm_tl], writes=[gsm_tl])
        p.op("dve", lambda e: e.tensor_scalar(out=gs_[:, 32:32 + NS], in0=gs_[:, 8:8 + NS], scalar1=-1.0, scalar2=None, op0=ALU.mult), reads=[gsm_tl], writes=[gsm_tl])
        p.op("dve", lambda e: e.tensor_scalar(out=gs_[:, 40:40 + NS], in0=gs_[:, 8:8 + NS], scalar1=-1.0, scalar2=math.log(KSCALE), op0=ALU.mult, op1=ALU.add), reads=[gsm_tl], writes=[gsm_tl])
        for r in range(NS):
            p.op("act", lambda e: e.activation(out=g_ek[0:4, r * SUB:(r + 1) * SUB], in_=g_u[0:4, r * SUB:(r + 1) * SUB], func=AF.Exp, bias=gs_[:, 40 + r:41 + r]),
                 reads=[g_tl, gsm_tl], writes=[g2_tl])
            p.op("act", lambda e: e.activation(out=g_cl[0:4, r * SUB:(r + 1) * SUB], in_=g_nb[0:4, r * SUB:(r + 1) * SUB], func=AF.Exp, bias=gs_[:, 32 + r:33 + r]),
                 reads=[g_tl, gsm_tl], writes=[g2_tl])
        p.op("dve", lambda e: e.tensor_copy(out=gst[l][0:4, 0:1], in_=g_nb[0:4, n - 1:n]), reads=[g_tl], writes=[gst_tl[l]])
        p.op("dve", lambda e: e.tensor_copy(out=gst[l][0:4, 1:2], in_=gs_[:, 8 + NS - 1:8 + NS]), reads=[gsm_tl], writes=[gst_tl[l]])
        for r in range(NS):
            pv, pt = PA.get()
            p.op("pe", lambda e: e.transpose(out=pv[0:SUB, 0:4], in_=g_ek[0:4, r * SUB:(r + 1) * SUB], identity=ident_f[0:4, 0:4]), reads=[g2_tl, cst_tl], writes=[pt])
            p.op("pe", lambda e: e.transpose(out=pv[0:SUB, 4:8], in_=g_cl[0:4, r * SUB:(r + 1) * SUB], identity=ident_f[0:4, 0:4]), reads=[g2_tl, cst_tl], writes=[pt])
            p.op("dve", lambda e: e.tensor_copy(out=ekcl[0:SUB, r, :], in_=pv[0:SUB, 0:8]), reads=[pt], writes=[ekcl_tl[r]])
        p.op("dve", lambda e: e.tensor_tensor(out=gs_[:, 48:48 + NS * 4].rearrange("p (r h) -> p r h", h=4), in0=gs_[:, 24:24 + NS].unsqueeze(2).broadcast_to([4, NS, 4]),
                                              in1=d4[:, 0:NS * 4].rearrange("p (r h) -> p r h", h=4), op=ALU.mult), reads=[gsm_tl, cb_tl], writes=[gsm_tl])
        pv, pt = PA.get()
        p.op("pe", lambda e: e.matmul(pv[:, 0:NS * 4], lhsT=ones4[:, :], rhs=gs_[:, 48:48 + NS * 4], start=True, stop=True), reads=[gsm_tl, cb_tl], writes=[pt])
        p.op("dve", lambda e: e.tensor_copy(out=wcb[:, 0:NS * 4], in_=pv[:, 0:NS * 4]), reads=[pt], writes=[wcb_tl])

        dbg('gu', g_u[0:4, 0:n], [g_tl])
        dbg('gnb', g_nb[0:4, 0:n], [g_tl])
        dbg('gek', g_ek[0:4, 0:n], [g2_tl])
        dbg('gcl', g_cl[0:4, 0:n], [g2_tl])
        dbg('gsm', gsm[0:4, :], [gsm_tl])
        dbg('wcb', wcb[:, :], [wcb_tl])
        dbg('ekcl', ekcl[:, :, :], ekcl_tl)
        dbg('mq', U[:, 4:8, 0:n], U_tl[4:8])
        dbg('mk', U[:, 8:12, 0:n], U_tl[8:12])
        dbg('mv', mv_t[:, :, :], mv_tl)
        _chk('c')
        deferred = []
        for r in range(NS):
            q0 = r * SUB
            qs = slice(q0, q0 + SUB)
            p.tag = 'd-attn'
            if is_sample:
                blocks = [(j, 8 - j, 128, j * 128, False) for j in range(8)] + [(8, 0, TS, PAST, True)]
            else:
                qi = (tok0 + q0) // 128
                blocks = [(j, qi - j, 128, j * 128, False) for j in range(qi)] + [(qi, 0, 128, qi * 128, True)]
            aov, aot = AO.get()
            def live(hh, bi):
                dl = blocks[bi][1]
                return dl == 0 or SLOPES[hh] * (128.0 * dl - 127.0) <= 80.0
            units = [(hh, bi) for hh in range(4) for bi in range(len(blocks)) if live(hh, bi)]
            first_bi = {hh: min(bi for (h2, bi) in units if h2 == hh) for hh in range(4)}
            nb = len(blocks)
            Ocur = {}

            def emit_qk(hh, bi):
                (vb, dl, kb, kc0, diag) = blocks[bi]
                sv, st = PB.get()
                p.op("pe", lambda e: e.matmul(sv[0:kb, 0:2 * SUB].rearrange("p (c s) -> p c s", c=2), lhsT=Kc[l][:, hh, kc0:kc0 + kb],
                                              rhs=U[:, hh:hh + 25:24, qs], start=True, stop=True),
                     reads=[Kc_tl[l][hh], U_tl[hh], U_tl[24 + hh]], writes=[st])
                ptv, ptt = HP.get()
                if not diag:
                    p.op("act", lambda e: e.activation(out=ptv[0:kb, 0:2 * SUB], in_=sv[0:kb, 0:2 * SUB], func=AF.Exp, scale=0.125, bias=biasT[0:kb, hh * 16 + dl:hh * 16 + dl + 1]),
                         reads=[st, cst_tl], writes=[ptt])
                else:
                    tv, tt_ = FP.get()
                    for c in range(2):
                        p.op("dve", lambda e: e.scalar_tensor_tensor(out=tv[0:kb, c * SUB:(c + 1) * SUB], in0=sv[0:kb, c * SUB:(c + 1) * SUB], scalar=0.125,
                                                                     in1=Dtab[0:kb, hh * 128:hh * 128 + SUB], op0=ALU.mult, op1=ALU.add),
                             reads=[st, cst_tl], writes=[tt_])
                    p.op("act", lambda e: e.activation(out=ptv[0:kb, 0:2 * SUB], in_=tv[0:kb, 0:2 * SUB], func=AF.Exp), reads=[tt_], writes=[ptt])
                return ptv, ptt

            def emit_pv(hh, bi, ptv, ptt):
                (vb, dl, kb, kc0, diag) = blocks[bi]
                if bi == first_bi[hh]:
                    Ocur[hh] = PC.get()
                Ov, Ot = Ocur[hh]
                for c in range(2):
                    p.op("pe", lambda e: e.matmul(Ov[0:SUB, c * 130:(c + 1) * 130], lhsT=ptv[0:kb, c * SUB:(c + 1) * SUB], rhs=Vc[l][0:kb, vb, hh * 130:(hh + 1) * 130],
                                                  start=(bi == first_bi[hh] and c == 0), stop=(bi == nb - 1), skip_group_check=True),
                         reads=[ptt, Vc_tl[l][vb]], writes=[Ot])
                if bi != nb - 1:
                    return
                rcv, rct = smalls.get()
                p.op("dve", lambda e: e.reciprocal(out=rcv[0:SUB, 0:1], in_=Ov[0:SUB, 128:129]), reads=[Ot], writes=[rct])
                p.op("dve", lambda e: e.reciprocal(out=rcv[0:SUB, 1:2], in_=Ov[0:SUB, 258:259]), reads=[Ot], writes=[rct])
                p.op("dve", lambda e: e.tensor_scalar(out=rcv[0:SUB, 1:2], in0=rcv[0:SUB, 1:2], scalar1=lamv[0:SUB, l, 0:1], scalar2=None, op0=ALU.mult), reads=[rct, lam_tl], writes=[rct])
                tv, tt_ = FP.get()
                p.op("dve", lambda e: e.tensor_scalar(out=tv[0:SUB, 0:128], in0=Ov[0:SUB, 130:258], scalar1=rcv[0:SUB, 1:2], scalar2=None, op0=ALU.mult), reads=[Ot, rct], writes=[tt_])
                p.op("dve", lambda e: e.scalar_tensor_tensor(out=aov[0:SUB, hh * 128:(hh + 1) * 128], in0=Ov[0:SUB, 0:128], scalar=rcv[0:SUB, 0:1], in1=tv[0:SUB, 0:128],
                                                             op0=ALU.mult, op1=ALU.add), reads=[Ot, rct, tt_], writes=[aot])

            pend = []
            for (hh, bi) in units:
                pt_ = emit_qk(hh, bi)
                pend.append((hh, bi) + pt_)
                if len(pend) > 5:
                    emit_pv(*pend.pop(0))
            while pend:
                emit_pv(*pend.pop(0))
            for fn_ in deferred:
                fn_()
            deferred.clear()
            dbg('ao', aov[0:SUB, :], [aot])
            ssv, sst = smalls.get()
            jv, jt = FP.get()
            p.op("act", lambda e: e.activation(out=jv[0:SUB, 0:512], in_=aov[0:SUB, 0:512], func=AF.Square), reads=[aot], writes=[jt])
            p.op("dve", lambda e: e.tensor_reduce(out=ssv[0:SUB, 0:4], in_=jv[0:SUB, 0:512].rearrange("p (h d) -> p h d", h=4), axis=AX.X, op=ALU.add), reads=[jt], writes=[sst])
            p.op("dve", lambda e: e.tensor_scalar(out=ssv[0:SUB, 0:4], in0=ssv[0:SUB, 0:4], scalar1=1.0 / 128.0, scalar2=LN_EPS, op0=ALU.mult, op1=ALU.add), reads=[sst], writes=[sst])
            p.op("act", lambda e: e.activation(out=ssv[0:SUB, 0:4], in_=ssv[0:SUB, 0:4], func=AF.Ln), reads=[sst], writes=[sst])
            p.op("act", lambda e: e.activation(out=ssv[0:SUB, 0:4], in_=ssv[0:SUB, 0:4], func=AF.Exp, scale=-0.5), reads=[sst], writes=[sst])
            _chk('d0e')
            anv, ant = ANP.get()
            for hh in range(4):
                p.op("act", lambda e: e.activation(out=anv[0:SUB, hh * 128:(hh + 1) * 128], in_=aov[0:SUB, hh * 128:(hh + 1) * 128], func=AF.Identity, scale=ssv[0:SUB, hh:hh + 1]),
                     reads=[aot, sst], writes=[ant])

            def an_transposes(anv=anv, ant=ant, qs=qs):
                pv, pt = PA.get()
                pvb = pv[:].bitcast(BF16)
                for hh in range(4):
                    p.op("pe", lambda e: e.transpose(out=pvb[:, hh * SUB:(hh + 1) * SUB], in_=anv[0:SUB, hh * 128:(hh + 1) * 128], identity=identb[0:SUB, 0:SUB]), reads=[ant, cb_tl], writes=[pt])
                for hh in range(4):
                    p.op("act", lambda e: e.activation(out=U[:, 16 + hh, qs], in_=pvb[:, hh * SUB:(hh + 1) * SUB], func=AF.Identity, scale=vcol(V_DAN + hh)), reads=[pt, vec_tl], writes=[U_tl[16 + hh]])

            _chk('d1')
            p.tag = 'd-mlstm'
            psS, psSt = PA.get()
            for hh in range(4):
                p.op("pe", lambda e: e.matmul(psS[0:SUB, hh * SUB:(hh + 1) * SUB], lhsT=U[:, 8 + hh, qs], rhs=U[:, 4 + hh, qs], start=True, stop=True),
                     reads=[U_tl[8 + hh], U_tl[4 + hh]], writes=[psSt])
            smv, smt = HP.get()
            for hh in range(4):
                p.op("dve", lambda e: e.scalar_tensor_tensor(out=smv[0:SUB, hh * SUB:(hh + 1) * SUB], in0=psS[0:SUB, hh * SUB:(hh + 1) * SUB], scalar=ekcl[0:SUB, r, hh:hh + 1],
                                                             in1=maskb[0:SUB, 0:SUB], op0=ALU.mult, op1=ALU.mult), reads=[psSt, ekcl_tl[r], cb_tl], writes=[smt])
            psK, psKt = PA.get()
            psKb = psK[:].bitcast(BF16)
            for hh in range(4):
                p.op("pe", lambda e: e.transpose(out=psKb[0:SUB, hh * 128:(hh + 1) * 128], in_=U[:, 8 + hh, qs], identity=identb[:, :]), reads=[U_tl[8 + hh], cb_tl], writes=[psKt])
            khv, kht = HP.get()
            for hh in range(4):
                p.op("act", lambda e: e.activation(out=khv[0:SUB, hh * 128:(hh + 1) * 128], in_=psKb[0:SUB, hh * 128:(hh + 1) * 128], func=AF.Identity, scale=ekcl[0:SUB, r, hh:hh + 1]),
                     reads=[psKt, ekcl_tl[r]], writes=[kht])
            gbv, gbt = HP.get()
            for hh in range(4):
                p.op("dve", lambda e: e.tensor_scalar(out=gbv[:, hh * 130:(hh + 1) * 130], in0=G[l][:, hh * 130:(hh + 1) * 130], scalar1=wcb[:, r * 4 + hh:r * 4 + hh + 1], scalar2=None, op0=ALU.mult),
                     reads=[G_tl[l], wcb_tl], writes=[gbt])
            Hps = [PA.get(), PA.get()]
            for hh in range(4):
                hv, ht = Hps[hh // 2]
                o0 = (hh % 2) * 130
                p.op("pe", lambda e: e.matmul(hv[0:SUB, o0:o0 + 130], lhsT=U[:, 4 + hh, qs], rhs=gbv[:, hh * 130:(hh + 1) * 130], start=(hh % 2 == 0), stop=False, skip_group_check=True),
                     reads=[U_tl[4 + hh], gbt], writes=[ht])
                p.op("pe", lambda e: e.matmul(hv[0:SUB, o0:o0 + 130], lhsT=smv[0:SUB, hh * SUB:(hh + 1) * SUB], rhs=mv_t[0:SUB, r, hh * 130:(hh + 1) * 130], start=False, stop=True, skip_group_check=True),
                     reads=[smt, mv_tl[r]], writes=[ht])
            ddv, ddt = smalls.get()
            hmv, hmt = FP.get()
            for hp_ in range(2):
                hv, ht = Hps[hp_]
                den = hv[0:SUB, 0:260].rearrange("p (c d) -> p c d", c=2)[:, :, 128]
                p.op("dve", lambda e: e.tensor_scalar(out=ddv[0:SUB, 8 + hp_ * 2:10 + hp_ * 2], in0=den, scalar1=-1.0, scalar2=None, op0=ALU.mult), reads=[ht], writes=[ddt])
                p.op("dve", lambda e: e.tensor_tensor(out=ddv[0:SUB, 8 + hp_ * 2:10 + hp_ * 2], in0=den, in1=ddv[0:SUB, 8 + hp_ * 2:10 + hp_ * 2], op=ALU.max), reads=[ht, ddt], writes=[ddt])
                p.op("dve", lambda e: e.tensor_tensor(out=ddv[0:SUB, hp_ * 2:hp_ * 2 + 2], in0=ddv[0:SUB, 8 + hp_ * 2:10 + hp_ * 2], in1=ekcl[0:SUB, r, 4 + hp_ * 2:6 + hp_ * 2], op=ALU.max), reads=[ddt, ekcl_tl[r]], writes=[ddt])
            p.op("dve", lambda e: e.reciprocal(out=ddv[0:SUB, 0:4], in_=ddv[0:SUB, 0:4]), reads=[ddt], writes=[ddt])
            for hh in range(4):
                hv, ht = Hps[hh // 2]
                o0 = (hh % 2) * 130
                p.op("act", lambda e: e.activation(out=hmv[0:SUB, hh * 128:(hh + 1) * 128], in_=hv[0:SUB, o0:o0 + 128], func=AF.Identity, scale=ddv[0:SUB, hh:hh + 1]), reads=[ht, ddt], writes=[hmt])
            dbg('hm', hmv[0:SUB, 0:512], [hmt])
            dbg('ddv', ddv[0:SUB, 0:16], [ddt])
            dbg('smv', smv[0:SUB, 0:512], [smt])
            dbg('khv', khv[0:SUB, 0:512], [kht])
            dbg('gbv', gbv[:, 0:520], [gbt])
            dGs = [PA.get(), PA.get()]
            for hh in range(4):
                gv, gt_ = dGs[hh // 2]
                o0 = (hh % 2) * 130
                p.op("pe", lambda e: e.matmul(gv[:, o0:o0 + 130], lhsT=khv[0:SUB, hh * 128:(hh + 1) * 128], rhs=mv_t[0:SUB, r, hh * 130:(hh + 1) * 130], start=(hh % 2 == 0), stop=True, skip_group_check=True),
                     reads=[kht, mv_tl[r]], writes=[gt_])
            an_transposes()
            for hh in range(4):
                gv, gt_ = dGs[hh // 2]
                o0 = (hh % 2) * 130
                p.op("dve", lambda e: e.scalar_tensor_tensor(out=G[l][:, hh * 130:(hh + 1) * 130], in0=G[l][:, hh * 130:(hh + 1) * 130], scalar=wcb[:, r * 4 + hh:r * 4 + hh + 1],
                                                             in1=gv[:, o0:o0 + 130], op0=ALU.mult, op1=ALU.add), reads=[gt_, G_tl[l], wcb_tl], writes=[G_tl[l]])
            stv, stt = smalls.get()
            mvv, mvt = smalls.get()
            for hh in range(4):
                p.op("dve", lambda e: e.bn_stats(out=stv[0:SUB, hh * 6:(hh + 1) * 6], in_=hmv[0:SUB, hh * 128:(hh + 1) * 128]), reads=[hmt], writes=[stt])
                p.op("dve", lambda e: e.bn_aggr(out=mvv[0:SUB, hh * 2:(hh + 1) * 2], in_=stv[0:SUB, hh * 6:(hh + 1) * 6]), reads=[stt], writes=[mvt])
            p.op("dve", lambda e: e.tensor_scalar(out=mvv[0:SUB, 8:12], in0=mvv[0:SUB, 0:8].rearrange("p (h t) -> p h t", t=2)[:, :, 1], scalar1=LN_EPS, scalar2=None, op0=ALU.add), reads=[mvt], writes=[mvt])
            p.op("act", lambda e: e.activation(out=mvv[0:SUB, 8:12], in_=mvv[0:SUB, 8:12], func=AF.Ln), reads=[mvt], writes=[mvt])
            p.op("act", lambda e: e.activation(out=mvv[0:SUB, 8:12], in_=mvv[0:SUB, 8:12], func=AF.Exp, scale=-0.5), reads=[mvt], writes=[mvt])
            p.op("dve", lambda e: e.scalar_tensor_tensor(out=mvv[0:SUB, 12:16], in0=mvv[0:SUB, 0:8].rearrange("p (h t) -> p h t", t=2)[:, :, 0], scalar=-1.0, in1=mvv[0:SUB, 8:12], op0=ALU.mult, op1=ALU.mult),
                 reads=[mvt], writes=[mvt])
            mnv, mnt = MNP.get()
            for hh in range(4):
                p.op("act", lambda e: e.activation(out=mnv[0:SUB, hh * 128:(hh + 1) * 128], in_=hmv[0:SUB, hh * 128:(hh + 1) * 128], func=AF.Identity, scale=mvv[0:SUB, 8 + hh:9 + hh], bias=mvv[0:SUB, 12 + hh:13 + hh]),
                     reads=[hmt, mvt], writes=[mnt])

            def mn_transposes(mnv=mnv, mnt=mnt, qs=qs):
                pv, pt = PA.get()
                pvb = pv[:].bitcast(BF16)
                for hh in range(4):
                    p.op("pe", lambda e: e.transpose(out=pvb[:, hh * SUB:(hh + 1) * SUB], in_=mnv[0:SUB, hh * 128:(hh + 1) * 128], identity=identb[0:SUB, 0:SUB]), reads=[mnt, cb_tl], writes=[pt])
                for hh in range(4):
                    p.op("dve", lambda e: e.scalar_tensor_tensor(out=U[:, 20 + hh, qs], in0=pvb[:, hh * SUB:(hh + 1) * SUB], scalar=vcol(V_MN + hh), in1=U[:, 12 + hh, qs], op0=ALU.mult, op1=ALU.mult),
                         reads=[pt, vec_tl, U_tl[12 + hh]], writes=[U_tl[20 + hh]])
            deferred.append(mn_transposes)
        for fn_ in deferred:
            fn_()
        deferred.clear()

        dbg('anT', U[:, 16:20, 0:n], U_tl[16:20])
        dbg('mnT', U[:, 20:24, 0:n], U_tl[20:24])
        dbg('G', G[l][:, :], [G_tl[l]])
        _chk('d')
        p.tag = 'e-gate'
        for gi, c0 in enumerate((0, 512, 1024, 1536)):
            wv, wt = wload("w_gate", l, 0, 8, c0, 512)
            for cc in range(4):
                gj = gi * 4 + cc
                pv, pt = proj_fm(wv, wt, cc, hs, h_tl, 8)
                p.op("act", lambda e: e.activation(out=U[:, gj, 0:n], in_=pv[:, 0:n], func=AF.Sigmoid, bias=vcol(V_BGATE + gj)), reads=[pt, vec_tl], writes=[U_tl[gj]])
        p.tag = 'e-merge'
        wa, wat = wload("w_br_a", l, 0, 4, 0, 1024)
        wb, wbt = wload("w_br_b", l, 0, 4, 0, 1024)
        an_ch = [U[:, 16 + k, 0:n] for k in range(4)]
        mn_ch = [U[:, 20 + k, 0:n] for k in range(4)]
        for j in range(8):
            pva, pta = proj_fm(wa, wat, j, an_ch, U_tl[16:20], 4)
            pvb_, ptb = proj_fm(wb, wbt, j, mn_ch, U_tl[20:24], 4)
            t1, t1t = FP.get()
            t2, t2t = FP.get()
            p.op("dve", lambda e: e.tensor_tensor(out=t1[:, 0:n], in0=pva[:, 0:n], in1=U[:, j, 0:n], op=ALU.mult), reads=[pta, U_tl[j]], writes=[t1t])
            p.op("dve", lambda e: e.tensor_tensor(out=t2[:, 0:n], in0=pvb_[:, 0:n], in1=U[:, 8 + j, 0:n], op=ALU.mult), reads=[ptb, U_tl[8 + j]], writes=[t2t])
            p.op("pool", lambda e: e.tensor_tensor(out=mixin[:, j, 0:n], in0=t1[:, 0:n], in1=t2[:, 0:n], op=ALU.add), reads=[t1t, t2t], writes=[mix_tl[j]])

        dbg('mixin', mixin[:, :, 0:n], mix_tl)
        _chk('e')
        p.tag = 'f-wo'
        mix_ch = [mixin[:, k, 0:n] for k in range(8)]
        for half in range(2):
            wv, wt = wload("w_o", l, 0, 8, half * 512, 512)
            for cc in range(4):
                j = half * 4 + cc
                pv, pt = proj_fm(wv, wt, cc, mix_ch, mix_tl, 8)
                p.op("dve", lambda e: e.scalar_tensor_tensor(out=xs[j], in0=pv[:, 0:n], scalar=m2col(2, j), in1=xs[j], op0=ALU.mult, op1=ALU.add), reads=[pt, mod_tl, x_tl[j]], writes=[x_tl[j]])
        p.tag = 'f-ln'
        B1, B2 = layer_norm_stats(xs, x_tl, n, LN_EPS / (ALPHA * ALPHA))
        for j in range(8):
            ln_apply(xs[j], x_tl[j], xs[j], x_tl[j], n, B1, B2, vcol(V_LN1G + j), vcol(V_LN1B + j), [vec_tl])

        dbg('x1', x_t[:, :, 0:n], x_tl)
        _chk('f')
        p.tag = 'g-ln'
        B1, B2 = layer_norm_stats(xs, x_tl, n, LN_EPS)
        for j in range(8):
            ln_apply(xs[j], x_tl[j], h_t[:, j, 0:n], h_tl[j], n, B1, B2, m2col(4, j), mcol(3, j), [mod_tl])

        _chk('g')
        p.tag = 'h-gu'
        i0 = 0
        while i0 < NFF:
            nch = min(4, NFF - i0)
            wg_, wgt = wload("w_gu", l, 0, 8, i0 * 128, nch * 128)
            wu_, wut = wload("w_gu", l, 0, 8, DFF + i0 * 128, nch * 128)
            for cc in range(nch):
                i = i0 + cc
                pvg, ptg = proj_fm(wg_, wgt, cc, hs, h_tl, 8)
                pvu, ptu = proj_fm(wu_, wut, cc, hs, h_tl, 8)
                sgv, sgt = FP.get()
                p.op("act", lambda e: e.activation(out=sgv[:, 0:n], in_=pvg[:, 0:n], func=AF.Silu), reads=[ptg], writes=[sgt])
                p.op("dve", lambda e: e.tensor_tensor(out=U[:, i, 0:n], in0=pvu[:, 0:n], in1=sgv[:, 0:n], op=ALU.mult), reads=[ptu, sgt], writes=[U_tl[i]])
            i0 += nch
        p.tag = 'h-down'
        hid_ch = [U[:, i, 0:n] for i in range(NFF)]
        for cg in range(2):
            banks = [(PA if cg == 0 else PBC).get() for _ in range(4)]
            for (k0, nk) in ((0, 8), (8, 8), (16, 6)):
                wv, wt = wload("w_down", l, k0, nk, cg * 512, 512)
                for cc in range(4):
                    proj_fm(wv, wt, cc, hid_ch, U_tl, nk, k0=k0, first=(k0 == 0), last=(k0 == 16), pv=banks[cc][0], pt=banks[cc][1])
            for cc in range(4):
                j = cg * 4 + cc
                pv, pt = banks[cc]
                p.op("dve", lambda e: e.scalar_tensor_tensor(out=xs[j], in0=pv[:, 0:n], scalar=m2col(5, j), in1=xs[j], op0=ALU.mult, op1=ALU.add), reads=[pt, mod_tl, x_tl[j]], writes=[x_tl[j]])
        p.tag = 'h-ln'
        B1, B2 = layer_norm_stats(xs, x_tl, n, LN_EPS / (ALPHA * ALPHA))
        for j in range(8):
            ln_apply(xs[j], x_tl[j], xs[j], x_tl[j], n, B1, B2, vcol(V_LN2G + j), vcol(V_LN2B + j), [vec_tl])

    for l in range(DEPTH):
        lam_init = 0.8 - 0.6 * math.exp(-0.3 * l)
        p.op("dve", lambda e: e.tensor_scalar(out=vec_sb[:, l, V_DAN:V_DAN + 4], in0=vec_sb[:, l, V_DAN:V_DAN + 4], scalar1=1.0 - lam_init, scalar2=None, op0=ALU.mult),
             reads=[vec_tl], writes=[vec_tl])

    try:
        _chk('pro')
        for si in range(NP):
            run_sequence(si, False)
        if with_sample:
            run_sequence(NP, True)
    except _Stop:
        pass

    for key, sem in p.dsems.items():
        if p.dcnt[key] > 0:
            nc.sync.wait_ge(sem, p.dcnt[key])
    p.sbuf_left = nc.sbuf_bytes_remaining
    return nc, p


def _consts():
    ident = np.eye(128, dtype=np.float32)
    s_idx = np.arange(128)[:, None]
    t_idx = np.arange(128)[None, :]
    maskST = (s_idx <= t_idx).astype(np.float32)
    Dtab = np.zeros((128, 4, 128), np.float32)
    biasT = np.zeros((128, 4, 16), np.float32)
    kl = np.arange(128)[:, None].astype(np.float64)
    ql = np.arange(128)[None, :].astype(np.float64)
    vis = (kl // 64) <= (ql // 64)
    for h in range(4):
        s = SLOPES[h]
        d = np.where(kl <= ql, s * kl, s * (2 * ql - kl))
        Dtab[:, h, :] = np.where(vis, d, NEG)
        for dl in range(16):
            biasT[:, h, dl] = s * kl[:, 0] - s * 128.0 * dl
    c = np.concatenate([ident, maskST, Dtab.reshape(128, 512), biasT.reshape(128, 64)], axis=1).astype(np.float32)
    d4 = np.zeros((4, 4, 4), np.float32)
    for h in range(4):
        d4[h, :, h] = 1.0
    return np.ascontiguousarray(c), np.ascontiguousarray(d4.reshape(4, 16))


def _vecs(inp):
    out = np.zeros((DEPTH, 128, NV), np.float32)
    for l in range(DEPTH):
        out[l, :, V_BADA:V_BADA + 48] = inp["b_ada"][l].reshape(48, 128).T
        out[l, :, V_CONVW:V_CONVW + 32] = inp["conv_w"][l].reshape(4, 8, 128).transpose(2, 0, 1).reshape(128, 32)
        out[l, :, V_CONVB:V_CONVB + 8] = inp["conv_b"][l].reshape(8, 128).T
        out[l, :, V_DAN:V_DAN + 4] = inp["da_norm_w"][l].reshape(4, 128).T
        out[l, :, V_MN:V_MN + 4] = inp["m_norm_w"][l].reshape(4, 128).T
        out[l, :, V_BGATE:V_BGATE + 16] = inp["b_gate"][l].reshape(16, 128).T
        out[l, :, V_LN1G:V_LN1G + 8] = inp["ln1_g"][l].reshape(8, 128).T
        out[l, :, V_LN1B:V_LN1B + 8] = inp["ln1_b"][l].reshape(8, 128).T
        out[l, :, V_LN2G:V_LN2G + 8] = inp["ln2_g"][l].reshape(8, 128).T
        out[l, :, V_LN2B:V_LN2B + 8] = inp["ln2_b"][l].reshape(8, 128).T
    return out


_PROG_CACHE = {}


def run_cores(inp, ncores, NP, T, with_sample=True, trace=False):
    key = (NP, T, with_sample)
    if key not in _PROG_CACHE:
        _PROG_CACHE[key] = build_program(NP, T, with_sample)
    nc, p = _PROG_CACHE[key]
    f32 = lambda a: np.ascontiguousarray(np.asarray(a, dtype=np.float32))
    consts, d4 = _consts()
    vecs = _vecs(inp)
    bifh = f32(np.asarray(inp["b_if"]).reshape(DEPTH, 2, 4).transpose(0, 2, 1))
    lamp = f32(np.asarray(inp["lam_p"]).reshape(DEPTH, 1, 256))
    shared = {"vecs": vecs, "bif": bifh, "lamp": lamp, "consts": consts, "delta4": d4}
    for k in W_SHAPES:
        shared[k] = f32(inp[k])
    in_maps = []
    for c in range(ncores):
        m = dict(shared)
        xp = np.asarray(inp["x_prompt"])[c * NP:(c + 1) * NP]
        m["xT"] = f32(xp.transpose(0, 2, 1))
        cs = [np.asarray(inp["c_prompt"])[c * NP + i] for i in range(NP)]
        if with_sample:
            cs.append(np.asarray(inp["c_sample"])[c])
        cmat = np.stack(cs, 0)
        m["cT"] = f32(cmat.reshape(len(cs), 8, 128).transpose(2, 1, 0))
        if with_sample:
            m["xsT"] = f32(np.asarray(inp["x_sample"])[c].T)
            m["ckT"] = f32(np.asarray(inp["cache_attn_k"])[:, c].transpose(0, 2, 3, 1))
            m["cvv"] = f32(np.asarray(inp["cache_attn_v"])[:, c].reshape(DEPTH, PAST, 512))
            C = np.asarray(inp["state_mlstm_C"])[:, c]
            nn = np.asarray(inp["state_mlstm_n"])[:, c]
            sG = np.zeros((DEPTH, 128, 4, 130), np.float32)
            sG[:, :, :, 0:128] = C.transpose(0, 3, 1, 2)
            sG[:, :, :, 128] = nn.transpose(0, 2, 1)
            sG[:, :, :, 129] = nn.transpose(0, 2, 1)
            m["sG"] = f32(sG.reshape(DEPTH, 128, 520))
            m["sm"] = f32(np.asarray(inp["state_mlstm_m"])[:, c].reshape(DEPTH, 4, 1))
            cv = np.asarray(inp["state_mlstm_conv"])[:, c]
            m["sconv"] = f32(cv.reshape(DEPTH, 3, 8, 128).transpose(0, 3, 2, 1).reshape(DEPTH, 128, 24))
        in_maps.append(m)
    res = run_bass_kernel_spmd(nc, in_maps, core_ids=list(range(ncores)), trace=trace)
    return res


def assemble(results, ncores, NP, T, with_sample=True):
    B = ncores * NP
    y = np.zeros((B, T, D), np.float32)
    ak = np.zeros((DEPTH, B, T, 4, 128), np.float32)
    av = np.zeros((DEPTH, B, T, 4, 128), np.float32)
    Cp = np.zeros((DEPTH, B, 4, 128, 128), np.float32)
    npp = np.zeros((DEPTH, B, 4, 128), np.float32)
    mp = np.zeros((DEPTH, B, 4), np.float32)
    cvp = np.zeros((DEPTH, B, 3, 1024), np.float32)
    Bs = ncores
    ys = np.zeros((Bs, TS, D), np.float32)
    aks = np.zeros((DEPTH, Bs, TS, 4, 128), np.float32)
    avs = np.zeros((DEPTH, Bs, TS, 4, 128), np.float32)
    Cs = np.zeros((DEPTH, Bs, 4, 128, 128), np.float32)
    ns = np.zeros((DEPTH, Bs, 4, 128), np.float32)
    ms = np.zeros((DEPTH, Bs, 4), np.float32)
    cvs = np.zeros((DEPTH, Bs, 3, 1024), np.float32)

    def unG(g):
        g = g.reshape(g.shape[:-1] + (4, 130))
        Cc = np.moveaxis(g[..., 0:128], -3, -1)
        nn = np.moveaxis(g[..., 128], -2, -1)
        return Cc, nn

    def unconv(cv):
        cv = cv.reshape(cv.shape[:-1] + (8, 3))
        return np.moveaxis(cv, -1, -3).swapaxes(-1, -2).reshape(cv.shape[:-3] + (3, 1024))

    for c in range(ncores):
        r = results[c]
        sl = slice(c * NP, (c + 1) * NP)
        y[sl] = r["yT"].transpose(0, 2, 1)
        ak[:, sl] = r["okT"].transpose(0, 1, 3, 2).reshape(DEPTH, NP, T, 4, 128)
        av[:, sl] = r["ov"].reshape(DEPTH, NP, T, 4, 128)
        Cc, nn = unG(r["oG"])
        Cp[:, sl] = Cc
        npp[:, sl] = nn
        mp[:, sl] = r["om"][..., 0]
        cvp[:, sl] = unconv(r["oconv"])
        if with_sample:
            ys[c] = r["ysT"].T
            aks[:, c] = r["oksT"].transpose(0, 2, 1).reshape(DEPTH, TS, 4, 128)
            avs[:, c] = r["ovs"].reshape(DEPTH, TS, 4, 128)
            Cc, nn = unG(r["oGs"])
            Cs[:, c] = Cc
            ns[:, c] = nn
            ms[:, c] = r["oms"][..., 0]
            cvs[:, c] = unconv(r["oconvs"])
    return (y, ys, ak, av, aks, avs, Cp, npp, mp, cvp, Cs, ns, ms, cvs)


def kernel(**inputs):
    ncores = 8
    NP = 4
    T = 2048
    res = run_cores(inputs, ncores, NP, T, True)
    return assemble(res.results, ncores, NP, T, True)
```
